# Optimizing a Trainium2 kernel written in Bass

```python
import math
import jax, jax.numpy as jnp
from jax import lax
import numpy as np

D_MODEL = 1024
BATCH = 4
SEQ = 4096
DEPTH = 1

GRID_W = 64
CTX_LEN = 256
N_HEADS = 8
HEAD_DIM = 64
V_DIM = 2 * HEAD_DIM
QK_W = N_HEADS * 2 * HEAD_DIM
ATTN_W = N_HEADS * V_DIM
S5_W = 512
S5_GROUP = 16
S5_GROUPS = S5_W // S5_GROUP
S5_STATE = 64
D_FF = 2816
CONV_W = 3
Q_BLOCK = 128
ROPE_THETA = 10000.0
EPS = 1e-6
MIN_NEG_RE = -1e-4
Q_OFF = 0
K_OFF = QK_W
V_OFF = 2 * QK_W
U_OFF = 2 * QK_W + ATTN_W
G_OFF = U_OFF + S5_W
N_IN = G_OFF + 2 * D_MODEL

kernel_name = 'hybrid_s5_diffattn_convffn_block'


def rms_norm(x, w):
    xf = x.astype(jnp.float32)
    y = xf * lax.rsqrt(jnp.mean(xf * xf, axis=-1, keepdims=True) + EPS)
    return y.astype(x.dtype) * w


def modulate(x, w, shift, scale):
    return rms_norm(x, w) * (1 + scale) + shift


def axial_rope_tables(rows):
    row = jnp.repeat(jnp.arange(rows), GRID_W).astype(jnp.float32)
    col = jnp.tile(jnp.arange(GRID_W), rows).astype(jnp.float32)
    half = HEAD_DIM // 2
    inv_freq = ROPE_THETA ** (-jnp.arange(0, half, 2, dtype=jnp.float32) / half)
    ang_r = row[:, None] * inv_freq
    ang_c = col[:, None] * inv_freq
    shp = (rows * GRID_W, 1, 1, half // 2)
    return (jnp.cos(ang_r).reshape(shp), jnp.sin(ang_r).reshape(shp),
            jnp.cos(ang_c).reshape(shp), jnp.sin(ang_c).reshape(shp))


def rope_1d(x, cos, sin):
    x1, x2 = jnp.split(x, 2, axis=-1)
    return jnp.concatenate([x1 * cos - x2 * sin, x2 * cos + x1 * sin], axis=-1)


def rope_2d(x, tabs):
    cos_r, sin_r, cos_c, sin_c = tabs
    x_row, x_col = jnp.split(x, 2, axis=-1)
    out = jnp.concatenate([rope_1d(x_row, cos_r, sin_r), rope_1d(x_col, cos_c, sin_c)], axis=-1)
    return out.astype(x.dtype)


def diff_attention(q, k, v, lam):
    s = jnp.einsum('bqhmd,bkhmd->bhmqk', q, k, preferred_element_type=jnp.float32) * (HEAD_DIM ** -0.5)
    p = jax.nn.softmax(s, axis=-1)
    a = p[:, :, 0] - lam.astype(jnp.float32) * p[:, :, 1]
    return jnp.einsum('bhqk,bkhe->bqhe', a.astype(v.dtype), v)


def blocked_latent_diff_attention(q, k_all, v_all, lam):
    b, l, h, m, dh = q.shape
    nb = l // Q_BLOCK
    qb = jnp.moveaxis(q.reshape(b, nb, Q_BLOCK, h, m, dh), 1, 0)
    ob = lax.map(lambda qi: diff_attention(qi, k_all, v_all, lam), qb)
    return jnp.moveaxis(ob, 0, 1).reshape(b, l, h, V_DIM)


def diff_head_out(o, subln_w, lam_init, w_branch):
    o = rms_norm(o, subln_w) * (1.0 - lam_init)
    return o.reshape(o.shape[0], o.shape[1], ATTN_W) @ w_branch


def s5_discretise(a_re, a_im, log_dt, b_re, b_im):
    lam = lax.complex(jnp.minimum(a_re.astype(jnp.float32), MIN_NEG_RE), a_im.astype(jnp.float32))
    dt = jnp.exp(log_dt.astype(jnp.float32))[:, None]
    a_bar = jnp.exp(lam * dt)
    b = lax.complex(b_re.astype(jnp.float32), b_im.astype(jnp.float32))
    b_bar = ((a_bar - 1.0) / lam)[..., None] * b
    return a_bar, b_bar


def ssm_combine(e1, e2):
    a1, b1 = e1
    a2, b2 = e2
    return a1 * a2, a2 * b1 + b2


def ssm_scan(bu, a_bar, s0, reverse):
    if s0 is not None:
        first = -1 if reverse else 0
        bu = bu.at[:, first].add(a_bar * s0)
    a = jnp.broadcast_to(a_bar, bu.shape)
    _, states = lax.associative_scan(ssm_combine, (a, bu), axis=1, reverse=reverse)
    return states


def s5_glu_proj(y, glu_w, glu_b, w_branch):
    h = jax.nn.gelu(y)
    return (h * jax.nn.sigmoid(h @ glu_w + glu_b)) @ w_branch


def s5_branch(u, uc, p, need_ctx):
    b, l, _ = u.shape
    n = uc.shape[1]
    ug = u.astype(jnp.float32).reshape(b, l, S5_GROUPS, S5_GROUP)
    ucg = uc.astype(jnp.float32).reshape(b, n, S5_GROUPS, S5_GROUP)
    d_skip = p['s5_d'].astype(jnp.float32)
    y = d_skip * u.astype(jnp.float32)
    yc = d_skip * uc.astype(jnp.float32) if need_ctx else None
    for d in range(2):
        rev = d == 1
        a_bar, b_bar = s5_discretise(p['s5_a_re'][d], p['s5_a_im'][d], p['s5_log_dt'][d],
                                     p['s5_b_re'][d], p['s5_b_im'][d])
        c_mat = lax.complex(p['s5_c_re'][d].astype(jnp.float32), p['s5_c_im'][d].astype(jnp.float32))
        s_ctx = ssm_scan(jnp.einsum('bngc,gpc->bngp', ucg, b_bar), a_bar, None, rev)
        s0 = s_ctx[:, 0] if rev else s_ctx[:, -1]
        s_lat = ssm_scan(jnp.einsum('blgc,gpc->blgp', ug, b_bar), a_bar, s0, rev)
        y = y + jnp.einsum('blgp,gcp->blgc', s_lat, c_mat).real.reshape(b, l, S5_W)
        if need_ctx:
            yc = yc + jnp.einsum('bngp,gcp->bngc', s_ctx, c_mat).real.reshape(b, n, S5_W)
    ys = s5_glu_proj(y.astype(u.dtype), p['glu_w'], p['glu_b'], p['w_branch_s5'])
    ysc = s5_glu_proj(yc.astype(uc.dtype), p['glu_w'], p['glu_b'], p['w_branch_s5']) if need_ctx else None
    return ys, ysc


def merge_branches(g, y_s5, y_attn, b_gate, w_out):
    g_s, g_a = jnp.split(g + b_gate, 2, axis=-1)
    return (jax.nn.sigmoid(g_s) * y_s5 + jax.nn.sigmoid(g_a) * y_attn) @ w_out


def conv_ffn(h, w_up, conv_w, conv_b, w_down):
    u = h @ w_up
    n = u.shape[1]
    pad = CONV_W // 2
    up = jnp.pad(u, ((0, 0), (pad, pad), (0, 0)))
    y = conv_b
    for j in range(CONV_W):
        y = y + up[:, j:j + n] * conv_w[j]
    a, g = jnp.split(y, 2, axis=-1)
    return (jax.nn.silu(g) * a) @ w_down


def hybrid_layer(x, xc, mod_x, mod_c, rope, p, lam_init, update_ctx):
    b, l, _ = x.shape
    n = xc.shape[1]
    sh1, sc1, ga1, sh2, sc2, ga2 = jnp.split(mod_x, 6, axis=-1)
    csh1, csc1, cga1, csh2, csc2, cga2 = jnp.split(mod_c, 6, axis=-1)
    w_in = p['w_in']
    z = modulate(x, p['norm1_w'], sh1, sc1) @ w_in
    q = rms_norm(z[..., Q_OFF:K_OFF].reshape(b, l, N_HEADS, 2, HEAD_DIM), p['q_norm_w'])
    k = rms_norm(z[..., K_OFF:V_OFF].reshape(b, l, N_HEADS, 2, HEAD_DIM), p['k_norm_w'])
    v = z[..., V_OFF:U_OFF].reshape(b, l, N_HEADS, V_DIM)
    u = z[..., U_OFF:G_OFF]
    g = z[..., G_OFF:]
    q = rope_2d(q, rope)
    k = rope_2d(k, rope)
    hc = modulate(xc, p['norm1_w'], csh1, csc1)
    zc = hc @ w_in[:, K_OFF:G_OFF]
    kc = rms_norm(zc[..., :QK_W].reshape(b, n, N_HEADS, 2, HEAD_DIM), p['k_norm_w'])
    vc = zc[..., QK_W:QK_W + ATTN_W].reshape(b, n, N_HEADS, V_DIM)
    uc = zc[..., QK_W + ATTN_W:]
    lam = (jnp.exp(jnp.sum((p['lam_q1'] * p['lam_k1']).astype(jnp.float32)))
           - jnp.exp(jnp.sum((p['lam_q2'] * p['lam_k2']).astype(jnp.float32))) + lam_init)
    k_all = jnp.concatenate([kc, k], axis=1)
    v_all = jnp.concatenate([vc, v], axis=1)
    y_attn = diff_head_out(blocked_latent_diff_attention(q, k_all, v_all, lam),
                           p['subln_w'], lam_init, p['w_branch_attn'])
    y_s5, y_s5_c = s5_branch(u, uc, p, update_ctx)
    x = x + ga1 * merge_branches(g, y_s5, y_attn, p['b_gate'], p['w_out'])
    x = x + ga2 * conv_ffn(modulate(x, p['norm2_w'], sh2, sc2), p['w_up'], p['conv_w'], p['conv_b'], p['w_down'])
    if update_ctx:
        qc = rms_norm((hc @ w_in[:, Q_OFF:K_OFF]).reshape(b, n, N_HEADS, 2, HEAD_DIM), p['q_norm_w'])
        gc = hc @ w_in[:, G_OFF:]
        yc_attn = diff_head_out(diff_attention(qc, kc, vc, lam), p['subln_w'], lam_init, p['w_branch_attn'])
        xc = xc + cga1 * merge_branches(gc, y_s5_c, yc_attn, p['b_gate'], p['w_out'])
        xc = xc + cga2 * conv_ffn(modulate(xc, p['norm2_w'], csh2, csc2), p['w_up'], p['conv_w'], p['conv_b'], p['w_down'])
    return x, xc


def setup_inputs(seed: int = 0) -> dict:
    key = jax.random.key(seed)
    ks = iter(jax.random.split(key, 40))

    def nrm(shape, s):
        return jax.random.normal(next(ks), shape, jnp.float32) * s

    G, P = S5_GROUPS, S5_STATE
    return {
        'x': nrm((BATCH, SEQ, D_MODEL), 1.0),
        'c': nrm((BATCH, D_MODEL), 1.0),
        'ctx': nrm((BATCH, CTX_LEN, D_MODEL), 1.0),
        'c_ctx': nrm((D_MODEL,), 1.0),
        'ada_w': nrm((DEPTH, D_MODEL, 6 * D_MODEL), 0.5 * D_MODEL ** -0.5),
        'ada_b': nrm((DEPTH, 6 * D_MODEL), 0.01),
        'norm1_w': 1.0 + nrm((DEPTH, D_MODEL), 0.01),
        'w_in': nrm((DEPTH, D_MODEL, N_IN), D_MODEL ** -0.5),
        'b_gate': nrm((DEPTH, 2 * D_MODEL), 0.01),
        'q_norm_w': 1.0 + nrm((DEPTH, HEAD_DIM), 0.01),
        'k_norm_w': 1.0 + nrm((DEPTH, HEAD_DIM), 0.01),
        'lam_q1': nrm((DEPTH, HEAD_DIM), 0.1),
        'lam_k1': nrm((DEPTH, HEAD_DIM), 0.1),
        'lam_q2': nrm((DEPTH, HEAD_DIM), 0.1),
        'lam_k2': nrm((DEPTH, HEAD_DIM), 0.1),
        'subln_w': 1.0 + nrm((DEPTH, V_DIM), 0.01),
        's5_a_re': -0.5 + nrm((DEPTH, 2, G, P), 0.01),
        's5_a_im': jnp.pi * jnp.arange(P, dtype=jnp.float32) + nrm((DEPTH, 2, G, P), 0.01),
        's5_log_dt': jax.random.uniform(next(ks), (DEPTH, 2, G), jnp.float32, math.log(1e-3), math.log(1e-1)),
        's5_b_re': nrm((DEPTH, 2, G, P, S5_GROUP), (2 * S5_GROUP) ** -0.5),
        's5_b_im': nrm((DEPTH, 2, G, P, S5_GROUP), (2 * S5_GROUP) ** -0.5),
        's5_c_re': nrm((DEPTH, 2, G, S5_GROUP, P), (2 * P) ** -0.5),
        's5_c_im': nrm((DEPTH, 2, G, S5_GROUP, P), (2 * P) ** -0.5),
        's5_d': nrm((DEPTH, S5_W), 1.0),
        'glu_w': nrm((DEPTH, S5_W, S5_W), S5_W ** -0.5),
        'glu_b': nrm((DEPTH, S5_W), 0.01),
        'w_branch_s5': nrm((DEPTH, S5_W, D_MODEL), S5_W ** -0.5),
        'w_branch_attn': nrm((DEPTH, ATTN_W, D_MODEL), ATTN_W ** -0.5),
        'w_out': nrm((DEPTH, D_MODEL, D_MODEL), D_MODEL ** -0.5),
        'norm2_w': 1.0 + nrm((DEPTH, D_MODEL), 0.01),
        'w_up': nrm((DEPTH, D_MODEL, 2 * D_FF), D_MODEL ** -0.5),
        'conv_w': nrm((DEPTH, CONV_W, 2 * D_FF), CONV_W ** -0.5),
        'conv_b': nrm((DEPTH, 2 * D_FF), 0.01),
        'w_down': nrm((DEPTH, D_FF, D_MODEL), D_FF ** -0.5),
    }


def reference(x, c, ctx, c_ctx, ada_w, ada_b, norm1_w, w_in, b_gate, q_norm_w, k_norm_w,
              lam_q1, lam_k1, lam_q2, lam_k2, subln_w, s5_a_re, s5_a_im, s5_log_dt,
              s5_b_re, s5_b_im, s5_c_re, s5_c_im, s5_d, glu_w, glu_b, w_branch_s5,
              w_branch_attn, w_out, norm2_w, w_up, conv_w, conv_b, w_down):
    rows = x.shape[1] // GRID_W
    rope = axial_rope_tables(rows)
    xc = ctx
    for li in range(DEPTH):
        p = {
            'norm1_w': norm1_w[li], 'w_in': w_in[li], 'b_gate': b_gate[li],
            'q_norm_w': q_norm_w[li], 'k_norm_w': k_norm_w[li],
            'lam_q1': lam_q1[li], 'lam_k1': lam_k1[li], 'lam_q2': lam_q2[li], 'lam_k2': lam_k2[li],
            'subln_w': subln_w[li],
            's5_a_re': s5_a_re[li], 's5_a_im': s5_a_im[li], 's5_log_dt': s5_log_dt[li],
            's5_b_re': s5_b_re[li], 's5_b_im': s5_b_im[li], 's5_c_re': s5_c_re[li], 's5_c_im': s5_c_im[li],
            's5_d': s5_d[li], 'glu_w': glu_w[li], 'glu_b': glu_b[li],
            'w_branch_s5': w_branch_s5[li], 'w_branch_attn': w_branch_attn[li], 'w_out': w_out[li],
            'norm2_w': norm2_w[li], 'w_up': w_up[li], 'conv_w': conv_w[li], 'conv_b': conv_b[li],
            'w_down': w_down[li],
        }
        mod_x = (jax.nn.silu(c) @ ada_w[li] + ada_b[li])[:, None, :]
        mod_c = (jax.nn.silu(c_ctx) @ ada_w[li] + ada_b[li])[None, None, :]
        lam_init = 0.8 - 0.6 * math.exp(-0.3 * li)
        x, xc = hybrid_layer(x, xc, mod_x, mod_c, rope, p, lam_init, li + 1 < DEPTH)
    return x
```

```python
import contextlib
import math
import numpy as np
import concourse.bass as bass
import concourse.mybir as mybir
from concourse.bass_utils import run_bass_kernel_spmd

F32 = mybir.dt.float32
BF16 = mybir.dt.bfloat16
I32 = mybir.dt.int32
AF = mybir.ActivationFunctionType
ALU = mybir.AluOpType
AX = mybir.AxisListType
DT_SIZE = {F32: 4, BF16: 2, I32: 4}
ENGS = ("pe", "act", "dve", "pool", "sp")
N_DSEM = 24
SB_BASE = 16512
SB_LIMIT = 229344

D = 1024
NOWN = 2048
NQ = 2049
NLAT = 4096
NCTX = 256
NALL = 4352
NIN = 5632
DFF = 2816
EPS = 1e-6
S5_ON = True
N_HEADS_RUN = 8
TWO_PI = 2.0 * math.pi


class Buf:
    __slots__ = ("name", "lw", "rd", "alias", "wd", "excl")

    def __init__(self, name, excl=False):
        self.name = name
        self.excl = excl
        self.lw = None
        self.rd = {}
        self.alias = []
        self.wd = {}


class KB:
    def __init__(self, nc):
        self.nc = nc
        self.q = {e: [] for e in ENGS}
        self.cnt = {e: 0 for e in ENGS}
        self.seen = {e: {} for e in ENGS}
        self.dval = [0] * N_DSEM
        self.dnext = 0
        self.targets = {e: set() for e in ENGS}
        self.sb_ptr = SB_BASE
        self.top_ptr = SB_LIMIT
        self.sb_hist = []
        self.sb_peak = SB_BASE
        self.uid = 0

    def alloc(self, name, shape, dtype, nsub=1, top=False):
        free = 1
        for s in shape[1:]:
            free *= s
        nbytes = free * DT_SIZE[dtype]
        if top:
            end = self.top_ptr // 64 * 64
            start = (end - nbytes) // 64 * 64
            assert start >= self.sb_ptr, f"SBUF overflow (top) allocating {name}: {self.sb_ptr - start} over"
            self.top_ptr = start
        else:
            start = (self.sb_ptr + 63) // 64 * 64
            end = start + nbytes
            assert end <= self.top_ptr, f"SBUF overflow allocating {name}: {end - self.top_ptr} over"
            self.sb_ptr = end
        self.sb_peak = max(self.sb_peak, self.sb_ptr + (SB_LIMIT - self.top_ptr))
        self.uid += 1
        t = self.nc.alloc_sbuf_tensor_at(f"{name}_{self.uid}", list(shape), dtype, offset=start)
        bufs = [Buf(f"{name}.{i}") for i in range(nsub)]
        old = []
        keep = []
        for (s, e, bl) in self.sb_hist:
            if s < end and start < e:
                old.extend(bl)
            keep.append((s, e, bl))
        for b in bufs:
            b.alias = list(old)
        self.sb_hist.append((start, end, bufs))
        return (t, bufs[0]) if nsub == 1 else (t, bufs)

    def mark(self):
        return self.sb_ptr

    def release(self, m):
        self.sb_ptr = m

    def mark_top(self):
        return self.top_ptr

    def release_top(self, m):
        self.top_ptr = m

    def _deps(self, eng, R, W):
        waits = {}
        seen = self.seen[eng]

        def need(key, val):
            if seen.get(key, 0) >= val:
                return
            if waits.get(key, 0) < val:
                waits[key] = val

        def need_all(b):
            if b.lw is not None and b.lw[0] != eng:
                need(*b.lw)
            for k, v in b.wd.items():
                need(k, v)
            for k, v in b.rd.items():
                if k != eng:
                    need(k, v)

        for b in R:
            if b.alias:
                for a in b.alias:
                    need_all(a)
            if b.lw is not None and not (eng == "pe" and b.lw[0] == "pe"):
                need(*b.lw)
            for k, v in b.wd.items():
                need(k, v)
            if b.excl:
                for k, v in b.rd.items():
                    if k != eng:
                        need(k, v)
        for b in W:
            if b.alias:
                for a in b.alias:
                    need_all(a)
                b.alias = []
            need_all(b)
        for k, v in waits.items():
            seen[k] = v
            if k in self.targets:
                self.targets[k].add(v)
        return list(waits.items())

    def _mark(self, tok, R, W):
        k, v = tok
        for b in R:
            if b.rd.get(k, 0) < v:
                b.rd[k] = v
        for b in W:
            if k[0] == "d" and k[1:].isdigit():
                b.wd[k] = v
            else:
                b.wd = {}
            b.lw = tok
            b.rd = {}

    def op(self, eng, fn, R=(), W=()):
        waits = self._deps(eng, R, W)
        self.cnt[eng] += 1
        idx = self.cnt[eng]
        self._mark((eng, idx), R, W)
        self.q[eng].append((waits, fn, idx, None))

    def dma(self, eng, out_ap, in_ap, R=(), W=(), **kw):
        k = self.dnext
        self.dnext = (self.dnext + 1) % N_DSEM
        key = f"d{k}"
        waits = self._deps(eng, R, W)
        prev = self.dval[k]
        if prev > 0 and self.seen[eng].get(key, 0) < prev:
            waits.append((key, prev))
            self.seen[eng][key] = prev
        self.dval[k] = prev + 16
        self._mark((key, prev + 16), R, W)
        self.q[eng].append((waits, lambda e: e.dma_start(out=out_ap, in_=in_ap, **kw), None, k))

    def wait_all(self, eng, bufs):
        waits = self._deps(eng, bufs, ())
        self.q[eng].append((waits, None, None, None))

    def emit(self):
        nc = self.nc
        sems = {}
        with contextlib.ExitStack() as st:
            for e in ENGS:
                sems[e] = st.enter_context(nc.semaphore(f"s_{e}"))
            for k in range(N_DSEM):
                sems[f"d{k}"] = st.enter_context(nc.semaphore(f"s_d{k}"))
            cmap = {}
            for e in ENGS:
                tl = sorted(self.targets[e])
                cmap[e] = {idx: i + 1 for i, idx in enumerate(tl)}
            block = st.enter_context(nc.Block())

            def replay(e, eng):
                tg = cmap[e]
                for (waits, fn, idx, dk) in self.q[e]:
                    for (key, val) in waits:
                        v = cmap[key][val] if key in cmap else val
                        eng.wait_ge(sems[key], v)
                    if fn is None:
                        continue
                    ins = fn(eng)
                    if dk is not None:
                        ins.then_inc(sems[f"d{dk}"], 16)
                    elif idx in tg:
                        ins.then_inc(sems[e], 1)

            @block.tensor
            def _(eng):
                replay("pe", eng)

            @block.scalar
            def _(eng):
                replay("act", eng)

            @block.vector
            def _(eng):
                replay("dve", eng)

            @block.gpsimd
            def _(eng):
                replay("pool", eng)

            @block.sync
            def _(eng):
                replay("sp", eng)


INPUT_SPECS = [
    ("xo", [NOWN, D]), ("xt", [NOWN, D]), ("cx", [NCTX, D]), ("cvec", [2, D]),
    ("ada_w", [D, 6 * D]), ("ada_b", [6 * D]), ("norm1_w", [D]), ("w_in", [D, NIN]),
    ("b_gate", [2 * D]), ("q_norm_w", [64]), ("k_norm_w", [64]), ("lamv", [4, 64]),
    ("subln_w", [128]), ("s5_a_re", [2, 32, 64]), ("s5_a_im", [2, 32, 64]), ("s5_log_dt", [64]),
    ("s5_b_re", [2, 32, 64, 16]), ("s5_b_im", [2, 32, 64, 16]), ("s5_c_re", [2, 32, 16, 64]),
    ("s5_c_im", [2, 32, 16, 64]), ("s5_d", [512]), ("glu_w", [512, 512]), ("glu_b", [512]),
    ("w_bs", [512, D]), ("w_ba", [D, D]), ("w_out", [D, D]), ("norm2_w", [D]),
    ("w_up", [D, NIN]), ("conv_w", [3, NIN]), ("conv_b", [NIN]), ("w_down", [DFF, D]),
    ("posinfo", [2]),
]


class _Stop(Exception):
    pass


def build(stop=None, dbg=()):
    nc = bass.Bass("TRN2", target_bir_lowering=False)
    I = {n: nc.dram_tensor(n, s, F32, kind="ExternalInput").ap() for n, s in INPUT_SPECS}
    out_d = nc.dram_tensor("out", [NOWN, D], F32, kind="ExternalOutput").ap()
    hT_spill = nc.dram_tensor("hT_spill", [128, 8 * NALL], BF16, kind="Internal").ap()
    hsT_spill = nc.dram_tensor("hsT_spill", [128, 4 * (NQ + 7)], BF16, kind="Internal").ap()
    hsT_spill_b = Buf("hsT_spill")
    hT_spill_b = Buf("hT_spill")
    kb = KB(nc)
    dbg_out = {}
    obufs = []

    def dbg_store(name, tile_ap, shape, rbufs, dtype=F32):
        if name not in dbg:
            return
        d = nc.dram_tensor("dbg_" + name, shape, dtype, kind="ExternalOutput").ap()
        ob = Buf("dbg_" + name)
        kb.dma("sp", d, tile_ap, R=rbufs, W=[ob])
        obufs.append(ob)
        dbg_out[name] = d

    def ckpt(name):
        if stop == name:
            raise _Stop()

    PS = [nc.alloc_psum_tensor(f"psb{i}", [128, 512], F32) for i in range(8)]
    PB = [Buf(f"psb{i}", excl=True) for i in range(8)]

    def psbf(i):
        return PS[i][:].bitcast(BF16)

    def mm(out, lhsT, rhs, start, stop, R, W, **kw):
        kb.op("pe", lambda e: e.matmul(out, lhsT, rhs, start=start, stop=stop, **kw), R, W)

    def tr(out, in_, ident, R, W):
        kb.op("pe", lambda e: e.transpose(out, in_, ident), R, W)

    def act(out, in_, func, R, W, bias=0.0, scale=1.0, accum=None, eng="act"):
        if accum is None:
            kb.op(eng, lambda e: e.activation(out, in_, func, bias=bias, scale=scale), R, W)
        else:
            kb.op(eng, lambda e: e.activation(out, in_, func, bias=bias, scale=scale, accum_out=accum), R, W)

    def tt(eng, out, in0, in1, op, R, W):
        kb.op(eng, lambda e: e.tensor_tensor(out, in0, in1, op), R, W)

    def ts(eng, out, in0, s1, s2, op0, op1, R, W):
        if s2 is None:
            kb.op(eng, lambda e: e.tensor_scalar(out, in0, s1, None, op0), R, W)
        else:
            kb.op(eng, lambda e: e.tensor_scalar(out, in0, s1, s2, op0, op1), R, W)

    def stt(out, in0, scalar, in1, op0, op1, R, W, accum=None):
        if accum is None:
            kb.op("dve", lambda e: e.scalar_tensor_tensor(out, in0, scalar, in1, op0, op1), R, W)
        else:
            kb.op("dve", lambda e: e.scalar_tensor_tensor(out, in0, scalar, in1, op0, op1, accum_out=accum), R, W)

    def cp(eng, out, in_, R, W):
        if eng == "act":
            kb.op(eng, lambda e: e.copy(out, in_), R, W)
        else:
            kb.op(eng, lambda e: e.tensor_copy(out, in_), R, W)

    def memset(eng, ap, val, W):
        kb.op(eng, lambda e: e.memset(ap, val), (), W)

    def iota(ap, pattern, base, cm, W):
        kb.op("pool", lambda e: e.iota(ap, pattern, base=base, channel_multiplier=cm), (), W)

    def asel(out, in_, pattern, cmp, fill, base, cm, R, W):
        kb.op("pool", lambda e: e.affine_select(out, in_, pattern=pattern, compare_op=cmp, fill=fill,
                                                 base=base, channel_multiplier=cm), R, W)

    def recip(out, in_, R, W):
        kb.op("dve", lambda e: e.reciprocal(out, in_), R, W)

    def body():
        ident_t, ident_b = kb.alloc("ident", [128, 128], BF16)
        memset("pool", ident_t[:], 0.0, [ident_b])
        asel(ident_t[:], ident_t[:], [[-1, 128]], ALU.not_equal, 1.0, 0, 1, [ident_b], [ident_b])

        EPS_t, EPS_b = kb.alloc("eps", [128, 1], F32)
        memset("pool", EPS_t[:, :], EPS, [EPS_b])
        mh2_t, mh2_b = kb.alloc("mh2", [128, 1], F32)
        memset("pool", mh2_t[:, :], -0.5, [mh2_b])
        MHALF2 = mh2_t[:, 0:1]
        NSTG = 2
        STG_W = 1024
        stg_t, stg_b = kb.alloc("stg", [128, NSTG, STG_W], F32, nsub=NSTG)
        stg_i = [0]
        cast_rr = [0]

        def wload(dst3, src3, nrow, ncol, Wb, cast_engs=("pool",)):
            if ncol <= STG_W:
                rp = STG_W // ncol
                pieces = [(r, min(rp, nrow - r), 0, ncol) for r in range(0, nrow, rp)]
            else:
                pieces = [(r, 1, c, min(STG_W, ncol - c)) for r in range(nrow) for c in range(0, ncol, STG_W)]
            for (r, nr, c, ncc) in pieces:
                i = stg_i[0]
                stg_i[0] = (i + 1) % NSTG
                sview = stg_t[:, i, 0:nr * ncc].rearrange("p (r c) -> p r c", r=nr)
                kb.dma("sp", sview, src3[:, r:r + nr, c:c + ncc], W=[stg_b[i]])
                ce = cast_engs[cast_rr[0] % len(cast_engs)]
                cast_rr[0] += 1
                cp(ce, dst3[:, r:r + nr, c:c + ncc], sview, [stg_b[i]], [Wb])

        def rows_to_cols(dst, dst_b, row_srcs):
            m0 = kb.mark()
            total = sum(n for _, n in row_srcs)
            rs_t, rs_b = kb.alloc("rs", [128, 128], F32)
            hi_t, hi_b = kb.alloc("rhi", [128, 128], BF16)
            lo_t, lo_b = kb.alloc("rlo", [128, 128], BF16)
            tmp_t, tmp_b = kb.alloc("rtmp", [128, 128], F32)
            r0 = 0
            for ap, n in row_srcs:
                kb.dma("sp", rs_t[r0:r0 + n, :], ap, W=[rs_b])
                r0 += n
            cp("dve", hi_t[0:total, :], rs_t[0:total, :], [rs_b], [hi_b])
            tt("dve", tmp_t[0:total, :], rs_t[0:total, :], hi_t[0:total, :], ALU.subtract, [rs_b, hi_b], [tmp_b])
            cp("dve", lo_t[0:total, :], tmp_t[0:total, :], [tmp_b], [lo_b])
            pv = psbf(0)
            tr(pv[:, 0:total], hi_t[0:total, :], ident_t[0:total, 0:total], [hi_b, ident_b], [PB[0]])
            tr(pv[:, 128:128 + total], lo_t[0:total, :], ident_t[0:total, 0:total], [lo_b, ident_b], [PB[0]])
            cp("dve", tmp_t[:, 0:total], pv[:, 0:total], [PB[0]], [tmp_b])
            tt("dve", dst, tmp_t[:, 0:total], pv[:, 128:128 + total], ALU.add, [tmp_b, PB[0]], [dst_b])
            kb.release(m0)

        V1_t, V1_b = kb.alloc("V1", [128, 104], F32)
        rows_to_cols(V1_t[:, :], V1_b, [
            (I["ada_b"].rearrange("(r c) -> r c", c=128), 48),
            (I["norm1_w"].rearrange("(r c) -> r c", c=128), 8),
            (I["norm2_w"].rearrange("(r c) -> r c", c=128), 8),
            (I["b_gate"].rearrange("(r c) -> r c", c=128), 16),
            (I["glu_b"].rearrange("(r c) -> r c", c=128), 4),
            (I["s5_d"].rearrange("(r c) -> r c", c=128), 4),
            (I["cvec"].rearrange("t (r c) -> (t r) c", c=128), 16),
        ])
        ADAB, N1W, N2W, BG, GLUB, S5D, CV = 0, 48, 56, 64, 80, 84, 88
        dbg_store("V1", V1_t[:, :], [128, 104], [V1_b])
        ckpt("V1")
        V2_t, V2_b = kb.alloc("V2", [128, 176], F32)
        cwv = I["conv_w"].rearrange("j (r c) -> (j r) c", c=128)
        rows_to_cols(V2_t[:, 0:88], V2_b, [(cwv[0:88, :], 88)])
        rows_to_cols(V2_t[:, 88:176], V2_b, [(cwv[88:132, :], 44), (I["conv_b"].rearrange("(r c) -> r c", c=128), 44)])

        ckpt("V2")
        sc_t, sc_b = kb.alloc("sc", [128, 8, 2], BF16)
        scb_t, scb_b = kb.alloc("scb", [128, 8, 128], BF16)
        act(sc_t[:, :, :].rearrange("p k t -> p t k"), V1_t[:, CV:CV + 16].rearrange("p (t k) -> p t k", t=2),
            AF.Silu, [V1_b], [sc_b])
        cp("dve", scb_t[:, :, :], sc_t[:, :, 0:1].broadcast_to([128, 8, 128]), [sc_b], [scb_b])

        ckpt("silu")
        modv_t, modv_b = kb.alloc("modv", [128, 48, 2], F32)
        ga_t, ga_b = kb.alloc("ga", [128, 2, D], F32)
        m_ada = kb.mark()
        adaw_t, adaw_b = kb.alloc("adaw", [128, 2, 8, D], BF16, nsub=2)
        abb_t, abb_b = kb.alloc("abb", [128, D], F32)
        adaw_src = I["ada_w"].rearrange("(kt p) n -> p kt n", p=128)
        for ci in range(6):
            bi = ci % 2
            wload(adaw_t[:, bi], adaw_src[:, :, ci * D:(ci + 1) * D], 8, D, adaw_b[bi], cast_engs=("pool", "dve", "act", "dve"))
            for ft in range(8):
                for kt in range(8):
                    mm(PS[1][:, (ci * 8 + ft) * 2:(ci * 8 + ft) * 2 + 2], adaw_t[:, bi, kt, ft * 128:(ft + 1) * 128],
                       sc_t[:, kt, :], kt == 0, kt == 7, [adaw_b[bi], sc_b], [PB[1]])
            if ci in (2, 5):
                gi = 0 if ci == 2 else 1
                kb.dma("sp", abb_t[:, :], I["ada_b"][ci * D:(ci + 1) * D].partition_broadcast(128), W=[abb_b])
                for hf in range(2):
                    for kt in range(8):
                        mm(PS[2 + hf][:, :], scb_t[:, kt, :], adaw_t[:, bi, kt, hf * 512:(hf + 1) * 512],
                           kt == 0, kt == 7, [adaw_b[bi], scb_b], [PB[2 + hf]])
                    tt("dve", ga_t[:, gi, hf * 512:(hf + 1) * 512], PS[2 + hf][:, :], abb_t[:, hf * 512:(hf + 1) * 512],
                       ALU.add, [PB[2 + hf], abb_b], [ga_b])
        tt("dve", modv_t[:, :, :], PS[1][:, 0:96].rearrange("p (f t) -> p f t", t=2),
           V1_t[:, ADAB:ADAB + 48, None].broadcast_to([128, 48, 2]), ALU.add, [PB[1], V1_b], [modv_b])
        kb.release(m_ada)
        MOD_t, MOD_b = kb.alloc("MOD", [128, 6, 8], F32)
        stt(MOD_t[:, 0, :], modv_t[:, 8:16, 0], 1.0, V1_t[:, N1W:N1W + 8], ALU.add, ALU.mult, [modv_b, V1_b], [MOD_b])
        cp("dve", MOD_t[:, 1, :], modv_t[:, 0:8, 0], [modv_b], [MOD_b])
        stt(MOD_t[:, 2, :], modv_t[:, 8:16, 1], 1.0, V1_t[:, N1W:N1W + 8], ALU.add, ALU.mult, [modv_b, V1_b], [MOD_b])
        cp("dve", MOD_t[:, 3, :], modv_t[:, 0:8, 1], [modv_b], [MOD_b])
        stt(MOD_t[:, 4, :], modv_t[:, 32:40, 0], 1.0, V1_t[:, N2W:N2W + 8], ALU.add, ALU.mult, [modv_b, V1_b], [MOD_b])
        cp("dve", MOD_t[:, 5, :], modv_t[:, 24:32, 0], [modv_b], [MOD_b])
        ckpt("mod")
        dbg_store("MOD", MOD_t[:, :, :], [128, 6, 8], [MOD_b])
        dbg_store("ga", ga_t[:, :, :], [128, 2, D], [ga_b])
        ckpt("mod2")

        mR0 = kb.mark()
        oT_box = {}

        def alloc_oT():
            oT_box["t"], oT_box["b"] = kb.alloc("oT", [128, 8, NQ], BF16)

        if not S5_ON:
            alloc_oT()
        HSPLIT = 2176
        hT_b = [Buf(f"hT{i}") for i in range(34)]
        hT_tiles = {}

        def alloc_hT():
            hT_tiles["o"], bo_ = kb.alloc("hTo", [128, 8, HSPLIT], BF16)
            hT_tiles["m"] = kb.mark()
            hT_tiles["r"], br_ = kb.alloc("hTr", [128, 8, NALL - HSPLIT], BF16)
            for i_, b_ in enumerate(hT_b):
                b_.alias = list(bo_.alias if i_ < 17 else br_.alias)
            kb.sb_hist[-2] = (kb.sb_hist[-2][0], kb.sb_hist[-2][1], hT_b[0:17])
            kb.sb_hist[-1] = (kb.sb_hist[-1][0], kb.sb_hist[-1][1], hT_b[17:34])

        def hT(kt, c0, n):
            if c0 + n <= HSPLIT:
                return hT_tiles["o"][:, kt, c0:c0 + n]
            assert c0 >= HSPLIT, (c0, n)
            return hT_tiles["r"][:, kt, c0 - HSPLIT:c0 - HSPLIT + n]

        alloc_hT()
        m1 = kb.mark()
        xin_t, xin_b = kb.alloc("xin", [128, 3, D], F32, nsub=3)
        xn_t, xn_b = kb.alloc("xn", [128, 2, D], BF16, nsub=2)
        junk_t, junk_b = kb.alloc("junk", [128, D], BF16)
        st_t, st_b = kb.alloc("st", [128, 4, 4], F32, nsub=4)

        def norm_tile(src_ap, xi, ji, col0, moda, modb, nrows=128):
            ti = col0 // 128
            sb_ = st_b[ji % 4]
            stv = st_t[:, ji % 4, :]
            act(junk_t[0:nrows, :], src_ap, AF.Square, [xin_b[xi]], [junk_b, sb_], accum=stv[0:nrows, 0:1])
            act(stv[0:nrows, 1:2], stv[0:nrows, 0:1], AF.Sqrt, [sb_], [sb_], bias=EPS_t[0:nrows, 0:1], scale=1.0 / D)
            ckpt("n_sqrt")
            recip(stv[0:nrows, 2:3], stv[0:nrows, 1:2], [sb_], [sb_])
            ckpt("n_recip")
            xb = ji % 2
            ts("dve", xn_t[0:nrows, xb, :], src_ap, stv[0:nrows, 2:3], None, ALU.mult, None, [xin_b[xi], sb_], [xn_b[xb]])
            ckpt("n_xn")
            pb = 4 + (ji % 2)
            pv = psbf(pb)
            for kt in range(8):
                tr(pv[:, kt * 128:kt * 128 + nrows], xn_t[0:nrows, xb, kt * 128:(kt + 1) * 128],
                   ident_t[0:nrows, 0:nrows], [xn_b[xb], ident_b], [PB[pb]])
            ckpt("n_tr")
            for kt in range(8):
                eng = "dve"
                ts(eng, hT(kt, col0, nrows), pv[:, kt * 128:kt * 128 + nrows],
                   MOD_t[:, moda, kt:kt + 1], MOD_t[:, modb, kt:kt + 1], ALU.mult, ALU.add,
                   [PB[pb], MOD_b], [hT_b[ti]])

        ji = 0
        for ti in range(34):
            if ti < 16:
                src = I["xo"][ti * 128:(ti + 1) * 128, :]
            elif ti < 32:
                src = I["xt"][(ti - 16) * 128:(ti - 15) * 128, :]
            else:
                src = I["cx"][(ti - 32) * 128:(ti - 31) * 128, :]
            xi = ti % 3
            kb.dma("sp", xin_t[:, xi, :], src, W=[xin_b[xi]])
            if ti < 32:
                norm_tile(xin_t[:, xi, :], xi, ji, ti * 128, 0, 1)
            else:
                norm_tile(xin_t[:, xi, :], xi, ji, ti * 128, 2, 3)
            ji += 1
            ckpt("n_tile1")
        kb.release(m1)
        dbg_store("hT", hT_tiles["o"][:, :, 0:256], [128, 8, 256], hT_b[0:2], BF16)
        dbg_store("hTc", hT_tiles["r"][:, :, 4096 - HSPLIT:4352 - HSPLIT], [128, 8, 256], hT_b[32:34], BF16)
        ckpt("stage1")
        hT_all = list(hT_b)

        def hT_bufs(c0, n):
            return hT_b[c0 // 128:(c0 + n + 127) // 128]

        w_in_v = I["w_in"].rearrange("(kt p) n -> p kt n", p=128)

        def s5_stage():
            mS = kb.mark()
            sinT = [None]

            def sin_turns(out_ap, x_ap, mul, add, R, W, tm):
                (t_ap, f_ap, g_ap, k_ap, tb) = tm
                ts("dve", t_ap, x_ap, mul, add, ALU.mult, ALU.add, R, [tb])
                cp("dve", k_ap, t_ap, [tb], [tb])
                cp("dve", f_ap, k_ap, [tb], [tb])
                tt("dve", f_ap, t_ap, f_ap, ALU.subtract, [tb], [tb])
                ts("dve", g_ap, f_ap, 0.5, None, ALU.is_gt, None, [tb], [tb])
                tt("dve", f_ap, f_ap, g_ap, ALU.subtract, [tb], [tb])
                ts("dve", g_ap, f_ap, -0.5, None, ALU.is_lt, None, [tb], [tb])
                tt("dve", f_ap, f_ap, g_ap, ALU.add, [tb], [tb])
                act(out_ap, f_ap, AF.Sin, [tb], W, scale=6.2831)

            U_t, U_b = kb.alloc("U", [128, 32, 544], BF16, top=True)
            RS_t, RS_b = kb.alloc("RS", [128, 8, 240], BF16, top=True)
            memset("pool", RS_t[:, :, :], 0.0, [RS_b])
            for k_ in range(8):
                asel(RS_t[:, k_, 112:128], RS_t[:, k_, 112:128], [[1, 16]], ALU.not_equal, 1.0, 16 * k_, -1, [RS_b], [RS_b])
            m_u = kb.mark()
            wu_t, wu_b = kb.alloc("wu", [128, 8, 512], BF16)
            wload(wu_t[:, :, :], w_in_v[:, :, 3072:3584], 8, 512, wu_b)
            uT_t, uT_b = kb.alloc("uTb", [128, 2, 4, 512], BF16, nsub=2)
            ei = 0
            for bi_, (c0_, n_) in enumerate(((0, 512), (512, 512), (1024, 512), (1536, 512), (2048, 128), (2176, 512), (2688, 512),
                                             (3200, 512), (3712, 512), (4224, 128))):
                ub = bi_ % 2
                for ct in range(4):
                    pb = ei % 2
                    for kt in range(8):
                        mm(PS[pb][:, 0:n_], wu_t[:, kt, ct * 128:(ct + 1) * 128], hT(kt, c0_, n_), kt == 0, kt == 7,
                           [wu_b] + hT_bufs(c0_, n_), [PB[pb]])
                    cp("act" if ei % 2 == 0 else "dve", uT_t[:, ub, ct, 0:n_], PS[pb][:, 0:n_], [PB[pb]], [uT_b[ub]])
                    ei += 1
                nch = n_ // 8
                ch0 = c0_ // 8
                for g4 in range(8):
                    pb = 2 + (g4 % 2)
                    for gq in range(4):
                        g = g4 * 4 + gq
                        ct, gl = g // 8, g % 8
                        for tau in range(8):
                            mm(PS[pb][:, gq * 64:gq * 64 + nch], RS_t[:, gl, 112 - 16 * tau:240 - 16 * tau],
                               uT_t[:, ub, ct, tau:n_:8], tau == 0, tau == 7, [RS_b, uT_b[ub]], [PB[pb]])
                    cp("act" if g4 % 2 == 0 else "dve", U_t[:, g4 * 4:g4 * 4 + 4, ch0:ch0 + nch],
                       PS[pb][:, 0:256].rearrange("p (q n) -> p q n", q=4)[:, :, 0:nch], [PB[pb]], [U_b])
            dbg_store("s5U", U_t[:, 0:2, :], [128, 2, 544], [U_b], BF16)
            ckpt("s5U")
            kb.release(m_u)
            for kt in range(8):
                kb.dma("sp", hT_spill[:, kt * NALL:kt * NALL + HSPLIT], hT_tiles["o"][:, kt, :], R=hT_b[0:17], W=[hT_spill_b])
                kb.dma("sp", hT_spill[:, kt * NALL + HSPLIT:(kt + 1) * NALL], hT_tiles["r"][:, kt, :], R=hT_b[17:34], W=[hT_spill_b])
            kb.release(mR0)

            P_t, P_b = kb.alloc("Pp", [128, 24, 64], F32)
            CS_t, CS_b = kb.alloc("CS", [128, 2, 64, 64], BF16)
            FX_t, FX_b = kb.alloc("FX", [128, 2, 64], F32)
            BB_t, BB_b = kb.alloc("BB", [128, 2, 64, 16], F32)
            CC_t, CC_b = kb.alloc("CC", [128, 2, 64, 16], BF16)
            TM_t, TM_b = kb.alloc("TM", [128, 4, 64, 8], F32)
            PSW_t, PSW_b = kb.alloc("PSW", [128, 128], BF16)
            MSK_t, MSK_b = kb.alloc("MSK", [128, 2, 128], F32)
            DC_t, DC_b = kb.alloc("DC", [128, 32], F32)
            IDF_t, IDF_b = kb.alloc("IDF", [128, 128], F32)
            PH_t, PH_b = kb.alloc("PH", [128, 8], F32)
            mS1 = kb.mark()

            memset("pool", PSW_t[:, :], 0.0, [PSW_b])
            asel(PSW_t[:, 0:64], PSW_t[:, 0:64], [[-1, 64]], ALU.not_equal, -1.0, -64, 1, [PSW_b], [PSW_b])
            asel(PSW_t[:, 64:128], PSW_t[:, 64:128], [[-1, 64]], ALU.not_equal, 1.0, 0, 1, [PSW_b], [PSW_b])
            memset("pool", MSK_t[:, :, :], 1.0, [MSK_b])
            mv = MSK_t[:, :, :].rearrange("p a (t c) -> p a t c", c=16)
            asel(mv[:, 0], mv[:, 0], [[16, 8], [0, 16]], ALU.is_ge, 0.0, 15, -1, [MSK_b], [MSK_b])
            asel(mv[:, 1], mv[:, 1], [[-16, 8], [0, 16]], ALU.is_ge, 0.0, 0, 1, [MSK_b], [MSK_b])
            cp("dve", IDF_t[:, :], ident_t[:, :], [ident_b], [IDF_b])
            for tau in range(8):
                kb.dma("sp", DC_t[tau * 16:(tau + 1) * 16, :], I["s5_d"].rearrange("(g c) -> c g", c=16), W=[DC_b],
                       allow_slow_non_contiguous=True)

            m_p = kb.mark()
            ARE, AIM, LDT, DT, RE, LR, TH, R1, AR, AI, NR, DEN, KR, KI, T0, T1, T2, T3, PHT, RHO1 = range(20)
            tmi_t, tmi_b = kb.alloc("tmi", [128, 1024], I32)
            tmf_t, tmf_b = kb.alloc("tmf", [128, 3, 1024], F32)
            tm64 = (tmf_t[:, 0, 0:64], tmf_t[:, 1, 0:64], tmf_t[:, 2, 0:64], tmi_t[:, 0:64], tmf_b)
            for half in range(2):
                for q4 in range(4):
                    sl = slice(q4 * 16, (q4 + 1) * 16)
                    kb.dma("sp", P_t[half * 64:(half + 1) * 64, ARE, sl], I["s5_a_re"].rearrange("d g p -> p (d g)")[:, sl], W=[P_b],
                           allow_slow_non_contiguous=True)
                    kb.dma("sp", P_t[half * 64:(half + 1) * 64, AIM, sl], I["s5_a_im"].rearrange("d g p -> p (d g)")[:, sl], W=[P_b],
                           allow_slow_non_contiguous=True)
            kb.dma("sp", P_t[:, LDT, :], I["s5_log_dt"].partition_broadcast(128), W=[P_b])
            act(P_t[:, DT, :], P_t[:, LDT, :], AF.Exp, [P_b], [P_b])
            ts("dve", P_t[:, RE, :], P_t[:, ARE, :], -1e-4, None, ALU.min, None, [P_b], [P_b])
            tt("dve", P_t[:, LR, :], P_t[:, RE, :], P_t[:, DT, :], ALU.mult, [P_b], [P_b])
            tt("dve", P_t[:, TH, :], P_t[:, AIM, :], P_t[:, DT, :], ALU.mult, [P_b], [P_b])
            ts("dve", P_t[:, TH, :], P_t[:, TH, :], 1.0 / TWO_PI, None, ALU.mult, None, [P_b], [P_b])
            act(P_t[:, R1, :], P_t[:, LR, :], AF.Exp, [P_b], [P_b])
            sin_turns(P_t[:, AI, :], P_t[:, TH, :], 1.0, 0.0, [P_b], [P_b], tm64)
            sin_turns(P_t[:, AR, :], P_t[:, TH, :], 1.0, 0.25, [P_b], [P_b], tm64)
            tt("dve", P_t[:, AI, :], P_t[:, AI, :], P_t[:, R1, :], ALU.mult, [P_b], [P_b])
            tt("dve", P_t[:, AR, :], P_t[:, AR, :], P_t[:, R1, :], ALU.mult, [P_b], [P_b])
            ts("dve", P_t[:, NR, :], P_t[:, AR, :], -1.0, None, ALU.add, None, [P_b], [P_b])
            tt("dve", P_t[:, T0, :], P_t[:, RE, :], P_t[:, RE, :], ALU.mult, [P_b], [P_b])
            tt("dve", P_t[:, T1, :], P_t[:, AIM, :], P_t[:, AIM, :], ALU.mult, [P_b], [P_b])
            tt("dve", P_t[:, DEN, :], P_t[:, T0, :], P_t[:, T1, :], ALU.add, [P_b], [P_b])
            recip(P_t[:, DEN, :], P_t[:, DEN, :], [P_b], [P_b])
            tt("dve", P_t[:, T0, :], P_t[:, NR, :], P_t[:, RE, :], ALU.mult, [P_b], [P_b])
            tt("dve", P_t[:, T1, :], P_t[:, AI, :], P_t[:, AIM, :], ALU.mult, [P_b], [P_b])
            tt("dve", P_t[:, T0, :], P_t[:, T0, :], P_t[:, T1, :], ALU.add, [P_b], [P_b])
            tt("dve", P_t[:, KR, :], P_t[:, T0, :], P_t[:, DEN, :], ALU.mult, [P_b], [P_b])
            tt("dve", P_t[:, T0, :], P_t[:, AI, :], P_t[:, RE, :], ALU.mult, [P_b], [P_b])
            tt("dve", P_t[:, T1, :], P_t[:, NR, :], P_t[:, AIM, :], ALU.mult, [P_b], [P_b])
            tt("dve", P_t[:, T0, :], P_t[:, T0, :], P_t[:, T1, :], ALU.subtract, [P_b], [P_b])
            tt("dve", P_t[:, KI, :], P_t[:, T0, :], P_t[:, DEN, :], ALU.mult, [P_b], [P_b])
            Braw_t, Braw_b = kb.alloc("Braw", [128, 2, 64, 16], F32)
            for half in range(2):
                for q4 in range(4):
                    sl = slice(q4 * 16, (q4 + 1) * 16)
                    kb.dma("sp", Braw_t[half * 64:(half + 1) * 64, 0, sl, :], I["s5_b_re"].rearrange("d g p c -> p (d g) c")[:, sl, :], W=[Braw_b])
                    kb.dma("sp", Braw_t[half * 64:(half + 1) * 64, 1, sl, :], I["s5_b_im"].rearrange("d g p c -> p (d g) c")[:, sl, :], W=[Braw_b])
            kr_b = P_t[:, KR, :, None].broadcast_to([128, 64, 16])
            ki_b = P_t[:, KI, :, None].broadcast_to([128, 64, 16])
            bt_t, bt_b = kb.alloc("btmp", [128, 64, 16], F32)
            tt("dve", BB_t[:, 0], Braw_t[:, 0], kr_b, ALU.mult, [Braw_b, P_b], [BB_b])
            tt("dve", bt_t[:, :, :], Braw_t[:, 1], ki_b, ALU.mult, [Braw_b, P_b], [bt_b])
            tt("dve", BB_t[:, 0], BB_t[:, 0], bt_t[:, :, :], ALU.subtract, [BB_b, bt_b], [BB_b])
            tt("dve", BB_t[:, 1], Braw_t[:, 1], kr_b, ALU.mult, [Braw_b, P_b], [BB_b])
            tt("dve", bt_t[:, :, :], Braw_t[:, 0], ki_b, ALU.mult, [Braw_b, P_b], [bt_b])
            tt("dve", BB_t[:, 1], BB_t[:, 1], bt_t[:, :, :], ALU.add, [BB_b, bt_b], [BB_b])
            cst_t, cst_b = kb.alloc("cstg", [128, 128], F32)
            csb_t, csb_b = kb.alloc("cstgb", [128, 128], BF16)
            for ri_, nm in enumerate(("s5_c_re", "s5_c_im")):
                src = I[nm].rearrange("d g c p -> (d g c) p")
                for t8 in range(8):
                    kb.dma("sp", cst_t[:, 0:64], src[t8 * 128:(t8 + 1) * 128, :], W=[cst_b])
                    kb.dma("sp", cst_t[:, 64:128], src[t8 * 128:(t8 + 1) * 128, :], W=[cst_b])
                    cp("dve", csb_t[:, :], cst_t[:, :], [cst_b], [csb_b])
                    pv = psbf(4)
                    tr(pv[:, 0:128], csb_t[:, :], ident_t[:, :], [csb_b, ident_b], [PB[4]])
                    cp("dve", CC_t[:, ri_, t8 * 8:(t8 + 1) * 8, :], pv[:, 0:128].rearrange("p (g c) -> p g c", c=16), [PB[4]], [CC_b])
            exi_t, exi_b = kb.alloc("exi", [128, 64, 16], I32)
            exf_t, exf_b = kb.alloc("exf", [128, 64, 16], F32)
            mg_t, mg_b = kb.alloc("mag", [128, 64, 16], F32)
            an_t, an_b = kb.alloc("angx", [128, 64, 16], F32)
            phases = [(0.25, 0.0), (0.5, 0.25), (0.0, 0.75), (0.25, 0.0), (0.25, 0.5), (0.5, 0.75), (0.5, 0.75), (0.75, 0.0)]
            for i_, (p0, p1) in enumerate(phases):
                memset("pool", PH_t[0:64, i_:i_ + 1], p0, [PH_b])
                memset("pool", PH_t[64:128, i_:i_ + 1], p1, [PH_b])
            lr_b = lambda n_: P_t[:, LR, :, None].broadcast_to([128, 64, n_])
            th_b = lambda n_: P_t[:, TH, :, None].broadcast_to([128, 64, n_])
            tmA = (tmf_t[:, 0, :].rearrange("p (g k) -> p g k", g=64), tmf_t[:, 1, :].rearrange("p (g k) -> p g k", g=64),
                   tmf_t[:, 2, :].rearrange("p (g k) -> p g k", g=64), tmi_t[:, :].rearrange("p (g k) -> p g k", g=64), tmf_b)
            tm8 = tuple(a[:, :, 0:8] for a in tmA[:4]) + (tmf_b,)
            iota(exi_t[:, 0:32, 0:8], [[0, 32], [-1, 8]], 7, 0, [exi_b])
            iota(exi_t[:, 32:64, 0:8], [[0, 32], [1, 8]], 0, 0, [exi_b])
            cp("dve", exf_t[:, :, 0:8], exi_t[:, :, 0:8], [exi_b], [exf_b])
            tt("dve", mg_t[:, :, 0:8], exf_t[:, :, 0:8], lr_b(8), ALU.mult, [exf_b, P_b], [mg_b])
            act(mg_t[:, :, 0:8], mg_t[:, :, 0:8], AF.Exp, [mg_b], [mg_b])
            tt("dve", an_t[:, :, 0:8], exf_t[:, :, 0:8], th_b(8), ALU.mult, [exf_b, P_b], [an_b])
            for i_ in range(4):
                sin_turns(TM_t[:, i_], an_t[:, :, 0:8], 1.0, PH_t[:, i_:i_ + 1], [an_b, PH_b], [TM_b], tm8)
                tt("dve", TM_t[:, i_], TM_t[:, i_], mg_t[:, :, 0:8], ALU.mult, [TM_b, mg_b], [TM_b])
            ts("dve", P_t[:, RHO1, :], P_t[:, LR, :], 8.0, None, ALU.mult, None, [P_b], [P_b])
            act(P_t[:, RHO1, :], P_t[:, RHO1, :], AF.Exp, [P_b], [P_b])
            ts("dve", P_t[:, PHT, :], P_t[:, TH, :], 8.0, None, ALU.mult, None, [P_b], [P_b])
            cp("dve", tmi_t[:, 0:64], P_t[:, PHT, :], [P_b], [tmf_b])
            cp("dve", tmf_t[:, 0, 0:64], tmi_t[:, 0:64], [tmf_b], [tmf_b])
            tt("dve", P_t[:, PHT, :], P_t[:, PHT, :], tmf_t[:, 0, 0:64], ALU.subtract, [P_b, tmf_b], [P_b])
            sin_turns(FX_t[:, 1, :], P_t[:, PHT, :], 64.0, 0.0, [P_b], [FX_b], tm64)
            sin_turns(FX_t[:, 0, :], P_t[:, PHT, :], 64.0, 0.25, [P_b], [FX_b], tm64)
            tt("dve", FX_t[:, 0, :], FX_t[:, 0, :], P_t[:, RHO1, :], ALU.mult, [FX_b, P_b], [FX_b])
            tt("dve", FX_t[:, 1, :], FX_t[:, 1, :], P_t[:, RHO1, :], ALU.mult, [FX_b, P_b], [FX_b])
            jf_t, jf_b = kb.alloc("jf", [128, 16, 64], F32)
            iota(tmi_t[:, :].rearrange("p (g j) -> p g j", g=16), [[0, 16], [1, 64]], 0, 0, [tmf_b])
            cp("dve", jf_t[:, :, :], tmi_t[:, :].rearrange("p (g j) -> p g j", g=16), [tmf_b], [jf_b])
            ja_t, ja_b = kb.alloc("ja", [128, 16, 64], F32)
            tmB = tuple(a.rearrange("p g k -> p (g k)").rearrange("p (g j) -> p g j", g=16) for a in tmA[:4]) + (tmf_b,)
            for q4 in range(4):
                sl = slice(q4 * 16, (q4 + 1) * 16)
                tt("dve", ja_t[:, :, :], jf_t[:, :, :], P_t[:, PHT, sl, None].broadcast_to([128, 16, 64]), ALU.mult, [jf_b, P_b], [ja_b])
                sin_turns(CS_t[:, 1, sl, :], ja_t[:, :, :], 1.0, 0.0, [ja_b], [CS_b], tmB)
                sin_turns(CS_t[:, 0, sl, :], ja_t[:, :, :], 1.0, 0.25, [ja_b], [CS_b], tmB)
            dbg_store("s5P", P_t[:, :, :], [128, 24, 64], [P_b])
            dbg_store("s5TM", TM_t[:, :, :, :], [128, 4, 64, 8], [TM_b])
            dbg_store("s5BB", BB_t[:, :, :, :], [128, 2, 64, 16], [BB_b])
            dbg_store("s5CC", CC_t[:, :, :, :], [128, 2, 64, 16], [CC_b], BF16)
            dbg_store("s5CS", CS_t[:, :, :, :], [128, 2, 64, 64], [CS_b], BF16)
            ckpt("s5prep")
            kb.release(m_p)

            WA_t, WA_b = kb.alloc("WA", [128, 5, 32, 64], BF16)
            WB_t, WB_b = kb.alloc("WB", [128, 9, 32, 64], BF16)
            memset("pool", WA_t[:, :, :, :], 0.0, [WA_b])
            memset("pool", WB_t[:, :, :, :], 0.0, [WB_b])
            m_e = kb.mark()
            mt_t, mt_b = kb.alloc("mtf", [128, 2, 3, 128], F32, nsub=2)
            mtb_t, mtb_b = kb.alloc("mtb", [128, 2, 2, 128], BF16, nsub=2)
            mx_t, mx_b = kb.alloc("mx", [128, 2, 2, 128], BF16, nsub=2)
            dm_t, dm_b = kb.alloc("dm", [128, 2, 2, 512], F32, nsub=2)

            def build_MT(gd, i2, which):
                ta = TM_t[:, 2 * which, gd, :, None].broadcast_to([128, 8, 16])
                tb_ = TM_t[:, 2 * which + 1, gd, :, None].broadcast_to([128, 8, 16])
                bre = BB_t[:, 0, gd, None, :].broadcast_to([128, 8, 16])
                bim = BB_t[:, 1, gd, None, :].broadcast_to([128, 8, 16])
                v = lambda k_: mt_t[:, i2, k_, :].rearrange("p (t c) -> p t c", c=16)
                tt("dve", v(0), ta, bre, ALU.mult, [TM_b, BB_b], [mt_b[i2]])
                tt("dve", v(1), tb_, bim, ALU.mult, [TM_b, BB_b], [mt_b[i2]])
                tt("pool", mtb_t[:, i2, which, :].rearrange("p (t c) -> p t c", c=16), v(0), v(1), ALU.add, [mt_b[i2]], [mtb_b[i2]])

            def e_stageA(it):
                g, dd = it // 2, it % 2
                gd = dd * 32 + g
                i2 = it % 2
                build_MT(gd, i2, 0)
                build_MT(gd, i2, 1)

            def e_stageA2(it):
                i2 = it % 2
                pv = psbf(4 + i2)
                tr(pv[:, 0:128], mtb_t[:, i2, 0, :], ident_t[:, :], [mtb_b[i2], ident_b], [PB[4 + i2]])
                tr(pv[:, 128:256], mtb_t[:, i2, 1, :], ident_t[:, :], [mtb_b[i2], ident_b], [PB[4 + i2]])
                cp("act", mx_t[:, i2, :, :], pv[:, 0:256].rearrange("p (w m) -> p w m", w=2), [PB[4 + i2]], [mx_b[i2]])

            def e_stageB(it):
                g, dd = it // 2, it % 2
                gd = dd * 32 + g
                i2 = it % 2
                cosv, sinv = CS_t[:, 0, gd, :], CS_t[:, 1, gd, :]
                if dd == 0:
                    pieces = [(512, 32, 0), (0, 256, 32)]
                else:
                    pieces = [(32, 512, 0), (0, 32, 0)]
                for pi2, (uc0, n_, pc0) in enumerate(pieces):
                    if dd == 0:
                        pe1, pe2 = 6, 7
                    else:
                        pe1, pe2 = (0, 1) if pi2 == 0 else (2, 3)
                    mm(PS[pe1][:, pc0:pc0 + n_], mx_t[:, i2, 0, :], U_t[:, g, uc0:uc0 + n_], True, True, [mx_b[i2], U_b], [PB[pe1]])
                    mm(PS[pe2][:, pc0:pc0 + n_], mx_t[:, i2, 1, :], U_t[:, g, uc0:uc0 + n_], True, True, [mx_b[i2], U_b], [PB[pe2]])
                    e1 = PS[pe1][:, pc0:pc0 + n_]
                    e2 = PS[pe2][:, pc0:pc0 + n_]
                    di = dd
                    d1 = dm_t[:, di, 0, 0:n_]
                    d2 = dm_t[:, di, 1, 0:n_]
                    if dd == 0 and pi2 == 0:
                        c_ap, s_ap = cosv[:, 32:64], sinv[:, 32:64]
                        o_ap = WA_t[:, 0, g, 32:64]
                        v3 = lambda a: a
                    elif dd == 0:
                        c_ap = CS_t[:, 0, gd, None, :].broadcast_to([128, 4, 64])
                        s_ap = CS_t[:, 1, gd, None, :].broadcast_to([128, 4, 64])
                        o_ap = WA_t[:, 1:5, g, :]
                        v3 = lambda a: a.rearrange("p (s j) -> p s j", j=64)
                    elif pi2 == 0:
                        c_ap = CS_t[:, 0, gd, None, ::-1].broadcast_to([128, 8, 64])
                        s_ap = CS_t[:, 1, gd, None, ::-1].broadcast_to([128, 8, 64])
                        o_ap = WB_t[:, 7::-1, g, ::-1]
                        v3 = lambda a: a.rearrange("p (s j) -> p s j", j=64)
                    else:
                        c_ap, s_ap = cosv[:, 31::-1], sinv[:, 31::-1]
                        o_ap = WB_t[:, 8, g, 31::-1]
                        v3 = lambda a: a
                    wb_ = WA_b if dd == 0 else WB_b
                    tt("dve", v3(d1), v3(e1), c_ap, ALU.mult, [PB[pe1], CS_b], [dm_b[di]])
                    tt("dve", v3(d2), v3(e2), s_ap, ALU.mult, [PB[pe2], CS_b], [dm_b[di]])
                    tt("pool", o_ap, v3(d1), v3(d2), ALU.add, [dm_b[di]], [wb_])

            e_stageA(0)
            e_stageA2(0)
            for it in range(64):
                if it + 1 < 64:
                    e_stageA(it + 1)
                e_stageB(it)
                if it + 1 < 64:
                    e_stageA2(it + 1)
            kb.release(m_e)
            dbg_store("s5Wpre", WB_t[:, :, 0:2, :], [128, 9, 2, 64], [WB_b], BF16)
            ckpt("s5E")

            m_s = kb.mark()
            RHO_t, RHO_b = kb.alloc("RHO", [128, 64, 64], F32)
            cp("dve", RHO_t[:, :, :], P_t[:, RHO1, :, None].broadcast_to([128, 64, 64]), [P_b], [RHO_b])
            memset("pool", RHO_t[:, :, 0:1], 0.0, [RHO_b])
            fx_t, fx_b = kb.alloc("fxt", [128, 2, 32], F32)
            for dd, (W_t, W_b, nseg) in enumerate(((WA_t, WA_b, 5), (WB_t, WB_b, 9))):
                for sg in range(nseg):
                    if sg > 0:
                        wend = W_t[:, sg - 1, :, 63]
                        mm(PS[6][:, 0:32], PSW_t[:, :], wend, True, True, [PSW_b, W_b], [PB[6]])
                        tt("dve", fx_t[:, 0, :], wend, FX_t[:, 0, dd * 32:(dd + 1) * 32], ALU.mult, [W_b, FX_b], [fx_b])
                        tt("dve", fx_t[:, 1, :], PS[6][:, 0:32], FX_t[:, 1, dd * 32:(dd + 1) * 32], ALU.mult, [PB[6], FX_b], [fx_b])
                        tt("dve", fx_t[:, 0, :], fx_t[:, 0, :], fx_t[:, 1, :], ALU.add, [fx_b], [fx_b])
                        tt("dve", W_t[:, sg, :, 0], W_t[:, sg, :, 0], fx_t[:, 0, :], ALU.add, [W_b, fx_b], [W_b])
                    wv = W_t[:, sg, :, :].rearrange("p g j -> p (g j)")
                    rv = RHO_t[:, dd * 32:(dd + 1) * 32, :].rearrange("p g j -> p (g j)")
                    kb.op("dve", (lambda wv_, rv_: (lambda e: e.tensor_tensor_scan(wv_, rv_, wv_, 0.0, ALU.mult, ALU.add)))(wv, rv),
                          [W_b, RHO_b], [W_b])
            kb.release(m_s)
            dbg_store("s5W", WB_t[:, :, 0:2, :], [128, 9, 2, 64], [WB_b], BF16)
            dbg_store("s5WA", WA_t[:, :, 0:2, :], [128, 5, 2, 64], [WA_b], BF16)
            ckpt("s5scan")

            m_y = kb.mark()
            TC_t, TC_b = kb.alloc("TC", [128, 3, 64, 16], F32)
            m_tc = kb.mark()
            tmi_t, tmi_b = kb.alloc("tmi2", [128, 32, 16], I32)
            tmf_t, tmf_b = kb.alloc("tmf2", [128, 3, 32, 16], F32)
            exi_t, exi_b = kb.alloc("exi2", [128, 32, 16], I32)
            exf_t, exf_b = kb.alloc("exf2", [128, 32, 16], F32)
            mg_t, mg_b = kb.alloc("mag2", [128, 32, 16], F32)
            an_t, an_b = kb.alloc("angx2", [128, 32, 16], F32)
            tmH = (tmf_t[:, 0], tmf_t[:, 1], tmf_t[:, 2], tmi_t[:, :, :], tmf_b)
            for dd in range(2):
                gs = slice(dd * 32, (dd + 1) * 32)
                if dd == 0:
                    iota(exi_t[:, :, :], [[0, 32], [1, 16]], -7, 0, [exi_b])
                else:
                    iota(exi_t[:, :, 0:8], [[0, 32], [-1, 8]], 0, 0, [exi_b])
                    iota(exi_t[:, :, 8:16], [[0, 32], [-1, 8]], 8, 0, [exi_b])
                cp("dve", exf_t[:, :, :], exi_t[:, :, :], [exi_b], [exf_b])
                tt("dve", mg_t[:, :, :], exf_t[:, :, :], P_t[:, LR, gs, None].broadcast_to([128, 32, 16]), ALU.mult, [exf_b, P_b], [mg_b])
                act(mg_t[:, :, :], mg_t[:, :, :], AF.Exp, [mg_b], [mg_b])
                tt("dve", an_t[:, :, :], exf_t[:, :, :], P_t[:, TH, gs, None].broadcast_to([128, 32, 16]), ALU.mult, [exf_b, P_b], [an_b])
                for i_, phi_ in enumerate((4, 5, 7)):
                    sin_turns(TC_t[:, i_, gs], an_t[:, :, :], 1.0, PH_t[:, phi_:phi_ + 1], [an_b, PH_b], [TC_b], tmH)
                    tt("dve", TC_t[:, i_, gs], TC_t[:, i_, gs], mg_t[:, :, :], ALU.mult, [TC_b, mg_b], [TC_b])
            dbg_store("s5TC", TC_t[:, :, :, :], [128, 3, 64, 16], [TC_b])
            kb.release(m_tc)
            Y_t, Y_b = kb.alloc("Ysb", [128, 32, 257], BF16, top=True)
            cm_t, cm_b = kb.alloc("cmf", [128, 2, 128], F32)
            cpw_t, cpw_b = kb.alloc("cpw", [128, 2, 6, 128], BF16, nsub=2)
            tp_t, tp_b = kb.alloc("toep", [128, 2, 128], BF16, nsub=2)
            tf_t, tf_b = kb.alloc("toepf", [128, 2, 128], F32)
            rm_t, rm_b = kb.alloc("rm", [128, 2, 4, 257], BF16, nsub=2)
            mt1_t, mt1_b = kb.alloc("mt1k", [128, 2, 2, 128], BF16, nsub=2)
            mtf2_t, mtf2_b = kb.alloc("mtf2", [128, 2, 128], F32)

            def build_C(gd, dst_ap, ta_i, k0, i2):
                ta = TC_t[:, ta_i, gd, k0:k0 + 8, None].broadcast_to([128, 8, 16])
                tb_ = TC_t[:, ta_i + 1, gd, k0:k0 + 8, None].broadcast_to([128, 8, 16])
                cre = CC_t[:, 0, gd, None, :].broadcast_to([128, 8, 16])
                cim = CC_t[:, 1, gd, None, :].broadcast_to([128, 8, 16])
                v = lambda k_: cm_t[:, k_, :].rearrange("p (t c) -> p t c", c=16)
                tt("dve", v(0), ta, cre, ALU.mult, [TC_b, CC_b], [cm_b])
                tt("dve", v(1), tb_, cim, ALU.mult, [TC_b, CC_b], [cm_b])
                tt("pool", dst_ap.rearrange("p (t c) -> p t c", c=16), v(0), v(1), ALU.add, [cm_b], [cpw_b[i2]])

            def y_stageA(g):
                i2 = g % 2
                for dd in range(2):
                    gd = dd * 32 + g
                    build_C(gd, cpw_t[:, i2, 3 * dd + 0, :], 0, 8, i2)
                    build_C(gd, cpw_t[:, i2, 3 * dd + 1, :], 1, 8, i2)
                    build_C(gd, cpw_t[:, i2, 3 * dd + 2, :], 0, 0, i2)
                    ta = TM_t[:, 0, gd, :, None].broadcast_to([128, 8, 16])
                    tb_ = TM_t[:, 1, gd, :, None].broadcast_to([128, 8, 16])
                    bre = BB_t[:, 0, gd, None, :].broadcast_to([128, 8, 16])
                    bim = BB_t[:, 1, gd, None, :].broadcast_to([128, 8, 16])
                    v = lambda k_: mtf2_t[:, k_, :].rearrange("p (t c) -> p t c", c=16)
                    tt("dve", v(0), ta, bre, ALU.mult, [TM_b, BB_b], [mtf2_b])
                    tt("dve", v(1), tb_, bim, ALU.mult, [TM_b, BB_b], [mtf2_b])
                    tt("pool", mt1_t[:, i2, dd, :].rearrange("p (t c) -> p t c", c=16), v(0), v(1), ALU.add, [mtf2_b], [mt1_b[i2]])

            def y_stageA2(g):
                i2 = g % 2
                for dd in range(2):
                    mm(PS[4 + dd][:, 0:128], mt1_t[:, i2, dd, :], cpw_t[:, i2, 3 * dd + 2, :], True, True, [mt1_b[i2], cpw_b[i2]], [PB[4 + dd]])
                tt("dve", tf_t[:, 0, :], PS[4][:, 0:128], MSK_t[:, 0, :], ALU.mult, [PB[4], MSK_b], [tf_b])
                tt("dve", tf_t[:, 1, :], PS[5][:, 0:128], MSK_t[:, 1, :], ALU.mult, [PB[5], MSK_b], [tf_b])
                tt("pool", tf_t[:, 0, :], tf_t[:, 0, :], tf_t[:, 1, :], ALU.add, [tf_b], [tf_b])
                stt(tp_t[:, i2, :], IDF_t[:, :], DC_t[:, g:g + 1], tf_t[:, 0, :], ALU.mult, ALU.add, [IDF_b, DC_b, tf_b], [tp_b[i2]])

            def y_stageB(g):
                i2 = g % 2
                R_ = [WA_b, WB_b, CS_b]
                for ci_ in range(2):
                    ca, cb = CS_t[:, ci_, g, :], CS_t[:, ci_, 32 + g, :]
                    eng = "dve" if ci_ == 0 else "pool"
                    tt(eng, rm_t[:, i2, ci_, 0:1], WA_t[:, 0, g, 63:64], ca[:, 63:64], ALU.mult, R_, [rm_b[i2]])
                    tt(eng, rm_t[:, i2, ci_, 1:257].rearrange("p (s j) -> p s j", j=64), WA_t[:, 1:5, g, :],
                       CS_t[:, ci_, g, None, :].broadcast_to([128, 4, 64]), ALU.mult, R_, [rm_b[i2]])
                    tt(eng, rm_t[:, i2, 2 + ci_, 0:31], WB_t[:, 8, g, 30::-1], cb[:, 30::-1], ALU.mult, R_, [rm_b[i2]])
                    tt(eng, rm_t[:, i2, 2 + ci_, 31:223].rearrange("p (s j) -> p s j", j=64), WB_t[:, 7:4:-1, g, ::-1],
                       CS_t[:, ci_, 32 + g, None, ::-1].broadcast_to([128, 3, 64]), ALU.mult, R_, [rm_b[i2]])
                    tt(eng, rm_t[:, i2, 2 + ci_, 223:257], WB_t[:, 4, g, 63:29:-1], cb[:, 63:29:-1], ALU.mult, R_, [rm_b[i2]])
                pb = 6 + i2
                mm(PS[pb][:, 0:257], tp_t[:, i2, :], U_t[:, g, 0:257], True, False, [tp_b[i2], U_b], [PB[pb]])
                mm(PS[pb][:, 0:257], cpw_t[:, i2, 0, :], rm_t[:, i2, 0, :], False, False, [cpw_b[i2], rm_b[i2]], [PB[pb]])
                mm(PS[pb][:, 0:257], cpw_t[:, i2, 1, :], rm_t[:, i2, 1, :], False, False, [cpw_b[i2], rm_b[i2]], [PB[pb]])
                mm(PS[pb][:, 0:257], cpw_t[:, i2, 3, :], rm_t[:, i2, 2, :], False, False, [cpw_b[i2], rm_b[i2]], [PB[pb]])
                mm(PS[pb][:, 0:257], cpw_t[:, i2, 4, :], rm_t[:, i2, 3, :], False, True, [cpw_b[i2], rm_b[i2]], [PB[pb]])
                cp("act", Y_t[:, g, :], PS[pb][:, 0:257], [PB[pb]], [Y_b])

            y_stageA(0)
            y_stageA2(0)
            for g in range(32):
                if g + 1 < 32:
                    y_stageA(g + 1)
                y_stageB(g)
                if g + 1 < 32:
                    y_stageA2(g + 1)
            dbg_store("s5Y", Y_t[:, 0:2, :], [128, 2, 257], [Y_b], BF16)
            ckpt("s5Y")

            kb.release(mR0)
            hp_t, hp_b = kb.alloc("hpre", [128, 4, 2056], BF16)
            gw_t, gw_b = kb.alloc("gluw", [128, 4, 512], BF16)
            wload(gw_t[:, :, :], I["glu_w"].rearrange("(kt p) n -> p kt n", p=128), 4, 512, gw_b)
            gx_t, gx_b = kb.alloc("gx", [128, 2, 4, 260], F32, nsub=2)
            gi = 0
            for ct in range(4):
                for t_ in range(8):
                    i2 = gi % 2
                    gi += 1
                    pb = 4 + i2
                    for gl in range(8):
                        mm(PS[pb][:, 0:257], RS_t[:, t_, 112 - 16 * gl:240 - 16 * gl], Y_t[:, ct * 8 + gl, :], gl == 0, gl == 7,
                           [RS_b, Y_b], [PB[pb]])
                    x_ = gx_t[:, i2, 0, 0:257]
                    cp("act", x_, PS[pb][:, 0:257], [PB[pb]], [gx_b[i2]])
                    tt("pool", gx_t[:, i2, 1, 0:257], x_, x_, ALU.mult, [gx_b[i2]], [gx_b[i2]])
                    ts("dve", gx_t[:, i2, 1, 0:257], gx_t[:, i2, 1, 0:257], 0.044715, 1.0, ALU.mult, ALU.add, [gx_b[i2]], [gx_b[i2]])
                    tt("pool", gx_t[:, i2, 1, 0:257], gx_t[:, i2, 1, 0:257], x_, ALU.mult, [gx_b[i2]], [gx_b[i2]])
                    act(gx_t[:, i2, 2, 0:257], gx_t[:, i2, 1, 0:257], AF.Sigmoid, [gx_b[i2]], [gx_b[i2]], scale=2.0 * math.sqrt(2.0 / math.pi))
                    tt("dve", hp_t[:, ct, t_:2056:8], x_, gx_t[:, i2, 2, 0:257], ALU.mult, [gx_b[i2]], [hp_b])
            hsT_t, hsT_b = kb.alloc("hsTs", [128, 4, NQ + 7], BF16)
            sgl_t, sgl_b = kb.alloc("sgl", [128, 2, 512], F32, nsub=2)
            gi = 0
            for (q0, nq) in ((0, 512), (512, 512), (1024, 512), (1536, 512), (2048, 8)):
                for c2 in range(4):
                    i2 = gi % 2
                    gi += 1
                    pb = 6 + i2
                    for ct in range(4):
                        mm(PS[pb][:, 0:nq], gw_t[:, ct, c2 * 128:(c2 + 1) * 128], hp_t[:, ct, q0:q0 + nq], ct == 0, ct == 3, [gw_b, hp_b], [PB[pb]])
                    act(sgl_t[:, i2, 0:nq], PS[pb][:, 0:nq], AF.Sigmoid, [PB[pb], V1_b], [sgl_b[i2]], bias=V1_t[:, GLUB + c2:GLUB + c2 + 1])
                    tt("dve", hsT_t[:, c2, q0:q0 + nq], hp_t[:, c2, q0:q0 + nq], sgl_t[:, i2, 0:nq], ALU.mult, [hp_b, sgl_b[i2]], [hsT_b])
            dbg_store("s5hs", hsT_t[:, :, 0:NQ], [128, 4, NQ], [hsT_b], BF16)
            kb.dma("sp", hsT_spill.rearrange("p (k n) -> p k n", k=4), hsT_t[:, :, :], R=[hsT_b], W=[hsT_spill_b])
            ckpt("s5end")
            kb.release(mR0)
            kb.release_top(SB_LIMIT)
            alloc_oT()
            alloc_hT()
            for kt in range(8):
                kb.dma("sp", hT_tiles["o"][:, kt, :], hT_spill[:, kt * NALL:kt * NALL + HSPLIT], R=[hT_spill_b], W=hT_b[0:17])
                kb.dma("sp", hT_tiles["r"][:, kt, :], hT_spill[:, kt * NALL + HSPLIT:(kt + 1) * NALL], R=[hT_spill_b], W=hT_b[17:34])

        if S5_ON:
            s5_stage()
        oT_t, oT_b = oT_box["t"], oT_box["b"]


        mA = kb.mark()
        lv_t, lv_b = kb.alloc("lv", [128, 4, 64], F32)
        kb.dma("sp", lv_t[:, :, :], I["lamv"].rearrange("a b -> (a b)").partition_broadcast(128).rearrange("p (a b) -> p a b", a=4), W=[lv_b])
        sm_t, sm_b = kb.alloc("sm", [128, 16], F32)
        tt("dve", lv_t[:, 0, :], lv_t[:, 0, :], lv_t[:, 1, :], ALU.mult, [lv_b], [lv_b])
        tt("dve", lv_t[:, 2, :], lv_t[:, 2, :], lv_t[:, 3, :], ALU.mult, [lv_b], [lv_b])
        kb.op("dve", lambda e: e.reduce_sum(sm_t[:, 0:1], lv_t[:, 0, :], AX.X), [lv_b], [sm_b])
        kb.op("dve", lambda e: e.reduce_sum(sm_t[:, 1:2], lv_t[:, 2, :], AX.X), [lv_b], [sm_b])
        act(sm_t[:, 2:4], sm_t[:, 0:2], AF.Exp, [sm_b], [sm_b])
        tt("dve", sm_t[:, 4:5], sm_t[:, 3:4], sm_t[:, 2:3], ALU.subtract, [sm_b], [sm_b])
        ts("dve", sm_t[:, 5:6], sm_t[:, 4:5], -0.2, None, ALU.add, None, [sm_b], [sm_b])
        NEGLAM = sm_t[:, 5:6]
        qk_t, qk_b = kb.alloc("qkw", [128, 2, 64], F32)
        kb.dma("sp", qk_t[:, 0, :], I["q_norm_w"].partition_broadcast(128), W=[qk_b])
        kb.dma("sp", qk_t[:, 1, :], I["k_norm_w"].partition_broadcast(128), W=[qk_b])
        kb.op("dve", lambda e: e.reduce_max(sm_t[:, 6:7], qk_t[:, 0, :], AX.X, apply_absolute_value=True), [qk_b], [sm_b])
        kb.op("dve", lambda e: e.reduce_max(sm_t[:, 7:8], qk_t[:, 1, :], AX.X, apply_absolute_value=True), [qk_b], [sm_b])
        tt("dve", sm_t[:, 8:9], sm_t[:, 6:7], sm_t[:, 7:8], ALU.mult, [sm_b], [sm_b])
        ts("dve", sm_t[:, 9:10], sm_t[:, 8:9], -8.0, None, ALU.mult, None, [sm_b], [sm_b])
        NBIAS = sm_t[:, 9:10]
        memset("pool", sm_t[:, 10:11], -0.5, [sm_b])
        MHALF = sm_t[:, 10:11]
        wcol_t, wcol_b = kb.alloc("wcol", [128, 2], F32)
        for half in range(2):
            kb.dma("sp", wcol_t[half * 64:(half + 1) * 64, 0:1], I["q_norm_w"].rearrange("(a b) -> a b", b=1), W=[wcol_b])
            kb.dma("sp", wcol_t[half * 64:(half + 1) * 64, 1:2], I["k_norm_w"].rearrange("(a b) -> a b", b=1), W=[wcol_b])
        sw_t, sw_b = kb.alloc("swbc", [128, 128], F32)
        kb.dma("sp", sw_t[:, :], I["subln_w"].partition_broadcast(128), W=[sw_b])
        ts("dve", sw_t[:, :], sw_t[:, :], 0.8, None, ALU.mult, None, [sw_b], [sw_b])
        bones_t, bones_b = kb.alloc("bones", [128, 128], BF16)
        memset("pool", bones_t[:, :], 0.0, [bones_b])
        memset("pool", bones_t[0:64, 0:64], 1.0, [bones_b])
        memset("pool", bones_t[64:128, 64:128], 1.0, [bones_b])
        prot_t, prot_b = kb.alloc("prot", [128, 128], BF16)
        memset("pool", prot_t[:, :], 0.0, [prot_b])
        prot_v = prot_t[:, :].rearrange("p (b h i) -> p b h i", b=4, h=2)
        asel(prot_v[:, :, 0, :], prot_v[:, :, 0, :], [[-32, 4], [-1, 16]], ALU.not_equal, -1.0, -16, 1, [prot_b], [prot_b])
        asel(prot_v[:, :, 1, :], prot_v[:, :, 1, :], [[-32, 4], [-1, 16]], ALU.not_equal, 1.0, 0, 1, [prot_b], [prot_b])

        def sin_turns(out_ap, x_ap, mul, add, n, R, W, tmps):
            (t_ap, f_ap, g_ap, k_ap, tb) = tmps
            ts("dve", t_ap, x_ap, mul, add, ALU.mult, ALU.add, R, [tb])
            cp("dve", k_ap, t_ap, [tb], [tb])
            cp("dve", f_ap, k_ap, [tb], [tb])
            tt("dve", f_ap, t_ap, f_ap, ALU.subtract, [tb], [tb])
            ts("dve", g_ap, f_ap, 0.5, None, ALU.is_gt, None, [tb], [tb])
            tt("dve", f_ap, f_ap, g_ap, ALU.subtract, [tb], [tb])
            ts("dve", g_ap, f_ap, -0.5, None, ALU.is_lt, None, [tb], [tb])
            tt("dve", f_ap, f_ap, g_ap, ALU.add, [tb], [tb])
            act(out_ap, f_ap, AF.Sin, [tb], W, scale=6.2831)

        cos_t, cos_b = kb.alloc("ropecos", [128, NLAT], BF16)
        sin_t, sin_b = kb.alloc("ropesin", [128, NLAT], BF16)
        mR = kb.mark()
        pi_t, pi_b = kb.alloc("posinfo", [128, 2], F32)
        kb.dma("sp", pi_t[:, :], I["posinfo"].partition_broadcast(128), W=[pi_b])
        pidx_t, pidx_b = kb.alloc("pidx", [128, 4], I32)
        pf_t, pf_b = kb.alloc("pf", [128, 12], F32)
        iota(pidx_t[:, 0:1], [[0, 1]], 0, 1, [pidx_b])
        cp("dve", pf_t[:, 5:6], pidx_t[:, 0:1], [pidx_b], [pf_b])
        ts("dve", pf_t[:, 6:7], pf_t[:, 5:6], 64.0, None, ALU.is_ge, None, [pf_b], [pf_b])
        stt(pf_t[:, 7:8], pf_t[:, 6:7], -64.0, pf_t[:, 5:6], ALU.mult, ALU.add, [pf_b], [pf_b])
        ts("dve", pf_t[:, 1:2], pf_t[:, 7:8], 32.0, None, ALU.is_ge, None, [pf_b], [pf_b])
        stt(pf_t[:, 8:9], pf_t[:, 1:2], -32.0, pf_t[:, 7:8], ALU.mult, ALU.add, [pf_b], [pf_b])
        ts("dve", pf_t[:, 9:10], pf_t[:, 8:9], 16.0, None, ALU.is_ge, None, [pf_b], [pf_b])
        stt(pf_t[:, 0:1], pf_t[:, 9:10], -16.0, pf_t[:, 8:9], ALU.mult, ALU.add, [pf_b], [pf_b])
        act(pf_t[:, 2:3], pf_t[:, 0:1], AF.Exp, [pf_b], [pf_b], scale=-math.log(10000.0) / 16.0)
        tt("dve", pf_t[:, 4:5], pf_t[:, 2:3], pf_t[:, 1:2], ALU.mult, [pf_b], [pf_b])
        tt("dve", pf_t[:, 3:4], pf_t[:, 2:3], pf_t[:, 4:5], ALU.subtract, [pf_b], [pf_b])
        ts("dve", pf_t[:, 3:5], pf_t[:, 3:5], 1.0 / TWO_PI, None, ALU.mult, None, [pf_b], [pf_b])
        RC = 1024
        ri_t, ri_b = kb.alloc("ri", [128, RC], I32)
        rf_t, rf_b = kb.alloc("rf", [128, RC], F32)
        cf_t, cf_b = kb.alloc("cf", [128, RC], F32)
        ang_t, ang_b = kb.alloc("ang", [128, RC], F32)
        tA_t, tA_b = kb.alloc("tA", [128, RC], F32)
        tB_t, _ = kb.alloc("tB", [128, RC], F32)
        tC_t, _ = kb.alloc("tC", [128, RC], F32)
        for rc in range(NLAT // RC):
            iota(ri_t[:, :].rearrange("p (a b) -> p a b", a=RC // 64), [[1, RC // 64], [0, 64]], rc * (RC // 64), 0, [ri_b])
            cp("dve", rf_t[:, :], ri_t[:, :], [ri_b], [rf_b])
            iota(ri_t[:, :].rearrange("p (a b) -> p a b", a=RC // 64), [[0, RC // 64], [1, 64]], 0, 0, [ri_b])
            cp("dve", cf_t[:, :], ri_t[:, :], [ri_b], [cf_b])
            ts("dve", rf_t[:, :], rf_t[:, :], pi_t[:, 0:1], pi_t[:, 1:2], ALU.mult, ALU.add, [rf_b, pi_b], [rf_b])
            ts("dve", cf_t[:, :], cf_t[:, :], pi_t[:, 0:1], pi_t[:, 1:2], ALU.mult, ALU.add, [cf_b, pi_b], [cf_b])
            ts("dve", ang_t[:, :], rf_t[:, :], pf_t[:, 3:4], None, ALU.mult, None, [rf_b, pf_b], [ang_b])
            stt(ang_t[:, :], cf_t[:, :], pf_t[:, 4:5], ang_t[:, :], ALU.mult, ALU.add, [cf_b, pf_b, ang_b], [ang_b])
            tmps = (tA_t[:, :], tB_t[:, :], tC_t[:, :], ri_t[:, :], tA_b)
            sin_turns(sin_t[:, rc * RC:(rc + 1) * RC], ang_t[:, :], 1.0, 0.0, RC, [ang_b], [sin_b, ri_b], tmps)
            sin_turns(cos_t[:, rc * RC:(rc + 1) * RC], ang_t[:, :], 1.0, 0.25, RC, [ang_b], [cos_b, ri_b], tmps)
        kb.release(mR)
        dbg_store("rope", cos_t[:, 0:256], [128, 256], [cos_b], BF16)
        dbg_store("ropes", sin_t[:, 0:256], [128, 256], [sin_b], BF16)
        ckpt("rope")

        wqkv_t, wqkv_b = kb.alloc("wqkv", [128, 2, 8, 384], BF16, nsub=2)
        KT_t, KT_b = kb.alloc("KT", [128, NALL], BF16)
        QT_t, QT_b = kb.alloc("QT", [128, NQ + 7], BF16)
        V_t, V_b = kb.alloc("Vaug", [128, 34, 129], BF16)
        memset("pool", V_t[:, :, :], 1.0, [V_b])
        sq_t, sq_b = kb.alloc("sq", [128, 2, 512], BF16, nsub=2)
        kw_t, kw_b = kb.alloc("kw", [128, 2, 512], BF16, nsub=2)
        u_t, u_b = kb.alloc("uu", [128, 2, 512], F32, nsub=2)
        t1_t, t1_b = kb.alloc("t1", [128, 2, 512], F32, nsub=2)
        t2_t, t2_b = kb.alloc("t2", [128, 2, 512], F32, nsub=2)
        PT_t, PT_b = kb.alloc("PT", [128, 4, 512], BF16, nsub=4)
        o_t, o_b = kb.alloc("oacc", [128, 2, 128], F32, nsub=2)
        on_t, on_b = kb.alloc("onrm", [128, 2, 128], BF16, nsub=2)
        fs_t, fs_b = kb.alloc("fstat", [128, 2, 8], F32, nsub=2)
        ojunk_t, ojunk_b = kb.alloc("ojunk", [128, 128], F32)
        blk_i = [0]

        def load_head_w(hh):
            bi = hh % 2
            for part in range(3):
                c0 = part * 1024 + hh * 128
                wload(wqkv_t[:, bi, :, part * 128:(part + 1) * 128], w_in_v[:, :, c0:c0 + 128], 8, 128, wqkv_b[bi])

        BSETS = ((5, 6, 7), (0, 1, 2))

        def qk_P1(hh, blk, spec):
            (dst_t, dst_b, dcol, scol, n, wsel, rope, ropecol) = spec
            bi = hh % 2
            pa = BSETS[blk % 2][0]
            hb = hT_bufs(scol, n)
            for kt in range(8):
                mm(PS[pa][:, 0:n], wqkv_t[:, bi, kt, wsel * 128:(wsel + 1) * 128], hT(kt, scol, n),
                   kt == 0, kt == 7, [wqkv_b[bi]] + hb, [PB[pa]])

        def qk_P2(hh, blk, spec):
            (dst_t, dst_b, dcol, scol, n, wsel, rope, ropecol) = spec
            pa, pb_, pc = BSETS[blk % 2]
            i2 = blk % 2
            act(sq_t[:, i2, 0:n], PS[pa][:, 0:n], AF.Square, [PB[pa]], [sq_b[i2]])
            act(kw_t[:, i2, 0:n], PS[pa][:, 0:n], AF.Copy, [PB[pa], wcol_b], [kw_b[i2]], scale=wcol_t[:, wsel:wsel + 1])
            mm(PS[pb_][:, 0:n], bones_t[:, :], sq_t[:, i2, 0:n], True, True, [bones_b, sq_b[i2]], [PB[pb_]])
            if rope:
                mm(PS[pc][:, 0:n], prot_t[:, :], kw_t[:, i2, 0:n], True, True, [prot_b, kw_b[i2]], [PB[pc]])
            act(u_t[:, i2, 0:n], PS[pb_][:, 0:n], AF.Ln, [PB[pb_], EPS_b], [u_b[i2]], bias=EPS_t[:, 0:1], scale=1.0 / 64.0)
            act(u_t[:, i2, 0:n], u_t[:, i2, 0:n], AF.Exp, [u_b[i2]], [u_b[i2]], scale=-0.5)
            if rope:
                tt("pool", t1_t[:, i2, 0:n], kw_t[:, i2, 0:n], cos_t[:, ropecol:ropecol + n], ALU.mult, [kw_b[i2], cos_b], [t1_b[i2]])
                tt("dve", t2_t[:, i2, 0:n], PS[pc][:, 0:n], sin_t[:, ropecol:ropecol + n], ALU.mult, [PB[pc], sin_b], [t2_b[i2]])
                tt("dve", t2_t[:, i2, 0:n], t2_t[:, i2, 0:n], t1_t[:, i2, 0:n], ALU.add, [t1_b[i2], t2_b[i2]], [t2_b[i2]])
                tt("dve", dst_t[:, dcol:dcol + n], t2_t[:, i2, 0:n], u_t[:, i2, 0:n], ALU.mult, [t2_b[i2], u_b[i2]], [dst_b])
            else:
                tt("dve", dst_t[:, dcol:dcol + n], kw_t[:, i2, 0:n], u_t[:, i2, 0:n], ALU.mult, [kw_b[i2], u_b[i2]], [dst_b])

        def qk_all(hh):
            specs = []
            for (c0_, n_) in ((0, 512), (512, 512), (1024, 512), (1536, 512), (2048, 128), (2176, 512), (2688, 512), (3200, 512), (3712, 384)):
                specs.append((KT_t, KT_b, c0_, c0_, n_, 1, True, c0_))
            specs.append((KT_t, KT_b, 4096, 4096, 256, 1, False, 0))
            for tb in range(4):
                specs.append((QT_t, QT_b, tb * 512, tb * 512, 512, 0, True, tb * 512))
            specs.append((QT_t, QT_b, 2048, 2048, 1, 0, True, 2048))
            qk_P1(hh, 0, specs[0])
            for i_, sp in enumerate(specs):
                if i_ + 1 < len(specs):
                    qk_P1(hh, i_ + 1, specs[i_ + 1])
                qk_P2(hh, i_, sp)
                if i_ < 9:
                    v_group(hh, i_)

        def v_group(hh, g4):
            bi = hh % 2
            nt = 4 if g4 < 8 else 2
            pb = 3 + (g4 % 2)
            for j in range(nt):
                kti = g4 * 4 + j
                for kt in range(8):
                    mm(PS[pb][:, j * 128:(j + 1) * 128], hT(kt, kti * 128, 128),
                       wqkv_t[:, bi, kt, 256:384], kt == 0, kt == 7, [wqkv_b[bi], hT_b[kti]], [PB[pb]])
            eng = "act" if g4 % 2 == 0 else "dve"
            cp(eng, V_t[:, g4 * 4:g4 * 4 + nt, 0:128], PS[pb][:, 0:nt * 128].rearrange("p (j e) -> p j e", e=128),
               [PB[pb]], [V_b])

        def acc_ap(m, j, rows):
            if j < 3:
                return PS[2 + m][0:rows, j * 129:(j + 1) * 129], PB[2 + m]
            return PS[4][0:rows, m * 129:(m + 1) * 129], PB[4]

        pt_i = [0]
        fin_i = [0]

        accS_t, accS_b = kb.alloc("accS", [128, 2, 8, 129], F32, nsub=2)

        def attn_head(hh):
            qblocks = [(0, 512), (512, 512), (1024, 512), (1536, 512), (2048, 1)]
            pending = []
            for qbi, (q0, nq) in enumerate(qblocks):
                nj = (nq + 127) // 128
                SB = (0, 1, 5, 6)

                def st_mm(kk, m, q0=q0, nq=nq):
                    sbk = SB[(kk % 2) * 2 + m]
                    mm(PS[sbk][:, 0:nq], KT_t[m * 64:(m + 1) * 64, kk * 128:(kk + 1) * 128],
                       QT_t[m * 64:(m + 1) * 64, q0:q0 + nq], True, True, [KT_b, QT_b], [PB[sbk]])

                st_mm(0, 0)
                st_mm(0, 1)
                for kk in range(34):
                    if kk + 1 < 34:
                        st_mm(kk + 1, 0)
                        st_mm(kk + 1, 1)
                    pis = []
                    for m in range(2):
                        sbk = SB[(kk % 2) * 2 + m]
                        pi_ = pt_i[0] % 4
                        pt_i[0] += 1
                        pis.append(pi_)
                        act(PT_t[:, pi_, 0:nq], PS[sbk][:, 0:nq], AF.Exp, [PB[sbk], sm_b], [PT_b[pi_]], bias=NBIAS, scale=0.125)
                    for m in range(2):
                        pi_ = pis[m]
                        for j in range(nj):
                            rows = min(128, nq - j * 128)
                            ap, pbuf = acc_ap(m, j, rows)
                            mm(ap, PT_t[:, pi_, j * 128:j * 128 + rows], V_t[:, kk, :], kk == 0, kk == 33,
                               [PT_b[pi_], V_b], [pbuf])
                    while pending and pending[0][0] <= kk:
                        pending.pop(0)[1]()
                ai = qbi % 2
                r0 = min(128, nq)
                nj3 = min(nj, 3)
                for m in range(2):
                    cp("dve", accS_t[0:r0, ai, m * 4:m * 4 + nj3, :], PS[2 + m][0:r0, 0:nj3 * 129].rearrange("p (j e) -> p j e", e=129),
                       [PB[2 + m]], [accS_b[ai]])
                if nj == 4:
                    cp("dve", accS_t[:, ai, 3:8:4, :], PS[4][:, 0:258].rearrange("p (m e) -> p m e", e=129), [PB[4]], [accS_b[ai]])

                def fin_dve(j, fi, q0=q0, nq=nq, ai=ai):
                    rows = min(128, nq - j * 128)
                    a0 = accS_t[0:rows, ai, j, :]
                    a1 = accS_t[0:rows, ai, 4 + j, :]
                    fs = fs_t[0:rows, fi, :]
                    fb = fs_b[fi]
                    ab = accS_b[ai]
                    recip(fs[:, 0:1], a0[:, 128:129], [ab], [fb])
                    recip(fs[:, 1:2], a1[:, 128:129], [ab], [fb])
                    tt("dve", fs[:, 2:3], fs[:, 1:2], NEGLAM[0:rows, :], ALU.mult, [fb, sm_b], [fb])
                    ts("dve", o_t[0:rows, fi, :], a0[:, 0:128], fs[:, 0:1], None, ALU.mult, None, [ab, fb], [o_b[fi]])
                    stt(o_t[0:rows, fi, :], a1[:, 0:128], fs[:, 2:3], o_t[0:rows, fi, :], ALU.mult, ALU.add, [ab, fb, o_b[fi]], [o_b[fi]])
                    stt(ojunk_t[0:rows, :], o_t[0:rows, fi, :], 1.0, o_t[0:rows, fi, :], ALU.mult, ALU.mult,
                        [o_b[fi]], [ojunk_b, fb], accum=fs[:, 3:4])
                    ts("dve", fs[:, 4:5], fs[:, 3:4], 1.0 / 128.0, EPS, ALU.mult, ALU.add, [fb], [fb])
                    tt("pool", fs[:, 5:6], fs[:, 4:5], MHALF[0:rows, :], ALU.pow, [fb, sm_b], [fb])
                    stt(on_t[0:rows, fi, :], o_t[0:rows, fi, :], fs[:, 5:6], sw_t[0:rows, :], ALU.mult, ALU.mult,
                        [o_b[fi], fb, sw_b], [on_b[fi]])

                def fin_pe(j, fi, q0=q0, nq=nq, hh=hh):
                    rows = min(128, nq - j * 128)
                    pv = psbf(7)
                    tr(pv[:, 0:rows], on_t[0:rows, fi, :], ident_t[0:rows, 0:rows], [on_b[fi], ident_b], [PB[7]])
                    cp("dve", oT_t[:, hh, q0 + j * 128:q0 + j * 128 + rows], pv[:, 0:rows], [PB[7]], [oT_b])

                while pending:
                    pending.pop(0)[1]()
                for j in range(nj):
                    fi = fin_i[0] % 2
                    fin_i[0] += 1
                    pending.append((2 + 7 * j, (lambda j=j, fi=fi, f=fin_dve: f(j, fi))))
                    pending.append((7 + 7 * j, (lambda j=j, fi=fi, f=fin_pe: f(j, fi))))
                pending.sort(key=lambda t: t[0])
            while pending:
                pending.pop(0)[1]()

        ckpt("b_alloc")
        load_head_w(0)
        ckpt("b_ld")
        for hh in range(N_HEADS_RUN):
            if hh + 1 < 8:
                load_head_w(hh + 1)
            ckpt("b_ld2")
            qk_all(hh)
            if hh == 0:
                dbg_store("KT", KT_t[:, 0:256], [128, 256], [KT_b], BF16)
                dbg_store("KTc", KT_t[:, 4096:4352], [128, 256], [KT_b], BF16)
                dbg_store("QT", QT_t[:, 0:256], [128, 256], [QT_b], BF16)
                dbg_store("V", V_t[:, 0:2, :], [128, 2, 129], [V_b], BF16)
                ckpt("proj0")
            attn_head(hh)
            if hh == 0:
                ckpt("attn0")
        kb.release(hT_tiles["m"])
        dbg_store("oT", oT_t[:, :, :], [128, 8, NQ], [oT_b], BF16)
        ckpt("attn")

        qblocks = [(0, 512), (512, 512), (1024, 512), (1536, 512), (2048, 1)]
        h2T_t, h2T_b = kb.alloc("h2T", [128, 8, NQ + 7], BF16, top=True)
        mT_t, mT_b = kb.alloc("mT", [128, 8, NQ], BF16, top=True)
        mM = kb.mark()
        hsT_t, hsT_b = kb.alloc("hsT", [128, 4, NQ + 7], BF16)
        if not S5_ON:
            memset("pool", hsT_t[:, :, :], 0.0, [hsT_b])
        else:
            kb.dma("sp", hsT_t[:, :, :], hsT_spill.rearrange("p (k n) -> p k n", k=4), R=[hsT_spill_b], W=[hsT_b])
        wm_t, wm_b = kb.alloc("wm", [128, 2, 28, 128], BF16, nsub=2)
        sg_t, sg_b_ = kb.alloc("sg", [128, 1, 2, 512], F32)
        sg_b = [sg_b_, sg_b_]
        w_bs_v = I["w_bs"].rearrange("(kt p) n -> p kt n", p=128)
        w_ba_v = I["w_ba"].rearrange("(kt p) n -> p kt n", p=128)

        def load_merge_w(ft):
            bi = ft % 2
            wload(wm_t[:, bi, 0:4, :], w_bs_v[:, :, ft * 128:(ft + 1) * 128], 4, 128, wm_b[bi])
            wload(wm_t[:, bi, 4:12, :], w_ba_v[:, :, ft * 128:(ft + 1) * 128], 8, 128, wm_b[bi])
            wload(wm_t[:, bi, 12:20, :], w_in_v[:, :, 3584 + ft * 128:3584 + (ft + 1) * 128], 8, 128, wm_b[bi])
            wload(wm_t[:, bi, 20:28, :], w_in_v[:, :, 4608 + ft * 128:4608 + (ft + 1) * 128], 8, 128, wm_b[bi])

        load_merge_w(0)
        mi = 0
        for ft in range(8):
            if ft + 1 < 8:
                load_merge_w(ft + 1)
            bi = ft % 2
            for (q0, nq) in qblocks:
                i2 = mi % 2
                mi += 1
                hb = hT_bufs(q0, nq)
                for kt in range(4):
                    mm(PS[0][:, 0:nq], wm_t[:, bi, kt, :], hsT_t[:, kt, q0:q0 + nq], kt == 0, kt == 3, [wm_b[bi], hsT_b], [PB[0]])
                for kt in range(8):
                    mm(PS[1][:, 0:nq], wm_t[:, bi, 4 + kt, :], oT_t[:, kt, q0:q0 + nq], kt == 0, kt == 7, [wm_b[bi], oT_b], [PB[1]])
                for kt in range(8):
                    mm(PS[2][:, 0:nq], wm_t[:, bi, 12 + kt, :], hT(kt, q0, nq), kt == 0, kt == 7, [wm_b[bi]] + hb, [PB[2]])
                for kt in range(8):
                    mm(PS[3][:, 0:nq], wm_t[:, bi, 20 + kt, :], hT(kt, q0, nq), kt == 0, kt == 7, [wm_b[bi]] + hb, [PB[3]])
                act(sg_t[:, 0, 0, 0:nq], PS[2][:, 0:nq], AF.Sigmoid, [PB[2], V1_b], [sg_b[i2]], bias=V1_t[:, BG + ft:BG + ft + 1])
                act(sg_t[:, 0, 1, 0:nq], PS[3][:, 0:nq], AF.Sigmoid, [PB[3], V1_b], [sg_b[i2]], bias=V1_t[:, BG + 8 + ft:BG + 9 + ft])
                tt("dve", sg_t[:, 0, 0, 0:nq], PS[0][:, 0:nq], sg_t[:, 0, 0, 0:nq], ALU.mult, [PB[0], sg_b[i2]], [sg_b[i2]])
                tt("dve", sg_t[:, 0, 1, 0:nq], PS[1][:, 0:nq], sg_t[:, 0, 1, 0:nq], ALU.mult, [PB[1], sg_b[i2]], [sg_b[i2]])
                tt("pool", mT_t[:, ft, q0:q0 + nq], sg_t[:, 0, 0, 0:nq], sg_t[:, 0, 1, 0:nq], ALU.add, [sg_b[i2]], [mT_b])
        kb.release(mR0)
        dbg_store("mT", mT_t[:, :, :], [128, 8, NQ], [mT_b], BF16)
        ckpt("merge")

        mX = kb.mark()
        wo_t, wo_b = kb.alloc("wo", [128, 8, D], BF16)
        wload(wo_t[:, :, :], I["w_out"].rearrange("(kt p) n -> p kt n", p=128), 8, D, wo_b)
        xr_t, xr_b = kb.alloc("xr", [128, 2, D], F32, nsub=2)
        xm_t, xm_b = kb.alloc("xm", [128, 2, D], F32, nsub=2)
        xn2_t, xn2_b = kb.alloc("xn2", [128, 2, D], BF16, nsub=2)
        st2_t, st2_b = kb.alloc("st2", [128, 2, 4], F32, nsub=2)
        xjunk_t, xjunk_b = kb.alloc("xjunk", [128, D], F32)
        xm_out = [Buf(f"xmout{i}") for i in range(16)]
        for ti in range(17):
            rows = 128 if ti < 16 else 1
            c0 = ti * 128
            i2 = ti % 2
            src = I["xo"][c0:c0 + rows, :] if ti < 16 else I["xt"][0:1, :]
            kb.dma("sp", xr_t[0:rows, i2, :], src, W=[xr_b[i2]])
            for hf in range(2):
                pb = 4 + hf
                for kt in range(8):
                    mm(PS[pb][0:rows, :], mT_t[:, kt, c0:c0 + rows], wo_t[:, kt, hf * 512:(hf + 1) * 512], kt == 0, kt == 7,
                       [mT_b, wo_b], [PB[pb]])
                tt("dve", xm_t[0:rows, i2, hf * 512:(hf + 1) * 512], PS[pb][0:rows, :], ga_t[0:rows, 0, hf * 512:(hf + 1) * 512],
                   ALU.mult, [PB[pb], ga_b], [xm_b[i2]])
            tt("pool", xm_t[0:rows, i2, :], xm_t[0:rows, i2, :], xr_t[0:rows, i2, :], ALU.add, [xm_b[i2], xr_b[i2]], [xm_b[i2]])
            if ti < 16:
                kb.dma("pool", out_d[c0:c0 + 128, :], xm_t[:, i2, :], R=[xm_b[i2]], W=[xm_out[ti]])
            sv = st2_t[0:rows, i2, :]
            stt(xjunk_t[0:rows, :], xm_t[0:rows, i2, :], 1.0, xm_t[0:rows, i2, :], ALU.mult, ALU.mult, [xm_b[i2]], [xjunk_b, st2_b[i2]],
                accum=sv[:, 0:1])
            ts("dve", sv[:, 1:2], sv[:, 0:1], 1.0 / D, EPS, ALU.mult, ALU.add, [st2_b[i2]], [st2_b[i2]])
            act(sv[:, 3:4], sv[:, 1:2], AF.Ln, [st2_b[i2]], [st2_b[i2]])
            act(sv[:, 2:3], sv[:, 3:4], AF.Exp, [st2_b[i2]], [st2_b[i2]], scale=-0.5)
            ts("dve", xn2_t[0:rows, i2, :], xm_t[0:rows, i2, :], sv[:, 2:3], None, ALU.mult, None, [xm_b[i2], st2_b[i2]], [xn2_b[i2]])
            pb = 6 + (ti % 2)
            pv = psbf(pb)
            for kt in range(8):
                tr(pv[:, kt * 128:kt * 128 + rows], xn2_t[0:rows, i2, kt * 128:(kt + 1) * 128], ident_t[0:rows, 0:rows],
                   [xn2_b[i2], ident_b], [PB[pb]])
            for kt in range(8):
                ts("dve", h2T_t[:, kt, c0:c0 + rows], pv[:, kt * 128:kt * 128 + rows], MOD_t[:, 4, kt:kt + 1], MOD_t[:, 5, kt:kt + 1],
                   ALU.mult, ALU.add, [PB[pb], MOD_b], [h2T_b])
        kb.release(mX)
        dbg_store("h2T", h2T_t[:, :, 0:NQ], [128, 8, NQ], [h2T_b], BF16)
        ckpt("xmid")

        actT_t, actT_b = kb.alloc("actT", [128, 22, NOWN], BF16)
        mF = kb.mark()
        wup_t, wup_b = kb.alloc("wup", [128, 2, 8, 256], BF16, nsub=2)
        ya_t, ya_b = kb.alloc("ya", [128, 2, 512], F32, nsub=2)
        yg_t, yg_b = kb.alloc("yg", [128, 2, 512], F32, nsub=2)
        sl_t, sl_b = kb.alloc("sl", [128, 2, 512], F32, nsub=2)
        w_up_v = I["w_up"].rearrange("(kt p) n -> p kt n", p=128)

        def load_up_w(fc):
            bi = fc % 2
            wload(wup_t[:, bi, :, 0:128], w_up_v[:, :, fc * 128:(fc + 1) * 128], 8, 128, wup_b[bi])
            wload(wup_t[:, bi, :, 128:256], w_up_v[:, :, DFF + fc * 128:DFF + (fc + 1) * 128], 8, 128, wup_b[bi])

        load_up_w(0)
        wd_v = I["w_down"].rearrange("(kt p) n -> p kt n", p=128)
        fblocks = [(0, 500), (500, 1000), (1000, 1500), (1500, 2000), (2000, 2048)]
        fi_ = 0
        for fc in range(22):
            if fc + 1 < 22:
                load_up_w(fc + 1)
            bi = fc % 2
            for (s0, e0) in fblocks:
                i2 = fi_ % 2
                fi_ += 1
                cin = max(s0 - 1, 0)
                nin = e0 + 1 - cin
                L = e0 - s0
                off = s0 - cin
                pa, pg = (0, 1) if i2 == 0 else (2, 3)
                for kt in range(8):
                    mm(PS[pa][:, 0:nin], wup_t[:, bi, kt, 0:128], h2T_t[:, kt, cin:cin + nin], kt == 0, kt == 7, [wup_b[bi], h2T_b], [PB[pa]])
                for kt in range(8):
                    mm(PS[pg][:, 0:nin], wup_t[:, bi, kt, 128:256], h2T_t[:, kt, cin:cin + nin], kt == 0, kt == 7, [wup_b[bi], h2T_b], [PB[pg]])
                for (pp, y_t, y_b, fcol) in ((pa, ya_t, ya_b, fc), (pg, yg_t, yg_b, 22 + fc)):
                    yv = y_t[:, i2, 0:L]
                    act(yv, PS[pp][:, off:off + L], AF.Identity, [PB[pp], V2_b], [y_b[i2]],
                        bias=V2_t[:, 132 + fcol:133 + fcol], scale=V2_t[:, 44 + fcol:45 + fcol])
                    stt(yv, PS[pp][:, off + 1:off + 1 + L], V2_t[:, 88 + fcol:89 + fcol], yv, ALU.mult, ALU.add, [PB[pp], V2_b, y_b[i2]], [y_b[i2]])
                    lo = 1 if s0 == 0 else 0
                    stt(y_t[:, i2, lo:L], PS[pp][:, off - 1 + lo:off - 1 + L], V2_t[:, fcol:fcol + 1], y_t[:, i2, lo:L], ALU.mult, ALU.add,
                        [PB[pp], V2_b, y_b[i2]], [y_b[i2]])
                act(sl_t[:, i2, 0:L], yg_t[:, i2, 0:L], AF.Silu, [yg_b[i2]], [sl_b[i2]])
                tt("pool", actT_t[:, fc, s0:e0], sl_t[:, i2, 0:L], ya_t[:, i2, 0:L], ALU.mult, [sl_b[i2], ya_b[i2]], [actT_b])
        dbg_store("actT", actT_t[:, :, 0:256], [128, 22, 256], [actT_b], BF16)
        ckpt("ffn_up")
        kb.release(mF)
        kb.release_top(SB_LIMIT)
        wd_t, wd_b = kb.alloc("wd", [128, 22, D], BF16)
        wload(wd_t[:, :, :], wd_v[:, :, :], 22, D, wd_b, cast_engs=("pool", "dve", "act"))
        xo2_t, xo2_b = kb.alloc("xo2", [128, 2, D], F32, nsub=2)
        fo_t, fo_b = kb.alloc("fo", [128, 2, D], F32, nsub=2)
        for ti in range(16):
            c0 = ti * 128
            i2 = ti % 2
            kb.dma("sp", xo2_t[:, i2, :], out_d[c0:c0 + 128, :], R=[xm_out[ti]], W=[xo2_b[i2]])
            for hf in range(2):
                pb = 4 + hf + 2 * (ti % 2)
                for fc in range(22):
                    mm(PS[pb][:, :], actT_t[:, fc, c0:c0 + 128], wd_t[:, fc, hf * 512:(hf + 1) * 512], fc == 0, fc == 21,
                       [actT_b, wd_b], [PB[pb]])
                tt("dve", fo_t[:, i2, hf * 512:(hf + 1) * 512], PS[pb][:, :], ga_t[:, 1, hf * 512:(hf + 1) * 512], ALU.mult,
                   [PB[pb], ga_b], [fo_b[i2]])
            tt("pool", fo_t[:, i2, :], fo_t[:, i2, :], xo2_t[:, i2, :], ALU.add, [fo_b[i2], xo2_b[i2]], [fo_b[i2]])
            ob = Buf(f"outf{ti}")
            kb.dma("pool", out_d[c0:c0 + 128, :], fo_t[:, i2, :], R=[fo_b[i2], xo2_b[i2]], W=[ob, xm_out[ti]])
            obufs.append(ob)

    try:
        body()
    except _Stop:
        pass

    kb.wait_all("sp", obufs)
    kb.emit()
    print("SBUF peak bytes/partition:", kb.sb_peak - SB_BASE, " ops:", {e: len(kb.q[e]) for e in ENGS})
    return nc, dbg_out


def make_in_maps(inputs):
    x = np.asarray(inputs["x"], np.float32)
    ctx = np.asarray(inputs["ctx"], np.float32)
    c = np.asarray(inputs["c"], np.float32)
    c_ctx = np.asarray(inputs["c_ctx"], np.float32)
    g = lambda k: np.ascontiguousarray(np.asarray(inputs[k], np.float32)[0])
    lamv = np.stack([g("lam_q1"), g("lam_k1"), g("lam_q2"), g("lam_k2")], 0)
    common = {
        "ada_w": g("ada_w"), "ada_b": g("ada_b"), "norm1_w": g("norm1_w"), "w_in": g("w_in"),
        "b_gate": g("b_gate"), "q_norm_w": g("q_norm_w"), "k_norm_w": g("k_norm_w"), "lamv": lamv,
        "subln_w": g("subln_w"), "s5_d": g("s5_d"), "glu_w": g("glu_w"), "glu_b": g("glu_b"),
        "w_bs": g("w_branch_s5"), "w_ba": g("w_branch_attn"), "w_out": g("w_out"), "norm2_w": g("norm2_w"),
        "w_up": g("w_up"), "conv_b": g("conv_b"), "w_down": g("w_down"),
    }
    maps = []
    for cid in range(8):
        b, h = cid // 2, cid % 2
        m = dict(common)
        if h == 0:
            m["xo"] = np.ascontiguousarray(x[b, 0:NOWN])
            m["xt"] = np.ascontiguousarray(x[b, NOWN:])
            m["cx"] = np.ascontiguousarray(ctx[b])
            do = [0, 1]
            m["conv_w"] = g("conv_w")
            m["posinfo"] = np.array([1.0, 0.0], np.float32)
        else:
            m["xo"] = np.ascontiguousarray(x[b, :NOWN - 1:-1])
            m["xt"] = np.ascontiguousarray(x[b, NOWN - 1::-1])
            m["cx"] = np.ascontiguousarray(ctx[b, ::-1])
            do = [1, 0]
            m["conv_w"] = np.ascontiguousarray(g("conv_w")[::-1])
            m["posinfo"] = np.array([-1.0, 63.0], np.float32)
        m["cvec"] = np.ascontiguousarray(np.stack([c[b], c_ctx], 0))
        for k in ("s5_a_re", "s5_a_im", "s5_b_re", "s5_b_im", "s5_c_re", "s5_c_im"):
            m[k] = np.ascontiguousarray(g(k)[do])
        m["s5_log_dt"] = np.ascontiguousarray(g("s5_log_dt")[do].reshape(64))
        maps.append(m)
    return maps


def assemble(results):
    out = np.zeros((4, NLAT, D), np.float32)
    for cid in range(8):
        b, h = cid // 2, cid % 2
        r = results[cid]["out"]
        if h == 0:
            out[b, 0:NOWN] = r
        else:
            out[b, NOWN:] = r[::-1]
    return out


_NC_CACHE = {}


def kernel(**inputs):
    if "nc" not in _NC_CACHE:
        _NC_CACHE["nc"] = build()[0]
    nc = _NC_CACHE["nc"]
    res = run_bass_kernel_spmd(nc, make_in_maps(inputs), core_ids=list(range(8)))
    return assemble(res.results)
```

```python
import contextlib
import math
import numpy as np
import concourse.bass as bass
import concourse.mybir as mybir
from concourse.bass_utils import run_bass_kernel_spmd

F32 = mybir.dt.float32
BF16 = mybir.dt.bfloat16
I32 = mybir.dt.int32
AF = mybir.ActivationFunctionType
ALU = mybir.AluOpType
AX = mybir.AxisListType
DT_SIZE = {F32: 4, BF16: 2, I32: 4}
ENGS = ("pe", "act", "dve", "pool", "sp")
N_DSEM = 24
SB_BASE = 16512
SB_LIMIT = 229344

D = 1024
NOWN = 2048
NQ = 2049
NLAT = 4096
NCTX = 256
NALL = 4352
NIN = 5632
DFF = 2816
EPS = 1e-6
S5_ON = True
N_HEADS_RUN = 8
TWO_PI = 2.0 * math.pi


class Buf:
    __slots__ = ("name", "lw", "rd", "alias", "wd", "excl")

    def __init__(self, name, excl=False):
        self.name = name
        self.excl = excl
        self.lw = None
        self.rd = {}
        self.alias = []
        self.wd = {}


class KB:
    def __init__(self, nc):
        self.nc = nc
        self.q = {e: [] for e in ENGS}
        self.cnt = {e: 0 for e in ENGS}
        self.seen = {e: {} for e in ENGS}
        self.dval = [0] * N_DSEM
        self.dnext = 0
        self.targets = {e: set() for e in ENGS}
        self.sb_ptr = SB_BASE
        self.top_ptr = SB_LIMIT
        self.sb_hist = []
        self.sb_peak = SB_BASE
        self.uid = 0

    def alloc(self, name, shape, dtype, nsub=1, top=False):
        free = 1
        for s in shape[1:]:
            free *= s
        nbytes = free * DT_SIZE[dtype]
        if top:
            end = self.top_ptr // 64 * 64
            start = (end - nbytes) // 64 * 64
            assert start >= self.sb_ptr, f"SBUF overflow (top) allocating {name}: {self.sb_ptr - start} over"
            self.top_ptr = start
        else:
            start = (self.sb_ptr + 63) // 64 * 64
            end = start + nbytes
            assert end <= self.top_ptr, f"SBUF overflow allocating {name}: {end - self.top_ptr} over"
            self.sb_ptr = end
        self.sb_peak = max(self.sb_peak, self.sb_ptr + (SB_LIMIT - self.top_ptr))
        self.uid += 1
        t = self.nc.alloc_sbuf_tensor_at(f"{name}_{self.uid}", list(shape), dtype, offset=start)
        bufs = [Buf(f"{name}.{i}") for i in range(nsub)]
        old = []
        keep = []
        for (s, e, bl) in self.sb_hist:
            if s < end and start < e:
                old.extend(bl)
            keep.append((s, e, bl))
        for b in bufs:
            b.alias = list(old)
        self.sb_hist.append((start, end, bufs))
        return (t, bufs[0]) if nsub == 1 else (t, bufs)

    def mark(self):
        return self.sb_ptr

    def release(self, m):
        self.sb_ptr = m

    def mark_top(self):
        return self.top_ptr

    def release_top(self, m):
        self.top_ptr = m

    def _deps(self, eng, R, W):
        waits = {}
        seen = self.seen[eng]

        def need(key, val):
            if seen.get(key, 0) >= val:
                return
            if waits.get(key, 0) < val:
                waits[key] = val

        def need_all(b):
            if b.lw is not None and b.lw[0] != eng:
                need(*b.lw)
            for k, v in b.wd.items():
                need(k, v)
            for k, v in b.rd.items():
                if k != eng:
                    need(k, v)

        for b in R:
            if b.alias:
                for a in b.alias:
                    need_all(a)
            if b.lw is not None and not (eng == "pe" and b.lw[0] == "pe"):
                need(*b.lw)
            for k, v in b.wd.items():
                need(k, v)
            if b.excl:
                for k, v in b.rd.items():
                    if k != eng:
                        need(k, v)
        for b in W:
            if b.alias:
                for a in b.alias:
                    need_all(a)
                b.alias = []
            need_all(b)
        for k, v in waits.items():
            seen[k] = v
            if k in self.targets:
                self.targets[k].add(v)
        return list(waits.items())

    def _mark(self, tok, R, W):
        k, v = tok
        for b in R:
            if b.rd.get(k, 0) < v:
                b.rd[k] = v
        for b in W:
            if k[0] == "d" and k[1:].isdigit():
                b.wd[k] = v
            else:
                b.wd = {}
            b.lw = tok
            b.rd = {}

    def op(self, eng, fn, R=(), W=()):
        waits = self._deps(eng, R, W)
        self.cnt[eng] += 1
        idx = self.cnt[eng]
        self._mark((eng, idx), R, W)
        self.q[eng].append((waits, fn, idx, None))

    def dma(self, eng, out_ap, in_ap, R=(), W=(), **kw):
        k = self.dnext
        self.dnext = (self.dnext + 1) % N_DSEM
        key = f"d{k}"
        waits = self._deps(eng, R, W)
        prev = self.dval[k]
        if prev > 0 and self.seen[eng].get(key, 0) < prev:
            waits.append((key, prev))
            self.seen[eng][key] = prev
        self.dval[k] = prev + 16
        self._mark((key, prev + 16), R, W)
        self.q[eng].append((waits, lambda e: e.dma_start(out=out_ap, in_=in_ap, **kw), None, k))

    def wait_all(self, eng, bufs):
        waits = self._deps(eng, bufs, ())
        self.q[eng].append((waits, None, None, None))

    def emit(self):
        nc = self.nc
        sems = {}
        with contextlib.ExitStack() as st:
            for e in ENGS:
                sems[e] = st.enter_context(nc.semaphore(f"s_{e}"))
            for k in range(N_DSEM):
                sems[f"d{k}"] = st.enter_context(nc.semaphore(f"s_d{k}"))
            cmap = {}
            for e in ENGS:
                tl = sorted(self.targets[e])
                cmap[e] = {idx: i + 1 for i, idx in enumerate(tl)}
            block = st.enter_context(nc.Block())

            def replay(e, eng):
                tg = cmap[e]
                for (waits, fn, idx, dk) in self.q[e]:
                    for (key, val) in waits:
                        v = cmap[key][val] if key in cmap else val
                        eng.wait_ge(sems[key], v)
                    if fn is None:
                        continue
                    ins = fn(eng)
                    if dk is not None:
                        ins.then_inc(sems[f"d{dk}"], 16)
                    elif idx in tg:
                        ins.then_inc(sems[e], 1)

            @block.tensor
            def _(eng):
                replay("pe", eng)

            @block.scalar
            def _(eng):
                replay("act", eng)

            @block.vector
            def _(eng):
                replay("dve", eng)

            @block.gpsimd
            def _(eng):
                replay("pool", eng)

            @block.sync
            def _(eng):
                replay("sp", eng)


INPUT_SPECS = [
    ("xo", [NOWN, D]), ("xt", [NOWN, D]), ("cx", [NCTX, D]), ("cvec", [2, D]),
    ("ada_w", [D, 6 * D]), ("ada_b", [6 * D]), ("norm1_w", [D]), ("w_in", [D, NIN]),
    ("b_gate", [2 * D]), ("q_norm_w", [64]), ("k_norm_w", [64]), ("lamv", [4, 64]),
    ("subln_w", [128]), ("s5_a_re", [2, 32, 64]), ("s5_a_im", [2, 32, 64]), ("s5_log_dt", [64]),
    ("s5_b_re", [2, 32, 64, 16]), ("s5_b_im", [2, 32, 64, 16]), ("s5_c_re", [2, 32, 16, 64]),
    ("s5_c_im", [2, 32, 16, 64]), ("s5_d", [512]), ("glu_w", [512, 512]), ("glu_b", [512]),
    ("w_bs", [512, D]), ("w_ba", [D, D]), ("w_out", [D, D]), ("norm2_w", [D]),
    ("w_up", [D, NIN]), ("conv_w", [3, NIN]), ("conv_b", [NIN]), ("w_down", [DFF, D]),
    ("posinfo", [2]),
]


class _Stop(Exception):
    pass


def build(stop=None, dbg=()):
    nc = bass.Bass("TRN2", target_bir_lowering=False)
    I = {n: nc.dram_tensor(n, s, F32, kind="ExternalInput").ap() for n, s in INPUT_SPECS}
    out_d = nc.dram_tensor("out", [NOWN, D], F32, kind="ExternalOutput").ap()
    hT_spill = nc.dram_tensor("hT_spill", [128, 8 * NALL], BF16, kind="Internal").ap()
    hsT_spill = nc.dram_tensor("hsT_spill", [128, 4 * (NQ + 7)], BF16, kind="Internal").ap()
    hsT_spill_b = Buf("hsT_spill")
    hT_spill_b = Buf("hT_spill")
    kb = KB(nc)
    dbg_out = {}
    obufs = []

    def dbg_store(name, tile_ap, shape, rbufs, dtype=F32):
        if name not in dbg:
            return
        d = nc.dram_tensor("dbg_" + name, shape, dtype, kind="ExternalOutput").ap()
        ob = Buf("dbg_" + name)
        kb.dma("sp", d, tile_ap, R=rbufs, W=[ob])
        obufs.append(ob)
        dbg_out[name] = d

    def ckpt(name):
        if stop == name:
            raise _Stop()

    PS = [nc.alloc_psum_tensor(f"psb{i}", [128, 512], F32) for i in range(8)]
    PB = [Buf(f"psb{i}", excl=True) for i in range(8)]

    def psbf(i):
        return PS[i][:].bitcast(BF16)

    def mm(out, lhsT, rhs, start, stop, R, W, **kw):
        kb.op("pe", lambda e: e.matmul(out, lhsT, rhs, start=start, stop=stop, **kw), R, W)

    def tr(out, in_, ident, R, W):
        kb.op("pe", lambda e: e.transpose(out, in_, ident), R, W)

    def act(out, in_, func, R, W, bias=0.0, scale=1.0, accum=None, eng="act"):
        if accum is None:
            kb.op(eng, lambda e: e.activation(out, in_, func, bias=bias, scale=scale), R, W)
        else:
            kb.op(eng, lambda e: e.activation(out, in_, func, bias=bias, scale=scale, accum_out=accum), R, W)

    def tt(eng, out, in0, in1, op, R, W):
        kb.op(eng, lambda e: e.tensor_tensor(out, in0, in1, op), R, W)

    def ts(eng, out, in0, s1, s2, op0, op1, R, W):
        if s2 is None:
            kb.op(eng, lambda e: e.tensor_scalar(out, in0, s1, None, op0), R, W)
        else:
            kb.op(eng, lambda e: e.tensor_scalar(out, in0, s1, s2, op0, op1), R, W)

    def stt(out, in0, scalar, in1, op0, op1, R, W, accum=None):
        if accum is None:
            kb.op("dve", lambda e: e.scalar_tensor_tensor(out, in0, scalar, in1, op0, op1), R, W)
        else:
            kb.op("dve", lambda e: e.scalar_tensor_tensor(out, in0, scalar, in1, op0, op1, accum_out=accum), R, W)

    def cp(eng, out, in_, R, W):
        if eng == "act":
            kb.op(eng, lambda e: e.copy(out, in_), R, W)
        else:
            kb.op(eng, lambda e: e.tensor_copy(out, in_), R, W)

    def memset(eng, ap, val, W):
        kb.op(eng, lambda e: e.memset(ap, val), (), W)

    def iota(ap, pattern, base, cm, W):
        kb.op("pool", lambda e: e.iota(ap, pattern, base=base, channel_multiplier=cm), (), W)

    def asel(out, in_, pattern, cmp, fill, base, cm, R, W):
        kb.op("pool", lambda e: e.affine_select(out, in_, pattern=pattern, compare_op=cmp, fill=fill,
                                                 base=base, channel_multiplier=cm), R, W)

    def recip(out, in_, R, W):
        kb.op("dve", lambda e: e.reciprocal(out, in_), R, W)

    def body():
        ident_t, ident_b = kb.alloc("ident", [128, 128], BF16)
        memset("pool", ident_t[:], 0.0, [ident_b])
        asel(ident_t[:], ident_t[:], [[-1, 128]], ALU.not_equal, 1.0, 0, 1, [ident_b], [ident_b])

        EPS_t, EPS_b = kb.alloc("eps", [128, 1], F32)
        memset("pool", EPS_t[:, :], EPS, [EPS_b])
        mh2_t, mh2_b = kb.alloc("mh2", [128, 1], F32)
        memset("pool", mh2_t[:, :], -0.5, [mh2_b])
        MHALF2 = mh2_t[:, 0:1]
        NSTG = 2
        STG_W = 1024
        stg_t, stg_b = kb.alloc("stg", [128, NSTG, STG_W], F32, nsub=NSTG)
        stg_i = [0]
        cast_rr = [0]

        def wload(dst3, src3, nrow, ncol, Wb, cast_engs=("pool",)):
            if ncol <= STG_W:
                rp = STG_W // ncol
                pieces = [(r, min(rp, nrow - r), 0, ncol) for r in range(0, nrow, rp)]
            else:
                pieces = [(r, 1, c, min(STG_W, ncol - c)) for r in range(nrow) for c in range(0, ncol, STG_W)]
            for (r, nr, c, ncc) in pieces:
                i = stg_i[0]
                stg_i[0] = (i + 1) % NSTG
                sview = stg_t[:, i, 0:nr * ncc].rearrange("p (r c) -> p r c", r=nr)
                kb.dma("sp", sview, src3[:, r:r + nr, c:c + ncc], W=[stg_b[i]])
                ce = cast_engs[cast_rr[0] % len(cast_engs)]
                cast_rr[0] += 1
                cp(ce, dst3[:, r:r + nr, c:c + ncc], sview, [stg_b[i]], [Wb])

        def rows_to_cols(dst, dst_b, row_srcs):
            m0 = kb.mark()
            total = sum(n for _, n in row_srcs)
            rs_t, rs_b = kb.alloc("rs", [128, 128], F32)
            hi_t, hi_b = kb.alloc("rhi", [128, 128], BF16)
            lo_t, lo_b = kb.alloc("rlo", [128, 128], BF16)
            tmp_t, tmp_b = kb.alloc("rtmp", [128, 128], F32)
            r0 = 0
            for ap, n in row_srcs:
                kb.dma("sp", rs_t[r0:r0 + n, :], ap, W=[rs_b])
                r0 += n
            cp("dve", hi_t[0:total, :], rs_t[0:total, :], [rs_b], [hi_b])
            tt("dve", tmp_t[0:total, :], rs_t[0:total, :], hi_t[0:total, :], ALU.subtract, [rs_b, hi_b], [tmp_b])
            cp("dve", lo_t[0:total, :], tmp_t[0:total, :], [tmp_b], [lo_b])
            pv = psbf(0)
            tr(pv[:, 0:total], hi_t[0:total, :], ident_t[0:total, 0:total], [hi_b, ident_b], [PB[0]])
            tr(pv[:, 128:128 + total], lo_t[0:total, :], ident_t[0:total, 0:total], [lo_b, ident_b], [PB[0]])
            cp("dve", tmp_t[:, 0:total], pv[:, 0:total], [PB[0]], [tmp_b])
            tt("dve", dst, tmp_t[:, 0:total], pv[:, 128:128 + total], ALU.add, [tmp_b, PB[0]], [dst_b])
            kb.release(m0)

        V1_t, V1_b = kb.alloc("V1", [128, 104], F32)
        rows_to_cols(V1_t[:, :], V1_b, [
            (I["ada_b"].rearrange("(r c) -> r c", c=128), 48),
            (I["norm1_w"].rearrange("(r c) -> r c", c=128), 8),
            (I["norm2_w"].rearrange("(r c) -> r c", c=128), 8),
            (I["b_gate"].rearrange("(r c) -> r c", c=128), 16),
            (I["glu_b"].rearrange("(r c) -> r c", c=128), 4),
            (I["s5_d"].rearrange("(r c) -> r c", c=128), 4),
            (I["cvec"].rearrange("t (r c) -> (t r) c", c=128), 16),
        ])
        ADAB, N1W, N2W, BG, GLUB, S5D, CV = 0, 48, 56, 64, 80, 84, 88
        dbg_store("V1", V1_t[:, :], [128, 104], [V1_b])
        ckpt("V1")
        V2_t, V2_b = kb.alloc("V2", [128, 176], F32)
        cwv = I["conv_w"].rearrange("j (r c) -> (j r) c", c=128)
        rows_to_cols(V2_t[:, 0:88], V2_b, [(cwv[0:88, :], 88)])
        rows_to_cols(V2_t[:, 88:176], V2_b, [(cwv[88:132, :], 44), (I["conv_b"].rearrange("(r c) -> r c", c=128), 44)])

        ckpt("V2")
        sc_t, sc_b = kb.alloc("sc", [128, 8, 2], BF16)
        scb_t, scb_b = kb.alloc("scb", [128, 8, 128], BF16)
        act(sc_t[:, :, :].rearrange("p k t -> p t k"), V1_t[:, CV:CV + 16].rearrange("p (t k) -> p t k", t=2),
            AF.Silu, [V1_b], [sc_b])
        cp("dve", scb_t[:, :, :], sc_t[:, :, 0:1].broadcast_to([128, 8, 128]), [sc_b], [scb_b])

        ckpt("silu")
        modv_t, modv_b = kb.alloc("modv", [128, 48, 2], F32)
        ga_t, ga_b = kb.alloc("ga", [128, 2, D], F32)
        m_ada = kb.mark()
        adaw_t, adaw_b = kb.alloc("adaw", [128, 2, 8, D], BF16, nsub=2)
        abb_t, abb_b = kb.alloc("abb", [128, D], F32)
        adaw_src = I["ada_w"].rearrange("(kt p) n -> p kt n", p=128)
        for ci in range(6):
            bi = ci % 2
            wload(adaw_t[:, bi], adaw_src[:, :, ci * D:(ci + 1) * D], 8, D, adaw_b[bi], cast_engs=("pool", "dve", "act", "dve"))
            for ft in range(8):
                for kt in range(8):
                    mm(PS[1][:, (ci * 8 + ft) * 2:(ci * 8 + ft) * 2 + 2], adaw_t[:, bi, kt, ft * 128:(ft + 1) * 128],
                       sc_t[:, kt, :], kt == 0, kt == 7, [adaw_b[bi], sc_b], [PB[1]])
            if ci in (2, 5):
                gi = 0 if ci == 2 else 1
                kb.dma("sp", abb_t[:, :], I["ada_b"][ci * D:(ci + 1) * D].partition_broadcast(128), W=[abb_b])
                for hf in range(2):
                    for kt in range(8):
                        mm(PS[2 + hf][:, :], scb_t[:, kt, :], adaw_t[:, bi, kt, hf * 512:(hf + 1) * 512],
                           kt == 0, kt == 7, [adaw_b[bi], scb_b], [PB[2 + hf]])
                    tt("dve", ga_t[:, gi, hf * 512:(hf + 1) * 512], PS[2 + hf][:, :], abb_t[:, hf * 512:(hf + 1) * 512],
                       ALU.add, [PB[2 + hf], abb_b], [ga_b])
        tt("dve", modv_t[:, :, :], PS[1][:, 0:96].rearrange("p (f t) -> p f t", t=2),
           V1_t[:, ADAB:ADAB + 48, None].broadcast_to([128, 48, 2]), ALU.add, [PB[1], V1_b], [modv_b])
        kb.release(m_ada)
        MOD_t, MOD_b = kb.alloc("MOD", [128, 6, 8], F32)
        stt(MOD_t[:, 0, :], modv_t[:, 8:16, 0], 1.0, V1_t[:, N1W:N1W + 8], ALU.add, ALU.mult, [modv_b, V1_b], [MOD_b])
        cp("dve", MOD_t[:, 1, :], modv_t[:, 0:8, 0], [modv_b], [MOD_b])
        stt(MOD_t[:, 2, :], modv_t[:, 8:16, 1], 1.0, V1_t[:, N1W:N1W + 8], ALU.add, ALU.mult, [modv_b, V1_b], [MOD_b])
        cp("dve", MOD_t[:, 3, :], modv_t[:, 0:8, 1], [modv_b], [MOD_b])
        stt(MOD_t[:, 4, :], modv_t[:, 32:40, 0], 1.0, V1_t[:, N2W:N2W + 8], ALU.add, ALU.mult, [modv_b, V1_b], [MOD_b])
        cp("dve", MOD_t[:, 5, :], modv_t[:, 24:32, 0], [modv_b], [MOD_b])
        ckpt("mod")
        dbg_store("MOD", MOD_t[:, :, :], [128, 6, 8], [MOD_b])
        dbg_store("ga", ga_t[:, :, :], [128, 2, D], [ga_b])
        ckpt("mod2")

        mR0 = kb.mark()
        oT_box = {}

        def alloc_oT():
            oT_box["t"], oT_box["b"] = kb.alloc("oT", [128, 8, NQ], BF16)

        if not S5_ON:
            alloc_oT()
        HSPLIT = 2176
        hT_b = [Buf(f"hT{i}") for i in range(34)]
        hT_tiles = {}

        def alloc_hT():
            hT_tiles["o"], bo_ = kb.alloc("hTo", [128, 8, HSPLIT], BF16)
            hT_tiles["m"] = kb.mark()
            hT_tiles["r"], br_ = kb.alloc("hTr", [128, 8, NALL - HSPLIT], BF16)
            for i_, b_ in enumerate(hT_b):
                b_.alias = list(bo_.alias if i_ < 17 else br_.alias)
            kb.sb_hist[-2] = (kb.sb_hist[-2][0], kb.sb_hist[-2][1], hT_b[0:17])
            kb.sb_hist[-1] = (kb.sb_hist[-1][0], kb.sb_hist[-1][1], hT_b[17:34])

        def hT(kt, c0, n):
            if c0 + n <= HSPLIT:
                return hT_tiles["o"][:, kt, c0:c0 + n]
            assert c0 >= HSPLIT, (c0, n)
            return hT_tiles["r"][:, kt, c0 - HSPLIT:c0 - HSPLIT + n]

        alloc_hT()
        m1 = kb.mark()
        xin_t, xin_b = kb.alloc("xin", [128, 3, D], F32, nsub=3)
        xn_t, xn_b = kb.alloc("xn", [128, 2, D], BF16, nsub=2)
        junk_t, junk_b = kb.alloc("junk", [128, D], BF16)
        st_t, st_b = kb.alloc("st", [128, 4, 4], F32, nsub=4)

        def norm_tile(src_ap, xi, ji, col0, moda, modb, nrows=128):
            ti = col0 // 128
            sb_ = st_b[ji % 4]
            stv = st_t[:, ji % 4, :]
            act(junk_t[0:nrows, :], src_ap, AF.Square, [xin_b[xi]], [junk_b, sb_], accum=stv[0:nrows, 0:1])
            act(stv[0:nrows, 1:2], stv[0:nrows, 0:1], AF.Sqrt, [sb_], [sb_], bias=EPS_t[0:nrows, 0:1], scale=1.0 / D)
            ckpt("n_sqrt")
            recip(stv[0:nrows, 2:3], stv[0:nrows, 1:2], [sb_], [sb_])
            ckpt("n_recip")
            xb = ji % 2
            ts("dve", xn_t[0:nrows, xb, :], src_ap, stv[0:nrows, 2:3], None, ALU.mult, None, [xin_b[xi], sb_], [xn_b[xb]])
            ckpt("n_xn")
            pb = 4 + (ji % 2)
            pv = psbf(pb)
            for kt in range(8):
                tr(pv[:, kt * 128:kt * 128 + nrows], xn_t[0:nrows, xb, kt * 128:(kt + 1) * 128],
                   ident_t[0:nrows, 0:nrows], [xn_b[xb], ident_b], [PB[pb]])
            ckpt("n_tr")
            for kt in range(8):
                eng = "dve"
                ts(eng, hT(kt, col0, nrows), pv[:, kt * 128:kt * 128 + nrows],
                   MOD_t[:, moda, kt:kt + 1], MOD_t[:, modb, kt:kt + 1], ALU.mult, ALU.add,
                   [PB[pb], MOD_b], [hT_b[ti]])

        ji = 0
        for ti in range(34):
            if ti < 16:
                src = I["xo"][ti * 128:(ti + 1) * 128, :]
            elif ti < 32:
                src = I["xt"][(ti - 16) * 128:(ti - 15) * 128, :]
            else:
                src = I["cx"][(ti - 32) * 128:(ti - 31) * 128, :]
            xi = ti % 3
            kb.dma("sp", xin_t[:, xi, :], src, W=[xin_b[xi]])
            if ti < 32:
                norm_tile(xin_t[:, xi, :], xi, ji, ti * 128, 0, 1)
            else:
                norm_tile(xin_t[:, xi, :], xi, ji, ti * 128, 2, 3)
            ji += 1
            ckpt("n_tile1")
        kb.release(m1)
        dbg_store("hT", hT_tiles["o"][:, :, 0:256], [128, 8, 256], hT_b[0:2], BF16)
        dbg_store("hTc", hT_tiles["r"][:, :, 4096 - HSPLIT:4352 - HSPLIT], [128, 8, 256], hT_b[32:34], BF16)
        ckpt("stage1")
        hT_all = list(hT_b)

        def hT_bufs(c0, n):
            return hT_b[c0 // 128:(c0 + n + 127) // 128]

        w_in_v = I["w_in"].rearrange("(kt p) n -> p kt n", p=128)

        def s5_stage():
            mS = kb.mark()
            sinT = [None]

            def sin_turns(out_ap, x_ap, mul, add, R, W, tm):
                (t_ap, f_ap, g_ap, k_ap, tb) = tm
                ts("dve", t_ap, x_ap, mul, add, ALU.mult, ALU.add, R, [tb])
                cp("dve", k_ap, t_ap, [tb], [tb])
                cp("dve", f_ap, k_ap, [tb], [tb])
                tt("dve", f_ap, t_ap, f_ap, ALU.subtract, [tb], [tb])
                ts("dve", g_ap, f_ap, 0.5, None, ALU.is_gt, None, [tb], [tb])
                tt("dve", f_ap, f_ap, g_ap, ALU.subtract, [tb], [tb])
                ts("dve", g_ap, f_ap, -0.5, None, ALU.is_lt, None, [tb], [tb])
                tt("dve", f_ap, f_ap, g_ap, ALU.add, [tb], [tb])
                act(out_ap, f_ap, AF.Sin, [tb], W, scale=6.2831)

            U_t, U_b = kb.alloc("U", [128, 32, 544], BF16, top=True)
            RS_t, RS_b = kb.alloc("RS", [128, 8, 240], BF16, top=True)
            memset("pool", RS_t[:, :, :], 0.0, [RS_b])
            for k_ in range(8):
                asel(RS_t[:, k_, 112:128], RS_t[:, k_, 112:128], [[1, 16]], ALU.not_equal, 1.0, 16 * k_, -1, [RS_b], [RS_b])
            m_u = kb.mark()
            wu_t, wu_b = kb.alloc("wu", [128, 8, 512], BF16)
            wload(wu_t[:, :, :], w_in_v[:, :, 3072:3584], 8, 512, wu_b)
            uT_t, uT_b = kb.alloc("uTb", [128, 2, 4, 512], BF16, nsub=2)
            ei = 0
            for bi_, (c0_, n_) in enumerate(((0, 512), (512, 512), (1024, 512), (1536, 512), (2048, 128), (2176, 512), (2688, 512),
                                             (3200, 512), (3712, 512), (4224, 128))):
                ub = bi_ % 2
                for ct in range(4):
                    pb = ei % 2
                    for kt in range(8):
                        mm(PS[pb][:, 0:n_], wu_t[:, kt, ct * 128:(ct + 1) * 128], hT(kt, c0_, n_), kt == 0, kt == 7,
                           [wu_b] + hT_bufs(c0_, n_), [PB[pb]])
                    cp("act" if ei % 2 == 0 else "dve", uT_t[:, ub, ct, 0:n_], PS[pb][:, 0:n_], [PB[pb]], [uT_b[ub]])
                    ei += 1
                nch = n_ // 8
                ch0 = c0_ // 8
                for g4 in range(8):
                    pb = 2 + (g4 % 2)
                    for gq in range(4):
                        g = g4 * 4 + gq
                        ct, gl = g // 8, g % 8
                        for tau in range(8):
                            mm(PS[pb][:, gq * 64:gq * 64 + nch], RS_t[:, gl, 112 - 16 * tau:240 - 16 * tau],
                               uT_t[:, ub, ct, tau:n_:8], tau == 0, tau == 7, [RS_b, uT_b[ub]], [PB[pb]])
                    cp("act" if g4 % 2 == 0 else "dve", U_t[:, g4 * 4:g4 * 4 + 4, ch0:ch0 + nch],
                       PS[pb][:, 0:256].rearrange("p (q n) -> p q n", q=4)[:, :, 0:nch], [PB[pb]], [U_b])
            dbg_store("s5U", U_t[:, 0:2, :], [128, 2, 544], [U_b], BF16)
            ckpt("s5U")
            kb.release(m_u)
            for kt in range(8):
                kb.dma("sp", hT_spill[:, kt * NALL:kt * NALL + HSPLIT], hT_tiles["o"][:, kt, :], R=hT_b[0:17], W=[hT_spill_b])
                kb.dma("sp", hT_spill[:, kt * NALL + HSPLIT:(kt + 1) * NALL], hT_tiles["r"][:, kt, :], R=hT_b[17:34], W=[hT_spill_b])
            kb.release(mR0)

            P_t, P_b = kb.alloc("Pp", [128, 24, 64], F32)
            CS_t, CS_b = kb.alloc("CS", [128, 2, 64, 64], BF16)
            FX_t, FX_b = kb.alloc("FX", [128, 2, 64], F32)
            BB_t, BB_b = kb.alloc("BB", [128, 2, 64, 16], F32)
            CC_t, CC_b = kb.alloc("CC", [128, 2, 64, 16], BF16)
            TM_t, TM_b = kb.alloc("TM", [128, 4, 64, 8], F32)
            PSW_t, PSW_b = kb.alloc("PSW", [128, 128], BF16)
            MSK_t, MSK_b = kb.alloc("MSK", [128, 2, 128], F32)
            DC_t, DC_b = kb.alloc("DC", [128, 32], F32)
            IDF_t, IDF_b = kb.alloc("IDF", [128, 128], F32)
            PH_t, PH_b = kb.alloc("PH", [128, 8], F32)
            mS1 = kb.mark()

            memset("pool", PSW_t[:, :], 0.0, [PSW_b])
            asel(PSW_t[:, 0:64], PSW_t[:, 0:64], [[-1, 64]], ALU.not_equal, -1.0, -64, 1, [PSW_b], [PSW_b])
            asel(PSW_t[:, 64:128], PSW_t[:, 64:128], [[-1, 64]], ALU.not_equal, 1.0, 0, 1, [PSW_b], [PSW_b])
            memset("pool", MSK_t[:, :, :], 1.0, [MSK_b])
            mv = MSK_t[:, :, :].rearrange("p a (t c) -> p a t c", c=16)
            asel(mv[:, 0], mv[:, 0], [[16, 8], [0, 16]], ALU.is_ge, 0.0, 15, -1, [MSK_b], [MSK_b])
            asel(mv[:, 1], mv[:, 1], [[-16, 8], [0, 16]], ALU.is_ge, 0.0, 0, 1, [MSK_b], [MSK_b])
            cp("dve", IDF_t[:, :], ident_t[:, :], [ident_b], [IDF_b])
            for tau in range(8):
                kb.dma("sp", DC_t[tau * 16:(tau + 1) * 16, :], I["s5_d"].rearrange("(g c) -> c g", c=16), W=[DC_b],
                       allow_slow_non_contiguous=True)

            m_p = kb.mark()
            ARE, AIM, LDT, DT, RE, LR, TH, R1, AR, AI, NR, DEN, KR, KI, T0, T1, T2, T3, PHT, RHO1 = range(20)
            tmi_t, tmi_b = kb.alloc("tmi", [128, 1024], I32)
            tmf_t, tmf_b = kb.alloc("tmf", [128, 3, 1024], F32)
            tm64 = (tmf_t[:, 0, 0:64], tmf_t[:, 1, 0:64], tmf_t[:, 2, 0:64], tmi_t[:, 0:64], tmf_b)
            for half in range(2):
                for q4 in range(4):
                    sl = slice(q4 * 16, (q4 + 1) * 16)
                    kb.dma("sp", P_t[half * 64:(half + 1) * 64, ARE, sl], I["s5_a_re"].rearrange("d g p -> p (d g)")[:, sl], W=[P_b],
                           allow_slow_non_contiguous=True)
                    kb.dma("sp", P_t[half * 64:(half + 1) * 64, AIM, sl], I["s5_a_im"].rearrange("d g p -> p (d g)")[:, sl], W=[P_b],
                           allow_slow_non_contiguous=True)
            kb.dma("sp", P_t[:, LDT, :], I["s5_log_dt"].partition_broadcast(128), W=[P_b])
            act(P_t[:, DT, :], P_t[:, LDT, :], AF.Exp, [P_b], [P_b])
            ts("dve", P_t[:, RE, :], P_t[:, ARE, :], -1e-4, None, ALU.min, None, [P_b], [P_b])
            tt("dve", P_t[:, LR, :], P_t[:, RE, :], P_t[:, DT, :], ALU.mult, [P_b], [P_b])
            tt("dve", P_t[:, TH, :], P_t[:, AIM, :], P_t[:, DT, :], ALU.mult, [P_b], [P_b])
            ts("dve", P_t[:, TH, :], P_t[:, TH, :], 1.0 / TWO_PI, None, ALU.mult, None, [P_b], [P_b])
            act(P_t[:, R1, :], P_t[:, LR, :], AF.Exp, [P_b], [P_b])
            sin_turns(P_t[:, AI, :], P_t[:, TH, :], 1.0, 0.0, [P_b], [P_b], tm64)
            sin_turns(P_t[:, AR, :], P_t[:, TH, :], 1.0, 0.25, [P_b], [P_b], tm64)
            tt("dve", P_t[:, AI, :], P_t[:, AI, :], P_t[:, R1, :], ALU.mult, [P_b], [P_b])
            tt("dve", P_t[:, AR, :], P_t[:, AR, :], P_t[:, R1, :], ALU.mult, [P_b], [P_b])
            ts("dve", P_t[:, NR, :], P_t[:, AR, :], -1.0, None, ALU.add, None, [P_b], [P_b])
            tt("dve", P_t[:, T0, :], P_t[:, RE, :], P_t[:, RE, :], ALU.mult, [P_b], [P_b])
            tt("dve", P_t[:, T1, :], P_t[:, AIM, :], P_t[:, AIM, :], ALU.mult, [P_b], [P_b])
            tt("dve", P_t[:, DEN, :], P_t[:, T0, :], P_t[:, T1, :], ALU.add, [P_b], [P_b])
            recip(P_t[:, DEN, :], P_t[:, DEN, :], [P_b], [P_b])
            tt("dve", P_t[:, T0, :], P_t[:, NR, :], P_t[:, RE, :], ALU.mult, [P_b], [P_b])
            tt("dve", P_t[:, T1, :], P_t[:, AI, :], P_t[:, AIM, :], ALU.mult, [P_b], [P_b])
            tt("dve", P_t[:, T0, :], P_t[:, T0, :], P_t[:, T1, :], ALU.add, [P_b], [P_b])
            tt("dve", P_t[:, KR, :], P_t[:, T0, :], P_t[:, DEN, :], ALU.mult, [P_b], [P_b])
            tt("dve", P_t[:, T0, :], P_t[:, AI, :], P_t[:, RE, :], ALU.mult, [P_b], [P_b])
            tt("dve", P_t[:, T1, :], P_t[:, NR, :], P_t[:, AIM, :], ALU.mult, [P_b], [P_b])
            tt("dve", P_t[:, T0, :], P_t[:, T0, :], P_t[:, T1, :], ALU.subtract, [P_b], [P_b])
            tt("dve", P_t[:, KI, :], P_t[:, T0, :], P_t[:, DEN, :], ALU.mult, [P_b], [P_b])
            Braw_t, Braw_b = kb.alloc("Braw", [128, 2, 64, 16], F32)
            for half in range(2):
                for q4 in range(4):
                    sl = slice(q4 * 16, (q4 + 1) * 16)
                    kb.dma("sp", Braw_t[half * 64:(half + 1) * 64, 0, sl, :], I["s5_b_re"].rearrange("d g p c -> p (d g) c")[:, sl, :], W=[Braw_b])
                    kb.dma("sp", Braw_t[half * 64:(half + 1) * 64, 1, sl, :], I["s5_b_im"].rearrange("d g p c -> p (d g) c")[:, sl, :], W=[Braw_b])
            kr_b = P_t[:, KR, :, None].broadcast_to([128, 64, 16])
            ki_b = P_t[:, KI, :, None].broadcast_to([128, 64, 16])
            bt_t, bt_b = kb.alloc("btmp", [128, 64, 16], F32)
            tt("dve", BB_t[:, 0], Braw_t[:, 0], kr_b, ALU.mult, [Braw_b, P_b], [BB_b])
            tt("dve", bt_t[:, :, :], Braw_t[:, 1], ki_b, ALU.mult, [Braw_b, P_b], [bt_b])
            tt("dve", BB_t[:, 0], BB_t[:, 0], bt_t[:, :, :], ALU.subtract, [BB_b, bt_b], [BB_b])
            tt("dve", BB_t[:, 1], Braw_t[:, 1], kr_b, ALU.mult, [Braw_b, P_b], [BB_b])
            tt("dve", bt_t[:, :, :], Braw_t[:, 0], ki_b, ALU.mult, [Braw_b, P_b], [bt_b])
            tt("dve", BB_t[:, 1], BB_t[:, 1], bt_t[:, :, :], ALU.add, [BB_b, bt_b], [BB_b])
            cst_t, cst_b = kb.alloc("cstg", [128, 128], F32)
            csb_t, csb_b = kb.alloc("cstgb", [128, 128], BF16)
            for ri_, nm in enumerate(("s5_c_re", "s5_c_im")):
                src = I[nm].rearrange("d g c p -> (d g c) p")
                for t8 in range(8):
                    kb.dma("sp", cst_t[:, 0:64], src[t8 * 128:(t8 + 1) * 128, :], W=[cst_b])
                    kb.dma("sp", cst_t[:, 64:128], src[t8 * 128:(t8 + 1) * 128, :], W=[cst_b])
                    cp("dve", csb_t[:, :], cst_t[:, :], [cst_b], [csb_b])
                    pv = psbf(4)
                    tr(pv[:, 0:128], csb_t[:, :], ident_t[:, :], [csb_b, ident_b], [PB[4]])
                    cp("dve", CC_t[:, ri_, t8 * 8:(t8 + 1) * 8, :], pv[:, 0:128].rearrange("p (g c) -> p g c", c=16), [PB[4]], [CC_b])
            exi_t, exi_b = kb.alloc("exi", [128, 64, 16], I32)
            exf_t, exf_b = kb.alloc("exf", [128, 64, 16], F32)
            mg_t, mg_b = kb.alloc("mag", [128, 64, 16], F32)
            an_t, an_b = kb.alloc("angx", [128, 64, 16], F32)
            phases = [(0.25, 0.0), (0.5, 0.25), (0.0, 0.75), (0.25, 0.0), (0.25, 0.5), (0.5, 0.75), (0.5, 0.75), (0.75, 0.0)]
            for i_, (p0, p1) in enumerate(phases):
                memset("pool", PH_t[0:64, i_:i_ + 1], p0, [PH_b])
                memset("pool", PH_t[64:128, i_:i_ + 1], p1, [PH_b])
            lr_b = lambda n_: P_t[:, LR, :, None].broadcast_to([128, 64, n_])
            th_b = lambda n_: P_t[:, TH, :, None].broadcast_to([128, 64, n_])
            tmA = (tmf_t[:, 0, :].rearrange("p (g k) -> p g k", g=64), tmf_t[:, 1, :].rearrange("p (g k) -> p g k", g=64),
                   tmf_t[:, 2, :].rearrange("p (g k) -> p g k", g=64), tmi_t[:, :].rearrange("p (g k) -> p g k", g=64), tmf_b)
            tm8 = tuple(a[:, :, 0:8] for a in tmA[:4]) + (tmf_b,)
            iota(exi_t[:, 0:32, 0:8], [[0, 32], [-1, 8]], 7, 0, [exi_b])
            iota(exi_t[:, 32:64, 0:8], [[0, 32], [1, 8]], 0, 0, [exi_b])
            cp("dve", exf_t[:, :, 0:8], exi_t[:, :, 0:8], [exi_b], [exf_b])
            tt("dve", mg_t[:, :, 0:8], exf_t[:, :, 0:8], lr_b(8), ALU.mult, [exf_b, P_b], [mg_b])
            act(mg_t[:, :, 0:8], mg_t[:, :, 0:8], AF.Exp, [mg_b], [mg_b])
            tt("dve", an_t[:, :, 0:8], exf_t[:, :, 0:8], th_b(8), ALU.mult, [exf_b, P_b], [an_b])
            for i_ in range(4):
                sin_turns(TM_t[:, i_], an_t[:, :, 0:8], 1.0, PH_t[:, i_:i_ + 1], [an_b, PH_b], [TM_b], tm8)
                tt("dve", TM_t[:, i_], TM_t[:, i_], mg_t[:, :, 0:8], ALU.mult, [TM_b, mg_b], [TM_b])
            ts("dve", P_t[:, RHO1, :], P_t[:, LR, :], 8.0, None, ALU.mult, None, [P_b], [P_b])
            act(P_t[:, RHO1, :], P_t[:, RHO1, :], AF.Exp, [P_b], [P_b])
            ts("dve", P_t[:, PHT, :], P_t[:, TH, :], 8.0, None, ALU.mult, None, [P_b], [P_b])
            cp("dve", tmi_t[:, 0:64], P_t[:, PHT, :], [P_b], [tmf_b])
            cp("dve", tmf_t[:, 0, 0:64], tmi_t[:, 0:64], [tmf_b], [tmf_b])
            tt("dve", P_t[:, PHT, :], P_t[:, PHT, :], tmf_t[:, 0, 0:64], ALU.subtract, [P_b, tmf_b], [P_b])
            sin_turns(FX_t[:, 1, :], P_t[:, PHT, :], 64.0, 0.0, [P_b], [FX_b], tm64)
            sin_turns(FX_t[:, 0, :], P_t[:, PHT, :], 64.0, 0.25, [P_b], [FX_b], tm64)
            tt("dve", FX_t[:, 0, :], FX_t[:, 0, :], P_t[:, RHO1, :], ALU.mult, [FX_b, P_b], [FX_b])
            tt("dve", FX_t[:, 1, :], FX_t[:, 1, :], P_t[:, RHO1, :], ALU.mult, [FX_b, P_b], [FX_b])
            jf_t, jf_b = kb.alloc("jf", [128, 16, 64], F32)
            iota(tmi_t[:, :].rearrange("p (g j) -> p g j", g=16), [[0, 16], [1, 64]], 0, 0, [tmf_b])
            cp("dve", jf_t[:, :, :], tmi_t[:, :].rearrange("p (g j) -> p g j", g=16), [tmf_b], [jf_b])
            ja_t, ja_b = kb.alloc("ja", [128, 16, 64], F32)
            tmB = tuple(a.rearrange("p g k -> p (g k)").rearrange("p (g j) -> p g j", g=16) for a in tmA[:4]) + (tmf_b,)
            for q4 in range(4):
                sl = slice(q4 * 16, (q4 + 1) * 16)
                tt("dve", ja_t[:, :, :], jf_t[:, :, :], P_t[:, PHT, sl, None].broadcast_to([128, 16, 64]), ALU.mult, [jf_b, P_b], [ja_b])
                sin_turns(CS_t[:, 1, sl, :], ja_t[:, :, :], 1.0, 0.0, [ja_b], [CS_b], tmB)
                sin_turns(CS_t[:, 0, sl, :], ja_t[:, :, :], 1.0, 0.25, [ja_b], [CS_b], tmB)
            dbg_store("s5P", P_t[:, :, :], [128, 24, 64], [P_b])
            dbg_store("s5TM", TM_t[:, :, :, :], [128, 4, 64, 8], [TM_b])
            dbg_store("s5BB", BB_t[:, :, :, :], [128, 2, 64, 16], [BB_b])
            dbg_store("s5CC", CC_t[:, :, :, :], [128, 2, 64, 16], [CC_b], BF16)
            dbg_store("s5CS", CS_t[:, :, :, :], [128, 2, 64, 64], [CS_b], BF16)
            ckpt("s5prep")
            kb.release(m_p)

            WA_t, WA_b = kb.alloc("WA", [128, 5, 32, 64], BF16)
            WB_t, WB_b = kb.alloc("WB", [128, 9, 32, 64], BF16)
            memset("pool", WA_t[:, :, :, :], 0.0, [WA_b])
            memset("pool", WB_t[:, :, :, :], 0.0, [WB_b])
            m_e = kb.mark()
            mt_t, mt_b = kb.alloc("mtf", [128, 2, 3, 128], F32, nsub=2)
            mtb_t, mtb_b = kb.alloc("mtb", [128, 2, 2, 128], BF16, nsub=2)
            mx_t, mx_b = kb.alloc("mx", [128, 2, 2, 128], BF16, nsub=2)
            dm_t, dm_b = kb.alloc("dm", [128, 2, 2, 512], F32, nsub=2)

            def build_MT(gd, i2, which):
                ta = TM_t[:, 2 * which, gd, :, None].broadcast_to([128, 8, 16])
                tb_ = TM_t[:, 2 * which + 1, gd, :, None].broadcast_to([128, 8, 16])
                bre = BB_t[:, 0, gd, None, :].broadcast_to([128, 8, 16])
                bim = BB_t[:, 1, gd, None, :].broadcast_to([128, 8, 16])
                v = lambda k_: mt_t[:, i2, k_, :].rearrange("p (t c) -> p t c", c=16)
                tt("dve", v(0), ta, bre, ALU.mult, [TM_b, BB_b], [mt_b[i2]])
                tt("dve", v(1), tb_, bim, ALU.mult, [TM_b, BB_b], [mt_b[i2]])
                tt("dve", mtb_t[:, i2, which, :].rearrange("p (t c) -> p t c", c=16), v(0), v(1), ALU.add, [mt_b[i2]], [mtb_b[i2]])

            def e_stageA(it):
                g, dd = it // 2, it % 2
                gd = dd * 32 + g
                i2 = it % 2
                build_MT(gd, i2, 0)
                build_MT(gd, i2, 1)

            def e_stageA2(it):
                i2 = it % 2
                pv = psbf(4 + i2)
                tr(pv[:, 0:128], mtb_t[:, i2, 0, :], ident_t[:, :], [mtb_b[i2], ident_b], [PB[4 + i2]])
                tr(pv[:, 128:256], mtb_t[:, i2, 1, :], ident_t[:, :], [mtb_b[i2], ident_b], [PB[4 + i2]])
                cp("act", mx_t[:, i2, :, :], pv[:, 0:256].rearrange("p (w m) -> p w m", w=2), [PB[4 + i2]], [mx_b[i2]])

            def e_stageB(it):
                g, dd = it // 2, it % 2
                gd = dd * 32 + g
                i2 = it % 2
                cosv, sinv = CS_t[:, 0, gd, :], CS_t[:, 1, gd, :]
                if dd == 0:
                    pieces = [(512, 32, 0), (0, 256, 32)]
                else:
                    pieces = [(32, 512, 0), (0, 32, 0)]
                for pi2, (uc0, n_, pc0) in enumerate(pieces):
                    if dd == 0:
                        pe1, pe2 = 6, 7
                    else:
                        pe1, pe2 = (0, 1) if pi2 == 0 else (2, 3)
                    mm(PS[pe1][:, pc0:pc0 + n_], mx_t[:, i2, 0, :], U_t[:, g, uc0:uc0 + n_], True, True, [mx_b[i2], U_b], [PB[pe1]])
                    mm(PS[pe2][:, pc0:pc0 + n_], mx_t[:, i2, 1, :], U_t[:, g, uc0:uc0 + n_], True, True, [mx_b[i2], U_b], [PB[pe2]])
                    e1 = PS[pe1][:, pc0:pc0 + n_]
                    e2 = PS[pe2][:, pc0:pc0 + n_]
                    di = dd
                    d1 = dm_t[:, di, 0, 0:n_]
                    d2 = dm_t[:, di, 1, 0:n_]
                    if dd == 0 and pi2 == 0:
                        c_ap, s_ap = cosv[:, 32:64], sinv[:, 32:64]
                        o_ap = WA_t[:, 0, g, 32:64]
                        v3 = lambda a: a
                    elif dd == 0:
                        c_ap = CS_t[:, 0, gd, None, :].broadcast_to([128, 4, 64])
                        s_ap = CS_t[:, 1, gd, None, :].broadcast_to([128, 4, 64])
                        o_ap = WA_t[:, 1:5, g, :]
                        v3 = lambda a: a.rearrange("p (s j) -> p s j", j=64)
                    elif pi2 == 0:
                        c_ap = CS_t[:, 0, gd, None, ::-1].broadcast_to([128, 8, 64])
                        s_ap = CS_t[:, 1, gd, None, ::-1].broadcast_to([128, 8, 64])
                        o_ap = WB_t[:, 7::-1, g, ::-1]
                        v3 = lambda a: a.rearrange("p (s j) -> p s j", j=64)
                    else:
                        c_ap, s_ap = cosv[:, 31::-1], sinv[:, 31::-1]
                        o_ap = WB_t[:, 8, g, 31::-1]
                        v3 = lambda a: a
                    wb_ = WA_b if dd == 0 else WB_b
                    tt("dve", v3(d1), v3(e1), c_ap, ALU.mult, [PB[pe1], CS_b], [dm_b[di]])
                    tt("dve", v3(d2), v3(e2), s_ap, ALU.mult, [PB[pe2], CS_b], [dm_b[di]])
                    tt("pool", o_ap, v3(d1), v3(d2), ALU.add, [dm_b[di]], [wb_])

            e_stageA(0)
            e_stageA2(0)
            for it in range(64):
                if it + 1 < 64:
                    e_stageA(it + 1)
                e_stageB(it)
                if it + 1 < 64:
                    e_stageA2(it + 1)
            kb.release(m_e)
            dbg_store("s5Wpre", WB_t[:, :, 0:2, :], [128, 9, 2, 64], [WB_b], BF16)
            ckpt("s5E")

            m_s = kb.mark()
            RHO_t, RHO_b = kb.alloc("RHO", [128, 64, 64], F32)
            cp("dve", RHO_t[:, :, :], P_t[:, RHO1, :, None].broadcast_to([128, 64, 64]), [P_b], [RHO_b])
            memset("pool", RHO_t[:, :, 0:1], 0.0, [RHO_b])
            fx_t, fx_b = kb.alloc("fxt", [128, 2, 32], F32)
            for dd, (W_t, W_b, nseg) in enumerate(((WA_t, WA_b, 5), (WB_t, WB_b, 9))):
                for sg in range(nseg):
                    if sg > 0:
                        wend = W_t[:, sg - 1, :, 63]
                        mm(PS[6][:, 0:32], PSW_t[:, :], wend, True, True, [PSW_b, W_b], [PB[6]])
                        tt("dve", fx_t[:, 0, :], wend, FX_t[:, 0, dd * 32:(dd + 1) * 32], ALU.mult, [W_b, FX_b], [fx_b])
                        tt("dve", fx_t[:, 1, :], PS[6][:, 0:32], FX_t[:, 1, dd * 32:(dd + 1) * 32], ALU.mult, [PB[6], FX_b], [fx_b])
                        tt("dve", fx_t[:, 0, :], fx_t[:, 0, :], fx_t[:, 1, :], ALU.add, [fx_b], [fx_b])
                        tt("dve", W_t[:, sg, :, 0], W_t[:, sg, :, 0], fx_t[:, 0, :], ALU.add, [W_b, fx_b], [W_b])
                    wv = W_t[:, sg, :, :].rearrange("p g j -> p (g j)")
                    rv = RHO_t[:, dd * 32:(dd + 1) * 32, :].rearrange("p g j -> p (g j)")
                    kb.op("dve", (lambda wv_, rv_: (lambda e: e.tensor_tensor_scan(wv_, rv_, wv_, 0.0, ALU.mult, ALU.add)))(wv, rv),
                          [W_b, RHO_b], [W_b])
            kb.release(m_s)
            dbg_store("s5W", WB_t[:, :, 0:2, :], [128, 9, 2, 64], [WB_b], BF16)
            dbg_store("s5WA", WA_t[:, :, 0:2, :], [128, 5, 2, 64], [WA_b], BF16)
            ckpt("s5scan")

            m_y = kb.mark()
            TC_t, TC_b = kb.alloc("TC", [128, 3, 64, 16], F32)
            m_tc = kb.mark()
            tmi_t, tmi_b = kb.alloc("tmi2", [128, 32, 16], I32)
            tmf_t, tmf_b = kb.alloc("tmf2", [128, 3, 32, 16], F32)
            exi_t, exi_b = kb.alloc("exi2", [128, 32, 16], I32)
            exf_t, exf_b = kb.alloc("exf2", [128, 32, 16], F32)
            mg_t, mg_b = kb.alloc("mag2", [128, 32, 16], F32)
            an_t, an_b = kb.alloc("angx2", [128, 32, 16], F32)
            tmH = (tmf_t[:, 0], tmf_t[:, 1], tmf_t[:, 2], tmi_t[:, :, :], tmf_b)
            for dd in range(2):
                gs = slice(dd * 32, (dd + 1) * 32)
                if dd == 0:
                    iota(exi_t[:, :, :], [[0, 32], [1, 16]], -7, 0, [exi_b])
                else:
                    iota(exi_t[:, :, 0:8], [[0, 32], [-1, 8]], 0, 0, [exi_b])
                    iota(exi_t[:, :, 8:16], [[0, 32], [-1, 8]], 8, 0, [exi_b])
                cp("dve", exf_t[:, :, :], exi_t[:, :, :], [exi_b], [exf_b])
                tt("dve", mg_t[:, :, :], exf_t[:, :, :], P_t[:, LR, gs, None].broadcast_to([128, 32, 16]), ALU.mult, [exf_b, P_b], [mg_b])
                act(mg_t[:, :, :], mg_t[:, :, :], AF.Exp, [mg_b], [mg_b])
                tt("dve", an_t[:, :, :], exf_t[:, :, :], P_t[:, TH, gs, None].broadcast_to([128, 32, 16]), ALU.mult, [exf_b, P_b], [an_b])
                for i_, phi_ in enumerate((4, 5, 7)):
                    sin_turns(TC_t[:, i_, gs], an_t[:, :, :], 1.0, PH_t[:, phi_:phi_ + 1], [an_b, PH_b], [TC_b], tmH)
                    tt("dve", TC_t[:, i_, gs], TC_t[:, i_, gs], mg_t[:, :, :], ALU.mult, [TC_b, mg_b], [TC_b])
            dbg_store("s5TC", TC_t[:, :, :, :], [128, 3, 64, 16], [TC_b])
            kb.release(m_tc)
            Y_t, Y_b = kb.alloc("Ysb", [128, 32, 257], BF16, top=True)
            cm_t, cm_b = kb.alloc("cmf", [128, 2, 128], F32)
            cpw_t, cpw_b = kb.alloc("cpw", [128, 2, 6, 128], BF16, nsub=2)
            tp_t, tp_b = kb.alloc("toep", [128, 2, 128], BF16, nsub=2)
            tf_t, tf_b = kb.alloc("toepf", [128, 2, 128], F32)
            rm_t, rm_b = kb.alloc("rm", [128, 2, 4, 257], BF16, nsub=2)
            mt1_t, mt1_b = kb.alloc("mt1k", [128, 2, 2, 128], BF16, nsub=2)
            mtf2_t, mtf2_b = kb.alloc("mtf2", [128, 2, 128], F32)

            def build_C(gd, dst_ap, ta_i, k0, i2):
                ta = TC_t[:, ta_i, gd, k0:k0 + 8, None].broadcast_to([128, 8, 16])
                tb_ = TC_t[:, ta_i + 1, gd, k0:k0 + 8, None].broadcast_to([128, 8, 16])
                cre = CC_t[:, 0, gd, None, :].broadcast_to([128, 8, 16])
                cim = CC_t[:, 1, gd, None, :].broadcast_to([128, 8, 16])
                v = lambda k_: cm_t[:, k_, :].rearrange("p (t c) -> p t c", c=16)
                tt("dve", v(0), ta, cre, ALU.mult, [TC_b, CC_b], [cm_b])
                tt("dve", v(1), tb_, cim, ALU.mult, [TC_b, CC_b], [cm_b])
                tt("pool", dst_ap.rearrange("p (t c) -> p t c", c=16), v(0), v(1), ALU.add, [cm_b], [cpw_b[i2]])

            def y_stageA(g):
                i2 = g % 2
                for dd in range(2):
                    gd = dd * 32 + g
                    build_C(gd, cpw_t[:, i2, 3 * dd + 0, :], 0, 8, i2)
                    build_C(gd, cpw_t[:, i2, 3 * dd + 1, :], 1, 8, i2)
                    build_C(gd, cpw_t[:, i2, 3 * dd + 2, :], 0, 0, i2)
                    ta = TM_t[:, 0, gd, :, None].broadcast_to([128, 8, 16])
                    tb_ = TM_t[:, 1, gd, :, None].broadcast_to([128, 8, 16])
                    bre = BB_t[:, 0, gd, None, :].broadcast_to([128, 8, 16])
                    bim = BB_t[:, 1, gd, None, :].broadcast_to([128, 8, 16])
                    v = lambda k_: mtf2_t[:, k_, :].rearrange("p (t c) -> p t c", c=16)
                    tt("dve", v(0), ta, bre, ALU.mult, [TM_b, BB_b], [mtf2_b])
                    tt("dve", v(1), tb_, bim, ALU.mult, [TM_b, BB_b], [mtf2_b])
                    tt("pool", mt1_t[:, i2, dd, :].rearrange("p (t c) -> p t c", c=16), v(0), v(1), ALU.add, [mtf2_b], [mt1_b[i2]])

            def y_stageA2(g):
                i2 = g % 2
                for dd in range(2):
                    mm(PS[4 + dd][:, 0:128], mt1_t[:, i2, dd, :], cpw_t[:, i2, 3 * dd + 2, :], True, True, [mt1_b[i2], cpw_b[i2]], [PB[4 + dd]])
                tt("dve", tf_t[:, 0, :], PS[4][:, 0:128], MSK_t[:, 0, :], ALU.mult, [PB[4], MSK_b], [tf_b])
                tt("dve", tf_t[:, 1, :], PS[5][:, 0:128], MSK_t[:, 1, :], ALU.mult, [PB[5], MSK_b], [tf_b])
                tt("pool", tf_t[:, 0, :], tf_t[:, 0, :], tf_t[:, 1, :], ALU.add, [tf_b], [tf_b])
                stt(tp_t[:, i2, :], IDF_t[:, :], DC_t[:, g:g + 1], tf_t[:, 0, :], ALU.mult, ALU.add, [IDF_b, DC_b, tf_b], [tp_b[i2]])

            def y_stageB(g):
                i2 = g % 2
                R_ = [WA_b, WB_b, CS_b]
                for ci_ in range(2):
                    ca, cb = CS_t[:, ci_, g, :], CS_t[:, ci_, 32 + g, :]
                    eng = "dve" if ci_ == 0 else "pool"
                    tt(eng, rm_t[:, i2, ci_, 0:1], WA_t[:, 0, g, 63:64], ca[:, 63:64], ALU.mult, R_, [rm_b[i2]])
                    tt(eng, rm_t[:, i2, ci_, 1:257].rearrange("p (s j) -> p s j", j=64), WA_t[:, 1:5, g, :],
                       CS_t[:, ci_, g, None, :].broadcast_to([128, 4, 64]), ALU.mult, R_, [rm_b[i2]])
                    tt(eng, rm_t[:, i2, 2 + ci_, 0:31], WB_t[:, 8, g, 30::-1], cb[:, 30::-1], ALU.mult, R_, [rm_b[i2]])
                    tt(eng, rm_t[:, i2, 2 + ci_, 31:223].rearrange("p (s j) -> p s j", j=64), WB_t[:, 7:4:-1, g, ::-1],
                       CS_t[:, ci_, 32 + g, None, ::-1].broadcast_to([128, 3, 64]), ALU.mult, R_, [rm_b[i2]])
                    tt(eng, rm_t[:, i2, 2 + ci_, 223:257], WB_t[:, 4, g, 63:29:-1], cb[:, 63:29:-1], ALU.mult, R_, [rm_b[i2]])
                pb = 6 + i2
                mm(PS[pb][:, 0:257], tp_t[:, i2, :], U_t[:, g, 0:257], True, False, [tp_b[i2], U_b], [PB[pb]])
                mm(PS[pb][:, 0:257], cpw_t[:, i2, 0, :], rm_t[:, i2, 0, :], False, False, [cpw_b[i2], rm_b[i2]], [PB[pb]])
                mm(PS[pb][:, 0:257], cpw_t[:, i2, 1, :], rm_t[:, i2, 1, :], False, False, [cpw_b[i2], rm_b[i2]], [PB[pb]])
                mm(PS[pb][:, 0:257], cpw_t[:, i2, 3, :], rm_t[:, i2, 2, :], False, False, [cpw_b[i2], rm_b[i2]], [PB[pb]])
                mm(PS[pb][:, 0:257], cpw_t[:, i2, 4, :], rm_t[:, i2, 3, :], False, True, [cpw_b[i2], rm_b[i2]], [PB[pb]])
                cp("act", Y_t[:, g, :], PS[pb][:, 0:257], [PB[pb]], [Y_b])

            y_stageA(0)
            y_stageA2(0)
            for g in range(32):
                if g + 1 < 32:
                    y_stageA(g + 1)
                y_stageB(g)
                if g + 1 < 32:
                    y_stageA2(g + 1)
            dbg_store("s5Y", Y_t[:, 0:2, :], [128, 2, 257], [Y_b], BF16)
            ckpt("s5Y")

            kb.release(mR0)
            hp_t, hp_b = kb.alloc("hpre", [128, 4, 2056], BF16)
            gw_t, gw_b = kb.alloc("gluw", [128, 4, 512], BF16)
            wload(gw_t[:, :, :], I["glu_w"].rearrange("(kt p) n -> p kt n", p=128), 4, 512, gw_b)
            gx_t, gx_b = kb.alloc("gx", [128, 2, 4, 260], F32, nsub=2)
            gi = 0
            for ct in range(4):
                for t_ in range(8):
                    i2 = gi % 2
                    gi += 1
                    pb = 4 + i2
                    for gl in range(8):
                        mm(PS[pb][:, 0:257], RS_t[:, t_, 112 - 16 * gl:240 - 16 * gl], Y_t[:, ct * 8 + gl, :], gl == 0, gl == 7,
                           [RS_b, Y_b], [PB[pb]])
                    x_ = gx_t[:, i2, 0, 0:257]
                    cp("act", x_, PS[pb][:, 0:257], [PB[pb]], [gx_b[i2]])
                    tt("pool", gx_t[:, i2, 1, 0:257], x_, x_, ALU.mult, [gx_b[i2]], [gx_b[i2]])
                    ts("dve", gx_t[:, i2, 1, 0:257], gx_t[:, i2, 1, 0:257], 0.044715, 1.0, ALU.mult, ALU.add, [gx_b[i2]], [gx_b[i2]])
                    tt("pool", gx_t[:, i2, 1, 0:257], gx_t[:, i2, 1, 0:257], x_, ALU.mult, [gx_b[i2]], [gx_b[i2]])
                    act(gx_t[:, i2, 2, 0:257], gx_t[:, i2, 1, 0:257], AF.Sigmoid, [gx_b[i2]], [gx_b[i2]], scale=2.0 * math.sqrt(2.0 / math.pi))
                    tt("dve", hp_t[:, ct, t_:2056:8], x_, gx_t[:, i2, 2, 0:257], ALU.mult, [gx_b[i2]], [hp_b])
            hsT_t, hsT_b = kb.alloc("hsTs", [128, 4, NQ + 7], BF16)
            sgl_t, sgl_b = kb.alloc("sgl", [128, 2, 512], F32, nsub=2)
            gi = 0
            for (q0, nq) in ((0, 512), (512, 512), (1024, 512), (1536, 512), (2048, 8)):
                for c2 in range(4):
                    i2 = gi % 2
                    gi += 1
                    pb = 6 + i2
                    for ct in range(4):
                        mm(PS[pb][:, 0:nq], gw_t[:, ct, c2 * 128:(c2 + 1) * 128], hp_t[:, ct, q0:q0 + nq], ct == 0, ct == 3, [gw_b, hp_b], [PB[pb]])
                    act(sgl_t[:, i2, 0:nq], PS[pb][:, 0:nq], AF.Sigmoid, [PB[pb], V1_b], [sgl_b[i2]], bias=V1_t[:, GLUB + c2:GLUB + c2 + 1])
                    tt("dve", hsT_t[:, c2, q0:q0 + nq], hp_t[:, c2, q0:q0 + nq], sgl_t[:, i2, 0:nq], ALU.mult, [hp_b, sgl_b[i2]], [hsT_b])
            dbg_store("s5hs", hsT_t[:, :, 0:NQ], [128, 4, NQ], [hsT_b], BF16)
            kb.dma("sp", hsT_spill.rearrange("p (k n) -> p k n", k=4), hsT_t[:, :, :], R=[hsT_b], W=[hsT_spill_b])
            ckpt("s5end")
            kb.release(mR0)
            kb.release_top(SB_LIMIT)
            alloc_oT()
            alloc_hT()
            for kt in range(8):
                kb.dma("sp", hT_tiles["o"][:, kt, :], hT_spill[:, kt * NALL:kt * NALL + HSPLIT], R=[hT_spill_b], W=hT_b[0:17])
                kb.dma("sp", hT_tiles["r"][:, kt, :], hT_spill[:, kt * NALL + HSPLIT:(kt + 1) * NALL], R=[hT_spill_b], W=hT_b[17:34])

        if S5_ON:
            s5_stage()
        oT_t, oT_b = oT_box["t"], oT_box["b"]


        mA = kb.mark()
        lv_t, lv_b = kb.alloc("lv", [128, 4, 64], F32)
        kb.dma("sp", lv_t[:, :, :], I["lamv"].rearrange("a b -> (a b)").partition_broadcast(128).rearrange("p (a b) -> p a b", a=4), W=[lv_b])
        sm_t, sm_b = kb.alloc("sm", [128, 16], F32)
        tt("dve", lv_t[:, 0, :], lv_t[:, 0, :], lv_t[:, 1, :], ALU.mult, [lv_b], [lv_b])
        tt("dve", lv_t[:, 2, :], lv_t[:, 2, :], lv_t[:, 3, :], ALU.mult, [lv_b], [lv_b])
        kb.op("dve", lambda e: e.reduce_sum(sm_t[:, 0:1], lv_t[:, 0, :], AX.X), [lv_b], [sm_b])
        kb.op("dve", lambda e: e.reduce_sum(sm_t[:, 1:2], lv_t[:, 2, :], AX.X), [lv_b], [sm_b])
        act(sm_t[:, 2:4], sm_t[:, 0:2], AF.Exp, [sm_b], [sm_b])
        tt("dve", sm_t[:, 4:5], sm_t[:, 3:4], sm_t[:, 2:3], ALU.subtract, [sm_b], [sm_b])
        ts("dve", sm_t[:, 5:6], sm_t[:, 4:5], -0.2, None, ALU.add, None, [sm_b], [sm_b])
        NEGLAM = sm_t[:, 5:6]
        qk_t, qk_b = kb.alloc("qkw", [128, 2, 64], F32)
        kb.dma("sp", qk_t[:, 0, :], I["q_norm_w"].partition_broadcast(128), W=[qk_b])
        kb.dma("sp", qk_t[:, 1, :], I["k_norm_w"].partition_broadcast(128), W=[qk_b])
        kb.op("dve", lambda e: e.reduce_max(sm_t[:, 6:7], qk_t[:, 0, :], AX.X, apply_absolute_value=True), [qk_b], [sm_b])
        kb.op("dve", lambda e: e.reduce_max(sm_t[:, 7:8], qk_t[:, 1, :], AX.X, apply_absolute_value=True), [qk_b], [sm_b])
        tt("dve", sm_t[:, 8:9], sm_t[:, 6:7], sm_t[:, 7:8], ALU.mult, [sm_b], [sm_b])
        ts("dve", sm_t[:, 9:10], sm_t[:, 8:9], -8.0, None, ALU.mult, None, [sm_b], [sm_b])
        NBIAS = sm_t[:, 9:10]
        memset("pool", sm_t[:, 10:11], -0.5, [sm_b])
        MHALF = sm_t[:, 10:11]
        wcol_t, wcol_b = kb.alloc("wcol", [128, 2], F32)
        for half in range(2):
            kb.dma("sp", wcol_t[half * 64:(half + 1) * 64, 0:1], I["q_norm_w"].rearrange("(a b) -> a b", b=1), W=[wcol_b])
            kb.dma("sp", wcol_t[half * 64:(half + 1) * 64, 1:2], I["k_norm_w"].rearrange("(a b) -> a b", b=1), W=[wcol_b])
        sw_t, sw_b = kb.alloc("swbc", [128, 128], F32)
        kb.dma("sp", sw_t[:, :], I["subln_w"].partition_broadcast(128), W=[sw_b])
        ts("dve", sw_t[:, :], sw_t[:, :], 0.8, None, ALU.mult, None, [sw_b], [sw_b])
        bones_t, bones_b = kb.alloc("bones", [128, 128], BF16)
        memset("pool", bones_t[:, :], 0.0, [bones_b])
        memset("pool", bones_t[0:64, 0:64], 1.0, [bones_b])
        memset("pool", bones_t[64:128, 64:128], 1.0, [bones_b])
        prot_t, prot_b = kb.alloc("prot", [128, 128], BF16)
        memset("pool", prot_t[:, :], 0.0, [prot_b])
        prot_v = prot_t[:, :].rearrange("p (b h i) -> p b h i", b=4, h=2)
        asel(prot_v[:, :, 0, :], prot_v[:, :, 0, :], [[-32, 4], [-1, 16]], ALU.not_equal, -1.0, -16, 1, [prot_b], [prot_b])
        asel(prot_v[:, :, 1, :], prot_v[:, :, 1, :], [[-32, 4], [-1, 16]], ALU.not_equal, 1.0, 0, 1, [prot_b], [prot_b])

        def sin_turns(out_ap, x_ap, mul, add, n, R, W, tmps):
            (t_ap, f_ap, g_ap, k_ap, tb) = tmps
            ts("dve", t_ap, x_ap, mul, add, ALU.mult, ALU.add, R, [tb])
            cp("dve", k_ap, t_ap, [tb], [tb])
            cp("dve", f_ap, k_ap, [tb], [tb])
            tt("dve", f_ap, t_ap, f_ap, ALU.subtract, [tb], [tb])
            ts("dve", g_ap, f_ap, 0.5, None, ALU.is_gt, None, [tb], [tb])
            tt("dve", f_ap, f_ap, g_ap, ALU.subtract, [tb], [tb])
            ts("dve", g_ap, f_ap, -0.5, None, ALU.is_lt, None, [tb], [tb])
            tt("dve", f_ap, f_ap, g_ap, ALU.add, [tb], [tb])
            act(out_ap, f_ap, AF.Sin, [tb], W, scale=6.2831)

        cos_t, cos_b = kb.alloc("ropecos", [128, NLAT], BF16)
        sin_t, sin_b = kb.alloc("ropesin", [128, NLAT], BF16)
        mR = kb.mark()
        pi_t, pi_b = kb.alloc("posinfo", [128, 2], F32)
        kb.dma("sp", pi_t[:, :], I["posinfo"].partition_broadcast(128), W=[pi_b])
        pidx_t, pidx_b = kb.alloc("pidx", [128, 4], I32)
        pf_t, pf_b = kb.alloc("pf", [128, 12], F32)
        iota(pidx_t[:, 0:1], [[0, 1]], 0, 1, [pidx_b])
        cp("dve", pf_t[:, 5:6], pidx_t[:, 0:1], [pidx_b], [pf_b])
        ts("dve", pf_t[:, 6:7], pf_t[:, 5:6], 64.0, None, ALU.is_ge, None, [pf_b], [pf_b])
        stt(pf_t[:, 7:8], pf_t[:, 6:7], -64.0, pf_t[:, 5:6], ALU.mult, ALU.add, [pf_b], [pf_b])
        ts("dve", pf_t[:, 1:2], pf_t[:, 7:8], 32.0, None, ALU.is_ge, None, [pf_b], [pf_b])
        stt(pf_t[:, 8:9], pf_t[:, 1:2], -32.0, pf_t[:, 7:8], ALU.mult, ALU.add, [pf_b], [pf_b])
        ts("dve", pf_t[:, 9:10], pf_t[:, 8:9], 16.0, None, ALU.is_ge, None, [pf_b], [pf_b])
        stt(pf_t[:, 0:1], pf_t[:, 9:10], -16.0, pf_t[:, 8:9], ALU.mult, ALU.add, [pf_b], [pf_b])
        act(pf_t[:, 2:3], pf_t[:, 0:1], AF.Exp, [pf_b], [pf_b], scale=-math.log(10000.0) / 16.0)
        tt("dve", pf_t[:, 4:5], pf_t[:, 2:3], pf_t[:, 1:2], ALU.mult, [pf_b], [pf_b])
        tt("dve", pf_t[:, 3:4], pf_t[:, 2:3], pf_t[:, 4:5], ALU.subtract, [pf_b], [pf_b])
        ts("dve", pf_t[:, 3:5], pf_t[:, 3:5], 1.0 / TWO_PI, None, ALU.mult, None, [pf_b], [pf_b])
        RC = 1024
        ri_t, ri_b = kb.alloc("ri", [128, RC], I32)
        rf_t, rf_b = kb.alloc("rf", [128, RC], F32)
        cf_t, cf_b = kb.alloc("cf", [128, RC], F32)
        ang_t, ang_b = kb.alloc("ang", [128, RC], F32)
        tA_t, tA_b = kb.alloc("tA", [128, RC], F32)
        tB_t, _ = kb.alloc("tB", [128, RC], F32)
        tC_t, _ = kb.alloc("tC", [128, RC], F32)
        for rc in range(NLAT // RC):
            iota(ri_t[:, :].rearrange("p (a b) -> p a b", a=RC // 64), [[1, RC // 64], [0, 64]], rc * (RC // 64), 0, [ri_b])
            cp("dve", rf_t[:, :], ri_t[:, :], [ri_b], [rf_b])
            iota(ri_t[:, :].rearrange("p (a b) -> p a b", a=RC // 64), [[0, RC // 64], [1, 64]], 0, 0, [ri_b])
            cp("dve", cf_t[:, :], ri_t[:, :], [ri_b], [cf_b])
            ts("dve", rf_t[:, :], rf_t[:, :], pi_t[:, 0:1], pi_t[:, 1:2], ALU.mult, ALU.add, [rf_b, pi_b], [rf_b])
            ts("dve", cf_t[:, :], cf_t[:, :], pi_t[:, 0:1], pi_t[:, 1:2], ALU.mult, ALU.add, [cf_b, pi_b], [cf_b])
            ts("dve", ang_t[:, :], rf_t[:, :], pf_t[:, 3:4], None, ALU.mult, None, [rf_b, pf_b], [ang_b])
            stt(ang_t[:, :], cf_t[:, :], pf_t[:, 4:5], ang_t[:, :], ALU.mult, ALU.add, [cf_b, pf_b, ang_b], [ang_b])
            tmps = (tA_t[:, :], tB_t[:, :], tC_t[:, :], ri_t[:, :], tA_b)
            sin_turns(sin_t[:, rc * RC:(rc + 1) * RC], ang_t[:, :], 1.0, 0.0, RC, [ang_b], [sin_b, ri_b], tmps)
            sin_turns(cos_t[:, rc * RC:(rc + 1) * RC], ang_t[:, :], 1.0, 0.25, RC, [ang_b], [cos_b, ri_b], tmps)
        kb.release(mR)
        dbg_store("rope", cos_t[:, 0:256], [128, 256], [cos_b], BF16)
        dbg_store("ropes", sin_t[:, 0:256], [128, 256], [sin_b], BF16)
        ckpt("rope")

        wqkv_t, wqkv_b = kb.alloc("wqkv", [128, 2, 8, 384], BF16, nsub=2)
        KT_t, KT_b = kb.alloc("KT", [128, NALL], BF16)
        QT_t, QT_b = kb.alloc("QT", [128, NQ + 7], BF16)
        V_t, V_b = kb.alloc("Vaug", [128, 34, 129], BF16)
        memset("pool", V_t[:, :, :], 1.0, [V_b])
        sq_t, sq_b = kb.alloc("sq", [128, 2, 512], BF16, nsub=2)
        kw_t, kw_b = kb.alloc("kw", [128, 2, 512], BF16, nsub=2)
        u_t, u_b = kb.alloc("uu", [128, 2, 512], F32, nsub=2)
        t1_t, t1_b = kb.alloc("t1", [128, 2, 512], F32, nsub=2)
        t2_t, t2_b = kb.alloc("t2", [128, 2, 512], F32, nsub=2)
        PT_t, PT_b = kb.alloc("PT", [128, 4, 512], BF16, nsub=4)
        o_t, o_b = kb.alloc("oacc", [128, 2, 128], F32, nsub=2)
        on_t, on_b = kb.alloc("onrm", [128, 2, 128], BF16, nsub=2)
        fs_t, fs_b = kb.alloc("fstat", [128, 2, 8], F32, nsub=2)
        ojunk_t, ojunk_b = kb.alloc("ojunk", [128, 128], F32)
        blk_i = [0]

        def load_head_w(hh):
            bi = hh % 2
            for part in range(3):
                c0 = part * 1024 + hh * 128
                wload(wqkv_t[:, bi, :, part * 128:(part + 1) * 128], w_in_v[:, :, c0:c0 + 128], 8, 128, wqkv_b[bi])

        BSETS = ((5, 6, 7), (0, 1, 2))

        def qk_P1(hh, blk, spec):
            (dst_t, dst_b, dcol, scol, n, wsel, rope, ropecol) = spec
            bi = hh % 2
            pa = BSETS[blk % 2][0]
            hb = hT_bufs(scol, n)
            for kt in range(8):
                mm(PS[pa][:, 0:n], wqkv_t[:, bi, kt, wsel * 128:(wsel + 1) * 128], hT(kt, scol, n),
                   kt == 0, kt == 7, [wqkv_b[bi]] + hb, [PB[pa]])

        def qk_P2(hh, blk, spec):
            (dst_t, dst_b, dcol, scol, n, wsel, rope, ropecol) = spec
            pa, pb_, pc = BSETS[blk % 2]
            i2 = blk % 2
            act(sq_t[:, i2, 0:n], PS[pa][:, 0:n], AF.Square, [PB[pa]], [sq_b[i2]])
            act(kw_t[:, i2, 0:n], PS[pa][:, 0:n], AF.Copy, [PB[pa], wcol_b], [kw_b[i2]], scale=wcol_t[:, wsel:wsel + 1])
            mm(PS[pb_][:, 0:n], bones_t[:, :], sq_t[:, i2, 0:n], True, True, [bones_b, sq_b[i2]], [PB[pb_]])
            if rope:
                mm(PS[pc][:, 0:n], prot_t[:, :], kw_t[:, i2, 0:n], True, True, [prot_b, kw_b[i2]], [PB[pc]])
            act(u_t[:, i2, 0:n], PS[pb_][:, 0:n], AF.Ln, [PB[pb_], EPS_b], [u_b[i2]], bias=EPS_t[:, 0:1], scale=1.0 / 64.0)
            act(u_t[:, i2, 0:n], u_t[:, i2, 0:n], AF.Exp, [u_b[i2]], [u_b[i2]], scale=-0.5)
            if rope:
                tt("pool", t1_t[:, i2, 0:n], kw_t[:, i2, 0:n], cos_t[:, ropecol:ropecol + n], ALU.mult, [kw_b[i2], cos_b], [t1_b[i2]])
                tt("dve", t2_t[:, i2, 0:n], PS[pc][:, 0:n], sin_t[:, ropecol:ropecol + n], ALU.mult, [PB[pc], sin_b], [t2_b[i2]])
                tt("dve", t2_t[:, i2, 0:n], t2_t[:, i2, 0:n], t1_t[:, i2, 0:n], ALU.add, [t1_b[i2], t2_b[i2]], [t2_b[i2]])
                tt("dve", dst_t[:, dcol:dcol + n], t2_t[:, i2, 0:n], u_t[:, i2, 0:n], ALU.mult, [t2_b[i2], u_b[i2]], [dst_b])
            else:
                tt("dve", dst_t[:, dcol:dcol + n], kw_t[:, i2, 0:n], u_t[:, i2, 0:n], ALU.mult, [kw_b[i2], u_b[i2]], [dst_b])

        def qk_all(hh):
            specs = []
            for (c0_, n_) in ((0, 512), (512, 512), (1024, 512), (1536, 512), (2048, 128), (2176, 512), (2688, 512), (3200, 512), (3712, 384)):
                specs.append((KT_t, KT_b, c0_, c0_, n_, 1, True, c0_))
            specs.append((KT_t, KT_b, 4096, 4096, 256, 1, False, 0))
            for tb in range(4):
                specs.append((QT_t, QT_b, tb * 512, tb * 512, 512, 0, True, tb * 512))
            specs.append((QT_t, QT_b, 2048, 2048, 1, 0, True, 2048))
            qk_P1(hh, 0, specs[0])
            for i_, sp in enumerate(specs):
                if i_ + 1 < len(specs):
                    qk_P1(hh, i_ + 1, specs[i_ + 1])
                qk_P2(hh, i_, sp)
                if i_ < 9:
                    v_group(hh, i_)

        def v_group(hh, g4):
            bi = hh % 2
            nt = 4 if g4 < 8 else 2
            pb = 3 + (g4 % 2)
            for j in range(nt):
                kti = g4 * 4 + j
                for kt in range(8):
                    mm(PS[pb][:, j * 128:(j + 1) * 128], hT(kt, kti * 128, 128),
                       wqkv_t[:, bi, kt, 256:384], kt == 0, kt == 7, [wqkv_b[bi], hT_b[kti]], [PB[pb]])
            eng = "act" if g4 % 2 == 0 else "dve"
            cp(eng, V_t[:, g4 * 4:g4 * 4 + nt, 0:128], PS[pb][:, 0:nt * 128].rearrange("p (j e) -> p j e", e=128),
               [PB[pb]], [V_b])

        def acc_ap(m, j, rows):
            if j < 3:
                return PS[2 + m][0:rows, j * 129:(j + 1) * 129], PB[2 + m]
            return PS[4][0:rows, m * 129:(m + 1) * 129], PB[4]

        pt_i = [0]
        fin_i = [0]

        accS_t, accS_b = kb.alloc("accS", [128, 2, 8, 129], F32, nsub=2)

        def attn_head(hh):
            qblocks = [(0, 512), (512, 512), (1024, 512), (1536, 512), (2048, 1)]
            pending = []
            for qbi, (q0, nq) in enumerate(qblocks):
                nj = (nq + 127) // 128
                SB = (0, 1, 5, 6)

                def st_mm(kk, m, q0=q0, nq=nq):
                    sbk = SB[(kk % 2) * 2 + m]
                    mm(PS[sbk][:, 0:nq], KT_t[m * 64:(m + 1) * 64, kk * 128:(kk + 1) * 128],
                       QT_t[m * 64:(m + 1) * 64, q0:q0 + nq], True, True, [KT_b, QT_b], [PB[sbk]])

                st_mm(0, 0)
                st_mm(0, 1)
                for kk in range(34):
                    if kk + 1 < 34:
                        st_mm(kk + 1, 0)
                        st_mm(kk + 1, 1)
                    pis = []
                    for m in range(2):
                        sbk = SB[(kk % 2) * 2 + m]
                        pi_ = pt_i[0] % 4
                        pt_i[0] += 1
                        pis.append(pi_)
                        act(PT_t[:, pi_, 0:nq], PS[sbk][:, 0:nq], AF.Exp, [PB[sbk], sm_b], [PT_b[pi_]], bias=NBIAS, scale=0.125)
                    for m in range(2):
                        pi_ = pis[m]
                        for j in range(nj):
                            rows = min(128, nq - j * 128)
                            ap, pbuf = acc_ap(m, j, rows)
                            mm(ap, PT_t[:, pi_, j * 128:j * 128 + rows], V_t[:, kk, :], kk == 0, kk == 33,
                               [PT_b[pi_], V_b], [pbuf])
                    while pending and pending[0][0] <= kk:
                        pending.pop(0)[1]()
                ai = qbi % 2
                r0 = min(128, nq)
                nj3 = min(nj, 3)
                for m in range(2):
                    cp("dve", accS_t[0:r0, ai, m * 4:m * 4 + nj3, :], PS[2 + m][0:r0, 0:nj3 * 129].rearrange("p (j e) -> p j e", e=129),
                       [PB[2 + m]], [accS_b[ai]])
                if nj == 4:
                    cp("dve", accS_t[:, ai, 3:8:4, :], PS[4][:, 0:258].rearrange("p (m e) -> p m e", e=129), [PB[4]], [accS_b[ai]])

                def fin_dve(j, fi, q0=q0, nq=nq, ai=ai):
                    rows = min(128, nq - j * 128)
                    a0 = accS_t[0:rows, ai, j, :]
                    a1 = accS_t[0:rows, ai, 4 + j, :]
                    fs = fs_t[0:rows, fi, :]
                    fb = fs_b[fi]
                    ab = accS_b[ai]
                    recip(fs[:, 0:1], a0[:, 128:129], [ab], [fb])
                    recip(fs[:, 1:2], a1[:, 128:129], [ab], [fb])
                    tt("dve", fs[:, 2:3], fs[:, 1:2], NEGLAM[0:rows, :], ALU.mult, [fb, sm_b], [fb])
                    ts("dve", o_t[0:rows, fi, :], a0[:, 0:128], fs[:, 0:1], None, ALU.mult, None, [ab, fb], [o_b[fi]])
                    stt(o_t[0:rows, fi, :], a1[:, 0:128], fs[:, 2:3], o_t[0:rows, fi, :], ALU.mult, ALU.add, [ab, fb, o_b[fi]], [o_b[fi]])
                    stt(ojunk_t[0:rows, :], o_t[0:rows, fi, :], 1.0, o_t[0:rows, fi, :], ALU.mult, ALU.mult,
                        [o_b[fi]], [ojunk_b, fb], accum=fs[:, 3:4])
                    ts("dve", fs[:, 4:5], fs[:, 3:4], 1.0 / 128.0, EPS, ALU.mult, ALU.add, [fb], [fb])
                    tt("pool", fs[:, 5:6], fs[:, 4:5], MHALF[0:rows, :], ALU.pow, [fb, sm_b], [fb])
                    stt(on_t[0:rows, fi, :], o_t[0:rows, fi, :], fs[:, 5:6], sw_t[0:rows, :], ALU.mult, ALU.mult,
                        [o_b[fi], fb, sw_b], [on_b[fi]])

                def fin_pe(j, fi, q0=q0, nq=nq, hh=hh):
                    rows = min(128, nq - j * 128)
                    pv = psbf(7)
                    tr(pv[:, 0:rows], on_t[0:rows, fi, :], ident_t[0:rows, 0:rows], [on_b[fi], ident_b], [PB[7]])
                    cp("dve", oT_t[:, hh, q0 + j * 128:q0 + j * 128 + rows], pv[:, 0:rows], [PB[7]], [oT_b])

                while pending:
                    pending.pop(0)[1]()
                for j in range(nj):
                    fi = fin_i[0] % 2
                    fin_i[0] += 1
                    pending.append((2 + 7 * j, (lambda j=j, fi=fi, f=fin_dve: f(j, fi))))
                    pending.append((7 + 7 * j, (lambda j=j, fi=fi, f=fin_pe: f(j, fi))))
                pending.sort(key=lambda t: t[0])
            while pending:
                pending.pop(0)[1]()

        ckpt("b_alloc")
        load_head_w(0)
        ckpt("b_ld")
        for hh in range(N_HEADS_RUN):
            if hh + 1 < 8:
                load_head_w(hh + 1)
            ckpt("b_ld2")
            qk_all(hh)
            if hh == 0:
                dbg_store("KT", KT_t[:, 0:256], [128, 256], [KT_b], BF16)
                dbg_store("KTc", KT_t[:, 4096:4352], [128, 256], [KT_b], BF16)
                dbg_store("QT", QT_t[:, 0:256], [128, 256], [QT_b], BF16)
                dbg_store("V", V_t[:, 0:2, :], [128, 2, 129], [V_b], BF16)
                ckpt("proj0")
            attn_head(hh)
            if hh == 0:
                ckpt("attn0")
        kb.release(hT_tiles["m"])
        dbg_store("oT", oT_t[:, :, :], [128, 8, NQ], [oT_b], BF16)
        ckpt("attn")

        qblocks = [(0, 512), (512, 512), (1024, 512), (1536, 512), (2048, 1)]
        h2T_t, h2T_b = kb.alloc("h2T", [128, 8, NQ + 7], BF16, top=True)
        mT_t, mT_b = kb.alloc("mT", [128, 8, NQ], BF16, top=True)
        mM = kb.mark()
        hsT_t, hsT_b = kb.alloc("hsT", [128, 4, NQ + 7], BF16)
        if not S5_ON:
            memset("pool", hsT_t[:, :, :], 0.0, [hsT_b])
        else:
            kb.dma("sp", hsT_t[:, :, :], hsT_spill.rearrange("p (k n) -> p k n", k=4), R=[hsT_spill_b], W=[hsT_b])
        wm_t, wm_b = kb.alloc("wm", [128, 2, 28, 128], BF16, nsub=2)
        sg_t, sg_b_ = kb.alloc("sg", [128, 1, 2, 512], F32)
        sg_b = [sg_b_, sg_b_]
        w_bs_v = I["w_bs"].rearrange("(kt p) n -> p kt n", p=128)
        w_ba_v = I["w_ba"].rearrange("(kt p) n -> p kt n", p=128)

        def load_merge_w(ft):
            bi = ft % 2
            wload(wm_t[:, bi, 0:4, :], w_bs_v[:, :, ft * 128:(ft + 1) * 128], 4, 128, wm_b[bi])
            wload(wm_t[:, bi, 4:12, :], w_ba_v[:, :, ft * 128:(ft + 1) * 128], 8, 128, wm_b[bi])
            wload(wm_t[:, bi, 12:20, :], w_in_v[:, :, 3584 + ft * 128:3584 + (ft + 1) * 128], 8, 128, wm_b[bi])
            wload(wm_t[:, bi, 20:28, :], w_in_v[:, :, 4608 + ft * 128:4608 + (ft + 1) * 128], 8, 128, wm_b[bi])

        load_merge_w(0)
        mi = 0
        for ft in range(8):
            if ft + 1 < 8:
                load_merge_w(ft + 1)
            bi = ft % 2
            for (q0, nq) in qblocks:
                i2 = mi % 2
                mi += 1
                hb = hT_bufs(q0, nq)
                for kt in range(4):
                    mm(PS[0][:, 0:nq], wm_t[:, bi, kt, :], hsT_t[:, kt, q0:q0 + nq], kt == 0, kt == 3, [wm_b[bi], hsT_b], [PB[0]])
                for kt in range(8):
                    mm(PS[1][:, 0:nq], wm_t[:, bi, 4 + kt, :], oT_t[:, kt, q0:q0 + nq], kt == 0, kt == 7, [wm_b[bi], oT_b], [PB[1]])
                for kt in range(8):
                    mm(PS[2][:, 0:nq], wm_t[:, bi, 12 + kt, :], hT(kt, q0, nq), kt == 0, kt == 7, [wm_b[bi]] + hb, [PB[2]])
                for kt in range(8):
                    mm(PS[3][:, 0:nq], wm_t[:, bi, 20 + kt, :], hT(kt, q0, nq), kt == 0, kt == 7, [wm_b[bi]] + hb, [PB[3]])
                act(sg_t[:, 0, 0, 0:nq], PS[2][:, 0:nq], AF.Sigmoid, [PB[2], V1_b], [sg_b[i2]], bias=V1_t[:, BG + ft:BG + ft + 1])
                act(sg_t[:, 0, 1, 0:nq], PS[3][:, 0:nq], AF.Sigmoid, [PB[3], V1_b], [sg_b[i2]], bias=V1_t[:, BG + 8 + ft:BG + 9 + ft])
                tt("dve", sg_t[:, 0, 0, 0:nq], PS[0][:, 0:nq], sg_t[:, 0, 0, 0:nq], ALU.mult, [PB[0], sg_b[i2]], [sg_b[i2]])
                tt("dve", sg_t[:, 0, 1, 0:nq], PS[1][:, 0:nq], sg_t[:, 0, 1, 0:nq], ALU.mult, [PB[1], sg_b[i2]], [sg_b[i2]])
                tt("pool", mT_t[:, ft, q0:q0 + nq], sg_t[:, 0, 0, 0:nq], sg_t[:, 0, 1, 0:nq], ALU.add, [sg_b[i2]], [mT_b])
        kb.release(mR0)
        dbg_store("mT", mT_t[:, :, :], [128, 8, NQ], [mT_b], BF16)
        ckpt("merge")

        mX = kb.mark()
        wo_t, wo_b = kb.alloc("wo", [128, 8, D], BF16)
        wload(wo_t[:, :, :], I["w_out"].rearrange("(kt p) n -> p kt n", p=128), 8, D, wo_b)
        xr_t, xr_b = kb.alloc("xr", [128, 2, D], F32, nsub=2)
        xm_t, xm_b = kb.alloc("xm", [128, 2, D], F32, nsub=2)
        xn2_t, xn2_b = kb.alloc("xn2", [128, 2, D], BF16, nsub=2)
        st2_t, st2_b = kb.alloc("st2", [128, 2, 4], F32, nsub=2)
        xjunk_t, xjunk_b = kb.alloc("xjunk", [128, D], F32)
        xm_out = [Buf(f"xmout{i}") for i in range(16)]
        for ti in range(17):
            rows = 128 if ti < 16 else 1
            c0 = ti * 128
            i2 = ti % 2
            src = I["xo"][c0:c0 + rows, :] if ti < 16 else I["xt"][0:1, :]
            kb.dma("sp", xr_t[0:rows, i2, :], src, W=[xr_b[i2]])
            for hf in range(2):
                pb = 4 + hf
                for kt in range(8):
                    mm(PS[pb][0:rows, :], mT_t[:, kt, c0:c0 + rows], wo_t[:, kt, hf * 512:(hf + 1) * 512], kt == 0, kt == 7,
                       [mT_b, wo_b], [PB[pb]])
                tt("dve", xm_t[0:rows, i2, hf * 512:(hf + 1) * 512], PS[pb][0:rows, :], ga_t[0:rows, 0, hf * 512:(hf + 1) * 512],
                   ALU.mult, [PB[pb], ga_b], [xm_b[i2]])
            tt("pool", xm_t[0:rows, i2, :], xm_t[0:rows, i2, :], xr_t[0:rows, i2, :], ALU.add, [xm_b[i2], xr_b[i2]], [xm_b[i2]])
            if ti < 16:
                kb.dma("pool", out_d[c0:c0 + 128, :], xm_t[:, i2, :], R=[xm_b[i2]], W=[xm_out[ti]])
            sv = st2_t[0:rows, i2, :]
            stt(xjunk_t[0:rows, :], xm_t[0:rows, i2, :], 1.0, xm_t[0:rows, i2, :], ALU.mult, ALU.mult, [xm_b[i2]], [xjunk_b, st2_b[i2]],
                accum=sv[:, 0:1])
            ts("dve", sv[:, 1:2], sv[:, 0:1], 1.0 / D, EPS, ALU.mult, ALU.add, [st2_b[i2]], [st2_b[i2]])
            act(sv[:, 3:4], sv[:, 1:2], AF.Ln, [st2_b[i2]], [st2_b[i2]])
            act(sv[:, 2:3], sv[:, 3:4], AF.Exp, [st2_b[i2]], [st2_b[i2]], scale=-0.5)
            ts("dve", xn2_t[0:rows, i2, :], xm_t[0:rows, i2, :], sv[:, 2:3], None, ALU.mult, None, [xm_b[i2], st2_b[i2]], [xn2_b[i2]])
            pb = 6 + (ti % 2)
            pv = psbf(pb)
            for kt in range(8):
                tr(pv[:, kt * 128:kt * 128 + rows], xn2_t[0:rows, i2, kt * 128:(kt + 1) * 128], ident_t[0:rows, 0:rows],
                   [xn2_b[i2], ident_b], [PB[pb]])
            for kt in range(8):
                ts("dve", h2T_t[:, kt, c0:c0 + rows], pv[:, kt * 128:kt * 128 + rows], MOD_t[:, 4, kt:kt + 1], MOD_t[:, 5, kt:kt + 1],
                   ALU.mult, ALU.add, [PB[pb], MOD_b], [h2T_b])
        kb.release(mX)
        dbg_store("h2T", h2T_t[:, :, 0:NQ], [128, 8, NQ], [h2T_b], BF16)
        ckpt("xmid")

        actT_t, actT_b = kb.alloc("actT", [128, 22, NOWN], BF16)
        mF = kb.mark()
        wup_t, wup_b = kb.alloc("wup", [128, 2, 8, 256], BF16, nsub=2)
        ya_t, ya_b = kb.alloc("ya", [128, 2, 512], F32, nsub=2)
        yg_t, yg_b = kb.alloc("yg", [128, 2, 512], F32, nsub=2)
        sl_t, sl_b = kb.alloc("sl", [128, 2, 512], F32, nsub=2)
        w_up_v = I["w_up"].rearrange("(kt p) n -> p kt n", p=128)

        def load_up_w(fc):
            bi = fc % 2
            wload(wup_t[:, bi, :, 0:128], w_up_v[:, :, fc * 128:(fc + 1) * 128], 8, 128, wup_b[bi])
            wload(wup_t[:, bi, :, 128:256], w_up_v[:, :, DFF + fc * 128:DFF + (fc + 1) * 128], 8, 128, wup_b[bi])

        load_up_w(0)
        wd_v = I["w_down"].rearrange("(kt p) n -> p kt n", p=128)
        fblocks = [(0, 500), (500, 1000), (1000, 1500), (1500, 2000), (2000, 2048)]
        fi_ = 0
        for fc in range(22):
            if fc + 1 < 22:
                load_up_w(fc + 1)
            bi = fc % 2
            for (s0, e0) in fblocks:
                i2 = fi_ % 2
                fi_ += 1
                cin = max(s0 - 1, 0)
                nin = e0 + 1 - cin
                L = e0 - s0
                off = s0 - cin
                pa, pg = (0, 1) if i2 == 0 else (2, 3)
                for kt in range(8):
                    mm(PS[pa][:, 0:nin], wup_t[:, bi, kt, 0:128], h2T_t[:, kt, cin:cin + nin], kt == 0, kt == 7, [wup_b[bi], h2T_b], [PB[pa]])
                for kt in range(8):
                    mm(PS[pg][:, 0:nin], wup_t[:, bi, kt, 128:256], h2T_t[:, kt, cin:cin + nin], kt == 0, kt == 7, [wup_b[bi], h2T_b], [PB[pg]])
                for (pp, y_t, y_b, fcol) in ((pa, ya_t, ya_b, fc), (pg, yg_t, yg_b, 22 + fc)):
                    yv = y_t[:, i2, 0:L]
                    act(yv, PS[pp][:, off:off + L], AF.Identity, [PB[pp], V2_b], [y_b[i2]],
                        bias=V2_t[:, 132 + fcol:133 + fcol], scale=V2_t[:, 44 + fcol:45 + fcol])
                    stt(yv, PS[pp][:, off + 1:off + 1 + L], V2_t[:, 88 + fcol:89 + fcol], yv, ALU.mult, ALU.add, [PB[pp], V2_b, y_b[i2]], [y_b[i2]])
                    lo = 1 if s0 == 0 else 0
                    stt(y_t[:, i2, lo:L], PS[pp][:, off - 1 + lo:off - 1 + L], V2_t[:, fcol:fcol + 1], y_t[:, i2, lo:L], ALU.mult, ALU.add,
                        [PB[pp], V2_b, y_b[i2]], [y_b[i2]])
                act(sl_t[:, i2, 0:L], yg_t[:, i2, 0:L], AF.Silu, [yg_b[i2]], [sl_b[i2]])
                tt("pool", actT_t[:, fc, s0:e0], sl_t[:, i2, 0:L], ya_t[:, i2, 0:L], ALU.mult, [sl_b[i2], ya_b[i2]], [actT_b])
        dbg_store("actT", actT_t[:, :, 0:256], [128, 22, 256], [actT_b], BF16)
        ckpt("ffn_up")
        kb.release(mF)
        kb.release_top(SB_LIMIT)
        wd_t, wd_b = kb.alloc("wd", [128, 22, D], BF16)
        wload(wd_t[:, :, :], wd_v[:, :, :], 22, D, wd_b, cast_engs=("pool", "dve", "act"))
        xo2_t, xo2_b = kb.alloc("xo2", [128, 2, D], F32, nsub=2)
        fo_t, fo_b = kb.alloc("fo", [128, 2, D], F32, nsub=2)
        for ti in range(16):
            c0 = ti * 128
            i2 = ti % 2
            kb.dma("sp", xo2_t[:, i2, :], out_d[c0:c0 + 128, :], R=[xm_out[ti]], W=[xo2_b[i2]])
            for hf in range(2):
                pb = 4 + hf + 2 * (ti % 2)
                for fc in range(22):
                    mm(PS[pb][:, :], actT_t[:, fc, c0:c0 + 128], wd_t[:, fc, hf * 512:(hf + 1) * 512], fc == 0, fc == 21,
                       [actT_b, wd_b], [PB[pb]])
                tt("dve", fo_t[:, i2, hf * 512:(hf + 1) * 512], PS[pb][:, :], ga_t[:, 1, hf * 512:(hf + 1) * 512], ALU.mult,
                   [PB[pb], ga_b], [fo_b[i2]])
            tt("pool", fo_t[:, i2, :], fo_t[:, i2, :], xo2_t[:, i2, :], ALU.add, [fo_b[i2], xo2_b[i2]], [fo_b[i2]])
            ob = Buf(f"outf{ti}")
            kb.dma("pool", out_d[c0:c0 + 128, :], fo_t[:, i2, :], R=[fo_b[i2], xo2_b[i2]], W=[ob, xm_out[ti]])
            obufs.append(ob)

    try:
        body()
    except _Stop:
        pass

    kb.wait_all("sp", obufs)
    kb.emit()
    print("SBUF peak bytes/partition:", kb.sb_peak - SB_BASE, " ops:", {e: len(kb.q[e]) for e in ENGS})
    return nc, dbg_out


def make_in_maps(inputs):
    x = np.asarray(inputs["x"], np.float32)
    ctx = np.asarray(inputs["ctx"], np.float32)
    c = np.asarray(inputs["c"], np.float32)
    c_ctx = np.asarray(inputs["c_ctx"], np.float32)
    g = lambda k: np.ascontiguousarray(np.asarray(inputs[k], np.float32)[0])
    lamv = np.stack([g("lam_q1"), g("lam_k1"), g("lam_q2"), g("lam_k2")], 0)
    common = {
        "ada_w": g("ada_w"), "ada_b": g("ada_b"), "norm1_w": g("norm1_w"), "w_in": g("w_in"),
        "b_gate": g("b_gate"), "q_norm_w": g("q_norm_w"), "k_norm_w": g("k_norm_w"), "lamv": lamv,
        "subln_w": g("subln_w"), "s5_d": g("s5_d"), "glu_w": g("glu_w"), "glu_b": g("glu_b"),
        "w_bs": g("w_branch_s5"), "w_ba": g("w_branch_attn"), "w_out": g("w_out"), "norm2_w": g("norm2_w"),
        "w_up": g("w_up"), "conv_b": g("conv_b"), "w_down": g("w_down"),
    }
    maps = []
    for cid in range(8):
        b, h = cid // 2, cid % 2
        m = dict(common)
        if h == 0:
            m["xo"] = np.ascontiguousarray(x[b, 0:NOWN])
            m["xt"] = np.ascontiguousarray(x[b, NOWN:])
            m["cx"] = np.ascontiguousarray(ctx[b])
            do = [0, 1]
            m["conv_w"] = g("conv_w")
            m["posinfo"] = np.array([1.0, 0.0], np.float32)
        else:
            m["xo"] = np.ascontiguousarray(x[b, :NOWN - 1:-1])
            m["xt"] = np.ascontiguousarray(x[b, NOWN - 1::-1])
            m["cx"] = np.ascontiguousarray(ctx[b, ::-1])
            do = [1, 0]
            m["conv_w"] = np.ascontiguousarray(g("conv_w")[::-1])
            m["posinfo"] = np.array([-1.0, 63.0], np.float32)
        m["cvec"] = np.ascontiguousarray(np.stack([c[b], c_ctx], 0))
        for k in ("s5_a_re", "s5_a_im", "s5_b_re", "s5_b_im", "s5_c_re", "s5_c_im"):
            m[k] = np.ascontiguousarray(g(k)[do])
        m["s5_log_dt"] = np.ascontiguousarray(g("s5_log_dt")[do].reshape(64))
        maps.append(m)
    return maps


def assemble(results):
    out = np.zeros((4, NLAT, D), np.float32)
    for cid in range(8):
        b, h = cid // 2, cid % 2
        r = results[cid]["out"]
        if h == 0:
            out[b, 0:NOWN] = r
        else:
            out[b, NOWN:] = r[::-1]
    return out


_NC_CACHE = {}


def kernel(**inputs):
    if "nc" not in _NC_CACHE:
        _NC_CACHE["nc"] = build()[0]
    nc = _NC_CACHE["nc"]
    res = run_bass_kernel_spmd(nc, make_in_maps(inputs), core_ids=list(range(8)))
    return assemble(res.results)
```

```python
import contextlib
import math
import numpy as np
import concourse.bass as bass
import concourse.mybir as mybir
from concourse.bass_utils import run_bass_kernel_spmd

F32 = mybir.dt.float32
BF16 = mybir.dt.bfloat16
I32 = mybir.dt.int32
AF = mybir.ActivationFunctionType
ALU = mybir.AluOpType
AX = mybir.AxisListType
DT_SIZE = {F32: 4, BF16: 2, I32: 4}
ENGS = ("pe", "act", "dve", "pool", "sp")
N_DSEM = 24
SB_BASE = 16512
SB_LIMIT = 229344

D = 1024
NOWN = 2048
NQ = 2049
NLAT = 4096
NCTX = 256
NALL = 4352
NIN = 5632
DFF = 2816
EPS = 1e-6
S5_ON = True
N_HEADS_RUN = 8
TWO_PI = 2.0 * math.pi


class Buf:
    __slots__ = ("name", "lw", "rd", "alias", "wd", "excl")

    def __init__(self, name, excl=False):
        self.name = name
        self.excl = excl
        self.lw = None
        self.rd = {}
        self.alias = []
        self.wd = {}


class KB:
    def __init__(self, nc):
        self.nc = nc
        self.q = {e: [] for e in ENGS}
        self.cnt = {e: 0 for e in ENGS}
        self.seen = {e: {} for e in ENGS}
        self.dval = [0] * N_DSEM
        self.dnext = 0
        self.targets = {e: set() for e in ENGS}
        self.sb_ptr = SB_BASE
        self.top_ptr = SB_LIMIT
        self.sb_hist = []
        self.sb_peak = SB_BASE
        self.uid = 0

    def alloc(self, name, shape, dtype, nsub=1, top=False):
        free = 1
        for s in shape[1:]:
            free *= s
        nbytes = free * DT_SIZE[dtype]
        if top:
            end = self.top_ptr // 64 * 64
            start = (end - nbytes) // 64 * 64
            assert start >= self.sb_ptr, f"SBUF overflow (top) allocating {name}: {self.sb_ptr - start} over"
            self.top_ptr = start
        else:
            start = (self.sb_ptr + 63) // 64 * 64
            end = start + nbytes
            assert end <= self.top_ptr, f"SBUF overflow allocating {name}: {end - self.top_ptr} over"
            self.sb_ptr = end
        self.sb_peak = max(self.sb_peak, self.sb_ptr + (SB_LIMIT - self.top_ptr))
        self.uid += 1
        t = self.nc.alloc_sbuf_tensor_at(f"{name}_{self.uid}", list(shape), dtype, offset=start)
        bufs = [Buf(f"{name}.{i}") for i in range(nsub)]
        old = []
        keep = []
        for (s, e, bl) in self.sb_hist:
            if s < end and start < e:
                old.extend(bl)
            keep.append((s, e, bl))
        for b in bufs:
            b.alias = list(old)
        self.sb_hist.append((start, end, bufs))
        return (t, bufs[0]) if nsub == 1 else (t, bufs)

    def mark(self):
        return self.sb_ptr

    def release(self, m):
        self.sb_ptr = m

    def mark_top(self):
        return self.top_ptr

    def release_top(self, m):
        self.top_ptr = m

    def _deps(self, eng, R, W):
        waits = {}
        seen = self.seen[eng]

        def need(key, val):
            if seen.get(key, 0) >= val:
                return
            if waits.get(key, 0) < val:
                waits[key] = val

        def need_all(b):
            if b.lw is not None and b.lw[0] != eng:
                need(*b.lw)
            for k, v in b.wd.items():
                need(k, v)
            for k, v in b.rd.items():
                if k != eng:
                    need(k, v)

        for b in R:
            if b.alias:
                for a in b.alias:
                    need_all(a)
            if b.lw is not None and not (eng == "pe" and b.lw[0] == "pe"):
                need(*b.lw)
            for k, v in b.wd.items():
                need(k, v)
            if b.excl:
                for k, v in b.rd.items():
                    if k != eng:
                        need(k, v)
        for b in W:
            if b.alias:
                for a in b.alias:
                    need_all(a)
                b.alias = []
            need_all(b)
        for k, v in waits.items():
            seen[k] = v
            if k in self.targets:
                self.targets[k].add(v)
        return list(waits.items())

    def _mark(self, tok, R, W):
        k, v = tok
        for b in R:
            if b.rd.get(k, 0) < v:
                b.rd[k] = v
        for b in W:
            if k[0] == "d" and k[1:].isdigit():
                b.wd[k] = v
            else:
                b.wd = {}
            b.lw = tok
            b.rd = {}

    def op(self, eng, fn, R=(), W=()):
        waits = self._deps(eng, R, W)
        self.cnt[eng] += 1
        idx = self.cnt[eng]
        self._mark((eng, idx), R, W)
        self.q[eng].append((waits, fn, idx, None))

    def dma(self, eng, out_ap, in_ap, R=(), W=(), **kw):
        k = self.dnext
        self.dnext = (self.dnext + 1) % N_DSEM
        key = f"d{k}"
        waits = self._deps(eng, R, W)
        prev = self.dval[k]
        if prev > 0 and self.seen[eng].get(key, 0) < prev:
            waits.append((key, prev))
            self.seen[eng][key] = prev
        self.dval[k] = prev + 16
        self._mark((key, prev + 16), R, W)
        self.q[eng].append((waits, lambda e: e.dma_start(out=out_ap, in_=in_ap, **kw), None, k))

    def wait_all(self, eng, bufs):
        waits = self._deps(eng, bufs, ())
        self.q[eng].append((waits, None, None, None))

    def emit(self):
        nc = self.nc
        sems = {}
        with contextlib.ExitStack() as st:
            for e in ENGS:
                sems[e] = st.enter_context(nc.semaphore(f"s_{e}"))
            for k in range(N_DSEM):
                sems[f"d{k}"] = st.enter_context(nc.semaphore(f"s_d{k}"))
            cmap = {}
            for e in ENGS:
                tl = sorted(self.targets[e])
                cmap[e] = {idx: i + 1 for i, idx in enumerate(tl)}
            block = st.enter_context(nc.Block())

            def replay(e, eng):
                tg = cmap[e]
                for (waits, fn, idx, dk) in self.q[e]:
                    for (key, val) in waits:
                        v = cmap[key][val] if key in cmap else val
                        eng.wait_ge(sems[key], v)
                    if fn is None:
                        continue
                    ins = fn(eng)
                    if dk is not None:
                        ins.then_inc(sems[f"d{dk}"], 16)
                    elif idx in tg:
                        ins.then_inc(sems[e], 1)

            @block.tensor
            def _(eng):
                replay("pe", eng)

            @block.scalar
            def _(eng):
                replay("act", eng)

            @block.vector
            def _(eng):
                replay("dve", eng)

            @block.gpsimd
            def _(eng):
                replay("pool", eng)

            @block.sync
            def _(eng):
                replay("sp", eng)


INPUT_SPECS = [
    ("xo", [NOWN, D]), ("xt", [NOWN, D]), ("cx", [NCTX, D]), ("cvec", [2, D]),
    ("ada_w", [D, 6 * D]), ("ada_b", [6 * D]), ("norm1_w", [D]), ("w_in", [D, NIN]),
    ("b_gate", [2 * D]), ("q_norm_w", [64]), ("k_norm_w", [64]), ("lamv", [4, 64]),
    ("subln_w", [128]), ("s5_a_re", [2, 32, 64]), ("s5_a_im", [2, 32, 64]), ("s5_log_dt", [64]),
    ("s5_b_re", [2, 32, 64, 16]), ("s5_b_im", [2, 32, 64, 16]), ("s5_c_re", [2, 32, 16, 64]),
    ("s5_c_im", [2, 32, 16, 64]), ("s5_d", [512]), ("glu_w", [512, 512]), ("glu_b", [512]),
    ("w_bs", [512, D]), ("w_ba", [D, D]), ("w_out", [D, D]), ("norm2_w", [D]),
    ("w_up", [D, NIN]), ("conv_w", [3, NIN]), ("conv_b", [NIN]), ("w_down", [DFF, D]),
    ("posinfo", [2]),
]


class _Stop(Exception):
    pass


def build(stop=None, dbg=()):
    nc = bass.Bass("TRN2", target_bir_lowering=False)
    I = {n: nc.dram_tensor(n, s, F32, kind="ExternalInput").ap() for n, s in INPUT_SPECS}
    out_d = nc.dram_tensor("out", [NOWN, D], F32, kind="ExternalOutput").ap()
    hT_spill = nc.dram_tensor("hT_spill", [128, 8 * NALL], BF16, kind="Internal").ap()
    hsT_spill = nc.dram_tensor("hsT_spill", [128, 4 * (NQ + 7)], BF16, kind="Internal").ap()
    hsT_spill_b = Buf("hsT_spill")
    hT_spill_b = Buf("hT_spill")
    kb = KB(nc)
    dbg_out = {}
    obufs = []

    def dbg_store(name, tile_ap, shape, rbufs, dtype=F32):
        if name not in dbg:
            return
        d = nc.dram_tensor("dbg_" + name, shape, dtype, kind="ExternalOutput").ap()
        ob = Buf("dbg_" + name)
        kb.dma("sp", d, tile_ap, R=rbufs, W=[ob])
        obufs.append(ob)
        dbg_out[name] = d

    def ckpt(name):
        if stop == name:
            raise _Stop()

    PS = [nc.alloc_psum_tensor(f"psb{i}", [128, 512], F32) for i in range(8)]
    PB = [Buf(f"psb{i}", excl=True) for i in range(8)]

    def psbf(i):
        return PS[i][:].bitcast(BF16)

    def mm(out, lhsT, rhs, start, stop, R, W, **kw):
        kb.op("pe", lambda e: e.matmul(out, lhsT, rhs, start=start, stop=stop, **kw), R, W)

    def tr(out, in_, ident, R, W):
        kb.op("pe", lambda e: e.transpose(out, in_, ident), R, W)

    def act(out, in_, func, R, W, bias=0.0, scale=1.0, accum=None, eng="act"):
        if accum is None:
            kb.op(eng, lambda e: e.activation(out, in_, func, bias=bias, scale=scale), R, W)
        else:
            kb.op(eng, lambda e: e.activation(out, in_, func, bias=bias, scale=scale, accum_out=accum), R, W)

    def tt(eng, out, in0, in1, op, R, W):
        kb.op(eng, lambda e: e.tensor_tensor(out, in0, in1, op), R, W)

    def ts(eng, out, in0, s1, s2, op0, op1, R, W):
        if s2 is None:
            kb.op(eng, lambda e: e.tensor_scalar(out, in0, s1, None, op0), R, W)
        else:
            kb.op(eng, lambda e: e.tensor_scalar(out, in0, s1, s2, op0, op1), R, W)

    def stt(out, in0, scalar, in1, op0, op1, R, W, accum=None):
        if accum is None:
            kb.op("dve", lambda e: e.scalar_tensor_tensor(out, in0, scalar, in1, op0, op1), R, W)
        else:
            kb.op("dve", lambda e: e.scalar_tensor_tensor(out, in0, scalar, in1, op0, op1, accum_out=accum), R, W)

    def cp(eng, out, in_, R, W):
        if eng == "act":
            kb.op(eng, lambda e: e.copy(out, in_), R, W)
        else:
            kb.op(eng, lambda e: e.tensor_copy(out, in_), R, W)

    def memset(eng, ap, val, W):
        kb.op(eng, lambda e: e.memset(ap, val), (), W)

    def iota(ap, pattern, base, cm, W):
        kb.op("pool", lambda e: e.iota(ap, pattern, base=base, channel_multiplier=cm), (), W)

    def asel(out, in_, pattern, cmp, fill, base, cm, R, W):
        kb.op("pool", lambda e: e.affine_select(out, in_, pattern=pattern, compare_op=cmp, fill=fill,
                                                 base=base, channel_multiplier=cm), R, W)

    def recip(out, in_, R, W):
        kb.op("dve", lambda e: e.reciprocal(out, in_), R, W)

    def body():
        ident_t, ident_b = kb.alloc("ident", [128, 128], BF16)
        memset("pool", ident_t[:], 0.0, [ident_b])
        asel(ident_t[:], ident_t[:], [[-1, 128]], ALU.not_equal, 1.0, 0, 1, [ident_b], [ident_b])

        EPS_t, EPS_b = kb.alloc("eps", [128, 1], F32)
        memset("pool", EPS_t[:, :], EPS, [EPS_b])
        mh2_t, mh2_b = kb.alloc("mh2", [128, 1], F32)
        memset("pool", mh2_t[:, :], -0.5, [mh2_b])
        MHALF2 = mh2_t[:, 0:1]
        NSTG = 3
        STG_W = 1024
        stg_t, stg_b = kb.alloc("stg", [128, NSTG, STG_W], F32, nsub=NSTG)
        stg_i = [0]
        cast_rr = [0]

        def wload(dst3, src3, nrow, ncol, Wb, cast_engs=("pool",)):
            if ncol <= STG_W:
                rp = STG_W // ncol
                pieces = [(r, min(rp, nrow - r), 0, ncol) for r in range(0, nrow, rp)]
            else:
                pieces = [(r, 1, c, min(STG_W, ncol - c)) for r in range(nrow) for c in range(0, ncol, STG_W)]
            for (r, nr, c, ncc) in pieces:
                i = stg_i[0]
                stg_i[0] = (i + 1) % NSTG
                sview = stg_t[:, i, 0:nr * ncc].rearrange("p (r c) -> p r c", r=nr)
                kb.dma("sp", sview, src3[:, r:r + nr, c:c + ncc], W=[stg_b[i]])
                ce = cast_engs[cast_rr[0] % len(cast_engs)]
                cast_rr[0] += 1
                cp(ce, dst3[:, r:r + nr, c:c + ncc], sview, [stg_b[i]], [Wb])

        def rows_to_cols(dst, dst_b, row_srcs):
            m0 = kb.mark()
            total = sum(n for _, n in row_srcs)
            rs_t, rs_b = kb.alloc("rs", [128, 128], F32)
            hi_t, hi_b = kb.alloc("rhi", [128, 128], BF16)
            lo_t, lo_b = kb.alloc("rlo", [128, 128], BF16)
            tmp_t, tmp_b = kb.alloc("rtmp", [128, 128], F32)
            r0 = 0
            for ap, n in row_srcs:
                kb.dma("sp", rs_t[r0:r0 + n, :], ap, W=[rs_b])
                r0 += n
            cp("dve", hi_t[0:total, :], rs_t[0:total, :], [rs_b], [hi_b])
            tt("dve", tmp_t[0:total, :], rs_t[0:total, :], hi_t[0:total, :], ALU.subtract, [rs_b, hi_b], [tmp_b])
            cp("dve", lo_t[0:total, :], tmp_t[0:total, :], [tmp_b], [lo_b])
            pv = psbf(0)
            tr(pv[:, 0:total], hi_t[0:total, :], ident_t[0:total, 0:total], [hi_b, ident_b], [PB[0]])
            tr(pv[:, 128:128 + total], lo_t[0:total, :], ident_t[0:total, 0:total], [lo_b, ident_b], [PB[0]])
            cp("dve", tmp_t[:, 0:total], pv[:, 0:total], [PB[0]], [tmp_b])
            tt("dve", dst, tmp_t[:, 0:total], pv[:, 128:128 + total], ALU.add, [tmp_b, PB[0]], [dst_b])
            kb.release(m0)

        V1_t, V1_b = kb.alloc("V1", [128, 104], F32)
        rows_to_cols(V1_t[:, :], V1_b, [
            (I["ada_b"].rearrange("(r c) -> r c", c=128), 48),
            (I["norm1_w"].rearrange("(r c) -> r c", c=128), 8),
            (I["norm2_w"].rearrange("(r c) -> r c", c=128), 8),
            (I["b_gate"].rearrange("(r c) -> r c", c=128), 16),
            (I["glu_b"].rearrange("(r c) -> r c", c=128), 4),
            (I["s5_d"].rearrange("(r c) -> r c", c=128), 4),
            (I["cvec"].rearrange("t (r c) -> (t r) c", c=128), 16),
        ])
        ADAB, N1W, N2W, BG, GLUB, S5D, CV = 0, 48, 56, 64, 80, 84, 88
        dbg_store("V1", V1_t[:, :], [128, 104], [V1_b])
        ckpt("V1")
        V2_t, V2_b = kb.alloc("V2", [128, 176], F32)
        cwv = I["conv_w"].rearrange("j (r c) -> (j r) c", c=128)
        rows_to_cols(V2_t[:, 0:88], V2_b, [(cwv[0:88, :], 88)])
        rows_to_cols(V2_t[:, 88:176], V2_b, [(cwv[88:132, :], 44), (I["conv_b"].rearrange("(r c) -> r c", c=128), 44)])

        ckpt("V2")
        sc_t, sc_b = kb.alloc("sc", [128, 8, 2], BF16)
        scb_t, scb_b = kb.alloc("scb", [128, 8, 128], BF16)
        act(sc_t[:, :, :].rearrange("p k t -> p t k"), V1_t[:, CV:CV + 16].rearrange("p (t k) -> p t k", t=2),
            AF.Silu, [V1_b], [sc_b])
        cp("dve", scb_t[:, :, :], sc_t[:, :, 0:1].broadcast_to([128, 8, 128]), [sc_b], [scb_b])

        ckpt("silu")
        modv_t, modv_b = kb.alloc("modv", [128, 48, 2], F32)
        ga_t, ga_b = kb.alloc("ga", [128, 2, D], F32)
        m_ada = kb.mark()
        adaw_t, adaw_b = kb.alloc("adaw", [128, 2, 8, D], BF16, nsub=2)
        abb_t, abb_b = kb.alloc("abb", [128, D], F32)
        adaw_src = I["ada_w"].rearrange("(kt p) n -> p kt n", p=128)
        for ci in range(6):
            bi = ci % 2
            wload(adaw_t[:, bi], adaw_src[:, :, ci * D:(ci + 1) * D], 8, D, adaw_b[bi], cast_engs=("pool", "dve", "act", "dve"))
            for ft in range(8):
                for kt in range(8):
                    mm(PS[1][:, (ci * 8 + ft) * 2:(ci * 8 + ft) * 2 + 2], adaw_t[:, bi, kt, ft * 128:(ft + 1) * 128],
                       sc_t[:, kt, :], kt == 0, kt == 7, [adaw_b[bi], sc_b], [PB[1]])
            if ci in (2, 5):
                gi = 0 if ci == 2 else 1
                kb.dma("sp", abb_t[:, :], I["ada_b"][ci * D:(ci + 1) * D].partition_broadcast(128), W=[abb_b])
                for hf in range(2):
                    for kt in range(8):
                        mm(PS[2 + hf][:, :], scb_t[:, kt, :], adaw_t[:, bi, kt, hf * 512:(hf + 1) * 512],
                           kt == 0, kt == 7, [adaw_b[bi], scb_b], [PB[2 + hf]])
                    tt("dve", ga_t[:, gi, hf * 512:(hf + 1) * 512], PS[2 + hf][:, :], abb_t[:, hf * 512:(hf + 1) * 512],
                       ALU.add, [PB[2 + hf], abb_b], [ga_b])
        tt("dve", modv_t[:, :, :], PS[1][:, 0:96].rearrange("p (f t) -> p f t", t=2),
           V1_t[:, ADAB:ADAB + 48, None].broadcast_to([128, 48, 2]), ALU.add, [PB[1], V1_b], [modv_b])
        kb.release(m_ada)
        MOD_t, MOD_b = kb.alloc("MOD", [128, 6, 8], F32)
        stt(MOD_t[:, 0, :], modv_t[:, 8:16, 0], 1.0, V1_t[:, N1W:N1W + 8], ALU.add, ALU.mult, [modv_b, V1_b], [MOD_b])
        cp("dve", MOD_t[:, 1, :], modv_t[:, 0:8, 0], [modv_b], [MOD_b])
        stt(MOD_t[:, 2, :], modv_t[:, 8:16, 1], 1.0, V1_t[:, N1W:N1W + 8], ALU.add, ALU.mult, [modv_b, V1_b], [MOD_b])
        cp("dve", MOD_t[:, 3, :], modv_t[:, 0:8, 1], [modv_b], [MOD_b])
        stt(MOD_t[:, 4, :], modv_t[:, 32:40, 0], 1.0, V1_t[:, N2W:N2W + 8], ALU.add, ALU.mult, [modv_b, V1_b], [MOD_b])
        cp("dve", MOD_t[:, 5, :], modv_t[:, 24:32, 0], [modv_b], [MOD_b])
        ckpt("mod")
        dbg_store("MOD", MOD_t[:, :, :], [128, 6, 8], [MOD_b])
        dbg_store("ga", ga_t[:, :, :], [128, 2, D], [ga_b])
        ckpt("mod2")

        mR0 = kb.mark()
        oT_box = {}

        def alloc_oT():
            oT_box["t"], oT_box["b"] = kb.alloc("oT", [128, 8, NQ], BF16)

        if not S5_ON:
            alloc_oT()
        HSPLIT = 2176
        hT_b = [Buf(f"hT{i}") for i in range(34)]
        hT_tiles = {}

        def alloc_hT():
            hT_tiles["o"], bo_ = kb.alloc("hTo", [128, 8, HSPLIT], BF16)
            hT_tiles["m"] = kb.mark()
            hT_tiles["r"], br_ = kb.alloc("hTr", [128, 8, NALL - HSPLIT], BF16)
            for i_, b_ in enumerate(hT_b):
                b_.alias = list(bo_.alias if i_ < 17 else br_.alias)
            kb.sb_hist[-2] = (kb.sb_hist[-2][0], kb.sb_hist[-2][1], hT_b[0:17])
            kb.sb_hist[-1] = (kb.sb_hist[-1][0], kb.sb_hist[-1][1], hT_b[17:34])

        def hT(kt, c0, n):
            if c0 + n <= HSPLIT:
                return hT_tiles["o"][:, kt, c0:c0 + n]
            assert c0 >= HSPLIT, (c0, n)
            return hT_tiles["r"][:, kt, c0 - HSPLIT:c0 - HSPLIT + n]

        alloc_hT()
        m1 = kb.mark()
        xin_t, xin_b = kb.alloc("xin", [128, 3, D], F32, nsub=3)
        xn_t, xn_b = kb.alloc("xn", [128, 2, D], BF16, nsub=2)
        junk_t, junk_b = kb.alloc("junk", [128, D], BF16)
        st_t, st_b = kb.alloc("st", [128, 4, 4], F32, nsub=4)

        def norm_tile(src_ap, xi, ji, col0, moda, modb, nrows=128):
            ti = col0 // 128
            sb_ = st_b[ji % 4]
            stv = st_t[:, ji % 4, :]
            act(junk_t[0:nrows, :], src_ap, AF.Square, [xin_b[xi]], [junk_b, sb_], accum=stv[0:nrows, 0:1])
            act(stv[0:nrows, 1:2], stv[0:nrows, 0:1], AF.Sqrt, [sb_], [sb_], bias=EPS_t[0:nrows, 0:1], scale=1.0 / D)
            ckpt("n_sqrt")
            recip(stv[0:nrows, 2:3], stv[0:nrows, 1:2], [sb_], [sb_])
            ckpt("n_recip")
            xb = ji % 2
            ts("dve", xn_t[0:nrows, xb, :], src_ap, stv[0:nrows, 2:3], None, ALU.mult, None, [xin_b[xi], sb_], [xn_b[xb]])
            ckpt("n_xn")
            pb = 4 + (ji % 2)
            pv = psbf(pb)
            for kt in range(8):
                tr(pv[:, kt * 128:kt * 128 + nrows], xn_t[0:nrows, xb, kt * 128:(kt + 1) * 128],
                   ident_t[0:nrows, 0:nrows], [xn_b[xb], ident_b], [PB[pb]])
            ckpt("n_tr")
            for kt in range(8):
                eng = "dve"
                ts(eng, hT(kt, col0, nrows), pv[:, kt * 128:kt * 128 + nrows],
                   MOD_t[:, moda, kt:kt + 1], MOD_t[:, modb, kt:kt + 1], ALU.mult, ALU.add,
                   [PB[pb], MOD_b], [hT_b[ti]])

        ji = 0
        for ti in range(34):
            if ti < 16:
                src = I["xo"][ti * 128:(ti + 1) * 128, :]
            elif ti < 32:
                src = I["xt"][(ti - 16) * 128:(ti - 15) * 128, :]
            else:
                src = I["cx"][(ti - 32) * 128:(ti - 31) * 128, :]
            xi = ti % 3
            kb.dma("sp", xin_t[:, xi, :], src, W=[xin_b[xi]])
            if ti < 32:
                norm_tile(xin_t[:, xi, :], xi, ji, ti * 128, 0, 1)
            else:
                norm_tile(xin_t[:, xi, :], xi, ji, ti * 128, 2, 3)
            ji += 1
            ckpt("n_tile1")
        kb.release(m1)
        dbg_store("hT", hT_tiles["o"][:, :, 0:256], [128, 8, 256], hT_b[0:2], BF16)
        dbg_store("hTc", hT_tiles["r"][:, :, 4096 - HSPLIT:4352 - HSPLIT], [128, 8, 256], hT_b[32:34], BF16)
        ckpt("stage1")
        hT_all = list(hT_b)

        def hT_bufs(c0, n):
            return hT_b[c0 // 128:(c0 + n + 127) // 128]

        w_in_v = I["w_in"].rearrange("(kt p) n -> p kt n", p=128)

        def s5_stage():
            mS = kb.mark()
            sinT = [None]

            def sin_turns(out_ap, x_ap, mul, add, R, W, tm):
                (t_ap, f_ap, g_ap, k_ap, tb) = tm
                ts("dve", t_ap, x_ap, mul, add, ALU.mult, ALU.add, R, [tb])
                cp("dve", k_ap, t_ap, [tb], [tb])
                cp("dve", f_ap, k_ap, [tb], [tb])
                tt("dve", f_ap, t_ap, f_ap, ALU.subtract, [tb], [tb])
                ts("dve", g_ap, f_ap, 0.5, None, ALU.is_gt, None, [tb], [tb])
                tt("dve", f_ap, f_ap, g_ap, ALU.subtract, [tb], [tb])
                ts("dve", g_ap, f_ap, -0.5, None, ALU.is_lt, None, [tb], [tb])
                tt("dve", f_ap, f_ap, g_ap, ALU.add, [tb], [tb])
                act(out_ap, f_ap, AF.Sin, [tb], W, scale=6.2831)

            U_t, U_b = kb.alloc("U", [128, 32, 544], BF16, top=True)
            RS_t, RS_b = kb.alloc("RS", [128, 8, 240], BF16, top=True)
            memset("pool", RS_t[:, :, :], 0.0, [RS_b])
            for k_ in range(8):
                asel(RS_t[:, k_, 112:128], RS_t[:, k_, 112:128], [[1, 16]], ALU.not_equal, 1.0, 16 * k_, -1, [RS_b], [RS_b])
            m_u = kb.mark()
            wu_t, wu_b = kb.alloc("wu", [128, 8, 512], BF16)
            wload(wu_t[:, :, :], w_in_v[:, :, 3072:3584], 8, 512, wu_b)
            uT_t, uT_b = kb.alloc("uTb", [128, 2, 4, 512], BF16, nsub=2)
            ei = 0
            for bi_, (c0_, n_) in enumerate(((0, 512), (512, 512), (1024, 512), (1536, 512), (2048, 128), (2176, 512), (2688, 512),
                                             (3200, 512), (3712, 512), (4224, 128))):
                ub = bi_ % 2
                for ct in range(4):
                    pb = ei % 2
                    for kt in range(8):
                        mm(PS[pb][:, 0:n_], wu_t[:, kt, ct * 128:(ct + 1) * 128], hT(kt, c0_, n_), kt == 0, kt == 7,
                           [wu_b] + hT_bufs(c0_, n_), [PB[pb]])
                    cp("act" if ei % 2 == 0 else "dve", uT_t[:, ub, ct, 0:n_], PS[pb][:, 0:n_], [PB[pb]], [uT_b[ub]])
                    ei += 1
                nch = n_ // 8
                ch0 = c0_ // 8
                for g4 in range(8):
                    pb = 2 + (g4 % 2)
                    for gq in range(4):
                        g = g4 * 4 + gq
                        ct, gl = g // 8, g % 8
                        for tau in range(8):
                            mm(PS[pb][:, gq * 64:gq * 64 + nch], RS_t[:, gl, 112 - 16 * tau:240 - 16 * tau],
                               uT_t[:, ub, ct, tau:n_:8], tau == 0, tau == 7, [RS_b, uT_b[ub]], [PB[pb]])
                    cp("act" if g4 % 2 == 0 else "dve", U_t[:, g4 * 4:g4 * 4 + 4, ch0:ch0 + nch],
                       PS[pb][:, 0:256].rearrange("p (q n) -> p q n", q=4)[:, :, 0:nch], [PB[pb]], [U_b])
            dbg_store("s5U", U_t[:, 0:2, :], [128, 2, 544], [U_b], BF16)
            ckpt("s5U")
            kb.release(m_u)
            for kt in range(8):
                kb.dma("sp", hT_spill[:, kt * NALL:kt * NALL + HSPLIT], hT_tiles["o"][:, kt, :], R=hT_b[0:17], W=[hT_spill_b])
                kb.dma("sp", hT_spill[:, kt * NALL + HSPLIT:(kt + 1) * NALL], hT_tiles["r"][:, kt, :], R=hT_b[17:34], W=[hT_spill_b])
            kb.release(mR0)

            P_t, P_b = kb.alloc("Pp", [128, 24, 64], F32)
            CS_t, CS_b = kb.alloc("CS", [128, 2, 64, 64], BF16)
            FX_t, FX_b = kb.alloc("FX", [128, 2, 64], F32)
            BB_t, BB_b = kb.alloc("BB", [128, 2, 64, 16], F32)
            CC_t, CC_b = kb.alloc("CC", [128, 2, 64, 16], BF16)
            TM_t, TM_b = kb.alloc("TM", [128, 4, 64, 8], F32)
            PSW_t, PSW_b = kb.alloc("PSW", [128, 128], BF16)
            MSK_t, MSK_b = kb.alloc("MSK", [128, 2, 128], F32)
            DC_t, DC_b = kb.alloc("DC", [128, 32], F32)
            IDF_t, IDF_b = kb.alloc("IDF", [128, 128], F32)
            PH_t, PH_b = kb.alloc("PH", [128, 8], F32)
            mS1 = kb.mark()

            memset("pool", PSW_t[:, :], 0.0, [PSW_b])
            asel(PSW_t[:, 0:64], PSW_t[:, 0:64], [[-1, 64]], ALU.not_equal, -1.0, -64, 1, [PSW_b], [PSW_b])
            asel(PSW_t[:, 64:128], PSW_t[:, 64:128], [[-1, 64]], ALU.not_equal, 1.0, 0, 1, [PSW_b], [PSW_b])
            memset("pool", MSK_t[:, :, :], 1.0, [MSK_b])
            mv = MSK_t[:, :, :].rearrange("p a (t c) -> p a t c", c=16)
            asel(mv[:, 0], mv[:, 0], [[16, 8], [0, 16]], ALU.is_ge, 0.0, 15, -1, [MSK_b], [MSK_b])
            asel(mv[:, 1], mv[:, 1], [[-16, 8], [0, 16]], ALU.is_ge, 0.0, 0, 1, [MSK_b], [MSK_b])
            cp("dve", IDF_t[:, :], ident_t[:, :], [ident_b], [IDF_b])
            for tau in range(8):
                kb.dma("sp", DC_t[tau * 16:(tau + 1) * 16, :], I["s5_d"].rearrange("(g c) -> c g", c=16), W=[DC_b],
                       allow_slow_non_contiguous=True)

            m_p = kb.mark()
            ARE, AIM, LDT, DT, RE, LR, TH, R1, AR, AI, NR, DEN, KR, KI, T0, T1, T2, T3, PHT, RHO1 = range(20)
            tmi_t, tmi_b = kb.alloc("tmi", [128, 1024], I32)
            tmf_t, tmf_b = kb.alloc("tmf", [128, 3, 1024], F32)
            tm64 = (tmf_t[:, 0, 0:64], tmf_t[:, 1, 0:64], tmf_t[:, 2, 0:64], tmi_t[:, 0:64], tmf_b)
            for half in range(2):
                for q4 in range(4):
                    sl = slice(q4 * 16, (q4 + 1) * 16)
                    kb.dma("sp", P_t[half * 64:(half + 1) * 64, ARE, sl], I["s5_a_re"].rearrange("d g p -> p (d g)")[:, sl], W=[P_b],
                           allow_slow_non_contiguous=True)
                    kb.dma("sp", P_t[half * 64:(half + 1) * 64, AIM, sl], I["s5_a_im"].rearrange("d g p -> p (d g)")[:, sl], W=[P_b],
                           allow_slow_non_contiguous=True)
            kb.dma("sp", P_t[:, LDT, :], I["s5_log_dt"].partition_broadcast(128), W=[P_b])
            act(P_t[:, DT, :], P_t[:, LDT, :], AF.Exp, [P_b], [P_b])
            ts("dve", P_t[:, RE, :], P_t[:, ARE, :], -1e-4, None, ALU.min, None, [P_b], [P_b])
            tt("dve", P_t[:, LR, :], P_t[:, RE, :], P_t[:, DT, :], ALU.mult, [P_b], [P_b])
            tt("dve", P_t[:, TH, :], P_t[:, AIM, :], P_t[:, DT, :], ALU.mult, [P_b], [P_b])
            ts("dve", P_t[:, TH, :], P_t[:, TH, :], 1.0 / TWO_PI, None, ALU.mult, None, [P_b], [P_b])
            act(P_t[:, R1, :], P_t[:, LR, :], AF.Exp, [P_b], [P_b])
            sin_turns(P_t[:, AI, :], P_t[:, TH, :], 1.0, 0.0, [P_b], [P_b], tm64)
            sin_turns(P_t[:, AR, :], P_t[:, TH, :], 1.0, 0.25, [P_b], [P_b], tm64)
            tt("dve", P_t[:, AI, :], P_t[:, AI, :], P_t[:, R1, :], ALU.mult, [P_b], [P_b])
            tt("dve", P_t[:, AR, :], P_t[:, AR, :], P_t[:, R1, :], ALU.mult, [P_b], [P_b])
            ts("dve", P_t[:, NR, :], P_t[:, AR, :], -1.0, None, ALU.add, None, [P_b], [P_b])
            tt("dve", P_t[:, T0, :], P_t[:, RE, :], P_t[:, RE, :], ALU.mult, [P_b], [P_b])
            tt("dve", P_t[:, T1, :], P_t[:, AIM, :], P_t[:, AIM, :], ALU.mult, [P_b], [P_b])
            tt("dve", P_t[:, DEN, :], P_t[:, T0, :], P_t[:, T1, :], ALU.add, [P_b], [P_b])
            recip(P_t[:, DEN, :], P_t[:, DEN, :], [P_b], [P_b])
            tt("dve", P_t[:, T0, :], P_t[:, NR, :], P_t[:, RE, :], ALU.mult, [P_b], [P_b])
            tt("dve", P_t[:, T1, :], P_t[:, AI, :], P_t[:, AIM, :], ALU.mult, [P_b], [P_b])
            tt("dve", P_t[:, T0, :], P_t[:, T0, :], P_t[:, T1, :], ALU.add, [P_b], [P_b])
            tt("dve", P_t[:, KR, :], P_t[:, T0, :], P_t[:, DEN, :], ALU.mult, [P_b], [P_b])
            tt("dve", P_t[:, T0, :], P_t[:, AI, :], P_t[:, RE, :], ALU.mult, [P_b], [P_b])
            tt("dve", P_t[:, T1, :], P_t[:, NR, :], P_t[:, AIM, :], ALU.mult, [P_b], [P_b])
            tt("dve", P_t[:, T0, :], P_t[:, T0, :], P_t[:, T1, :], ALU.subtract, [P_b], [P_b])
            tt("dve", P_t[:, KI, :], P_t[:, T0, :], P_t[:, DEN, :], ALU.mult, [P_b], [P_b])
            Braw_t, Braw_b = kb.alloc("Braw", [128, 2, 64, 16], F32)
            for half in range(2):
                for q4 in range(4):
                    sl = slice(q4 * 16, (q4 + 1) * 16)
                    kb.dma("sp", Braw_t[half * 64:(half + 1) * 64, 0, sl, :], I["s5_b_re"].rearrange("d g p c -> p (d g) c")[:, sl, :], W=[Braw_b])
                    kb.dma("sp", Braw_t[half * 64:(half + 1) * 64, 1, sl, :], I["s5_b_im"].rearrange("d g p c -> p (d g) c")[:, sl, :], W=[Braw_b])
            kr_b = P_t[:, KR, :, None].broadcast_to([128, 64, 16])
            ki_b = P_t[:, KI, :, None].broadcast_to([128, 64, 16])
            bt_t, bt_b = kb.alloc("btmp", [128, 64, 16], F32)
            tt("dve", BB_t[:, 0], Braw_t[:, 0], kr_b, ALU.mult, [Braw_b, P_b], [BB_b])
            tt("dve", bt_t[:, :, :], Braw_t[:, 1], ki_b, ALU.mult, [Braw_b, P_b], [bt_b])
            tt("dve", BB_t[:, 0], BB_t[:, 0], bt_t[:, :, :], ALU.subtract, [BB_b, bt_b], [BB_b])
            tt("dve", BB_t[:, 1], Braw_t[:, 1], kr_b, ALU.mult, [Braw_b, P_b], [BB_b])
            tt("dve", bt_t[:, :, :], Braw_t[:, 0], ki_b, ALU.mult, [Braw_b, P_b], [bt_b])
            tt("dve", BB_t[:, 1], BB_t[:, 1], bt_t[:, :, :], ALU.add, [BB_b, bt_b], [BB_b])
            cst_t, cst_b = kb.alloc("cstg", [128, 128], F32)
            csb_t, csb_b = kb.alloc("cstgb", [128, 128], BF16)
            for ri_, nm in enumerate(("s5_c_re", "s5_c_im")):
                src = I[nm].rearrange("d g c p -> (d g c) p")
                for t8 in range(8):
                    kb.dma("sp", cst_t[:, 0:64], src[t8 * 128:(t8 + 1) * 128, :], W=[cst_b])
                    kb.dma("sp", cst_t[:, 64:128], src[t8 * 128:(t8 + 1) * 128, :], W=[cst_b])
                    cp("dve", csb_t[:, :], cst_t[:, :], [cst_b], [csb_b])
                    pv = psbf(4)
                    tr(pv[:, 0:128], csb_t[:, :], ident_t[:, :], [csb_b, ident_b], [PB[4]])
                    cp("dve", CC_t[:, ri_, t8 * 8:(t8 + 1) * 8, :], pv[:, 0:128].rearrange("p (g c) -> p g c", c=16), [PB[4]], [CC_b])
            exi_t, exi_b = kb.alloc("exi", [128, 64, 16], I32)
            exf_t, exf_b = kb.alloc("exf", [128, 64, 16], F32)
            mg_t, mg_b = kb.alloc("mag", [128, 64, 16], F32)
            an_t, an_b = kb.alloc("angx", [128, 64, 16], F32)
            phases = [(0.25, 0.0), (0.5, 0.25), (0.0, 0.75), (0.25, 0.0), (0.25, 0.5), (0.5, 0.75), (0.5, 0.75), (0.75, 0.0)]
            for i_, (p0, p1) in enumerate(phases):
                memset("pool", PH_t[0:64, i_:i_ + 1], p0, [PH_b])
                memset("pool", PH_t[64:128, i_:i_ + 1], p1, [PH_b])
            lr_b = lambda n_: P_t[:, LR, :, None].broadcast_to([128, 64, n_])
            th_b = lambda n_: P_t[:, TH, :, None].broadcast_to([128, 64, n_])
            tmA = (tmf_t[:, 0, :].rearrange("p (g k) -> p g k", g=64), tmf_t[:, 1, :].rearrange("p (g k) -> p g k", g=64),
                   tmf_t[:, 2, :].rearrange("p (g k) -> p g k", g=64), tmi_t[:, :].rearrange("p (g k) -> p g k", g=64), tmf_b)
            tm8 = tuple(a[:, :, 0:8] for a in tmA[:4]) + (tmf_b,)
            iota(exi_t[:, 0:32, 0:8], [[0, 32], [-1, 8]], 7, 0, [exi_b])
            iota(exi_t[:, 32:64, 0:8], [[0, 32], [1, 8]], 0, 0, [exi_b])
            cp("dve", exf_t[:, :, 0:8], exi_t[:, :, 0:8], [exi_b], [exf_b])
            tt("dve", mg_t[:, :, 0:8], exf_t[:, :, 0:8], lr_b(8), ALU.mult, [exf_b, P_b], [mg_b])
            act(mg_t[:, :, 0:8], mg_t[:, :, 0:8], AF.Exp, [mg_b], [mg_b])
            tt("dve", an_t[:, :, 0:8], exf_t[:, :, 0:8], th_b(8), ALU.mult, [exf_b, P_b], [an_b])
            for i_ in range(4):
                sin_turns(TM_t[:, i_], an_t[:, :, 0:8], 1.0, PH_t[:, i_:i_ + 1], [an_b, PH_b], [TM_b], tm8)
                tt("dve", TM_t[:, i_], TM_t[:, i_], mg_t[:, :, 0:8], ALU.mult, [TM_b, mg_b], [TM_b])
            ts("dve", P_t[:, RHO1, :], P_t[:, LR, :], 8.0, None, ALU.mult, None, [P_b], [P_b])
            act(P_t[:, RHO1, :], P_t[:, RHO1, :], AF.Exp, [P_b], [P_b])
            ts("dve", P_t[:, PHT, :], P_t[:, TH, :], 8.0, None, ALU.mult, None, [P_b], [P_b])
            cp("dve", tmi_t[:, 0:64], P_t[:, PHT, :], [P_b], [tmf_b])
            cp("dve", tmf_t[:, 0, 0:64], tmi_t[:, 0:64], [tmf_b], [tmf_b])
            tt("dve", P_t[:, PHT, :], P_t[:, PHT, :], tmf_t[:, 0, 0:64], ALU.subtract, [P_b, tmf_b], [P_b])
            sin_turns(FX_t[:, 1, :], P_t[:, PHT, :], 64.0, 0.0, [P_b], [FX_b], tm64)
            sin_turns(FX_t[:, 0, :], P_t[:, PHT, :], 64.0, 0.25, [P_b], [FX_b], tm64)
            tt("dve", FX_t[:, 0, :], FX_t[:, 0, :], P_t[:, RHO1, :], ALU.mult, [FX_b, P_b], [FX_b])
            tt("dve", FX_t[:, 1, :], FX_t[:, 1, :], P_t[:, RHO1, :], ALU.mult, [FX_b, P_b], [FX_b])
            jf_t, jf_b = kb.alloc("jf", [128, 16, 64], F32)
            iota(tmi_t[:, :].rearrange("p (g j) -> p g j", g=16), [[0, 16], [1, 64]], 0, 0, [tmf_b])
            cp("dve", jf_t[:, :, :], tmi_t[:, :].rearrange("p (g j) -> p g j", g=16), [tmf_b], [jf_b])
            ja_t, ja_b = kb.alloc("ja", [128, 16, 64], F32)
            tmB = tuple(a.rearrange("p g k -> p (g k)").rearrange("p (g j) -> p g j", g=16) for a in tmA[:4]) + (tmf_b,)
            for q4 in range(4):
                sl = slice(q4 * 16, (q4 + 1) * 16)
                tt("dve", ja_t[:, :, :], jf_t[:, :, :], P_t[:, PHT, sl, None].broadcast_to([128, 16, 64]), ALU.mult, [jf_b, P_b], [ja_b])
                sin_turns(CS_t[:, 1, sl, :], ja_t[:, :, :], 1.0, 0.0, [ja_b], [CS_b], tmB)
                sin_turns(CS_t[:, 0, sl, :], ja_t[:, :, :], 1.0, 0.25, [ja_b], [CS_b], tmB)
            dbg_store("s5P", P_t[:, :, :], [128, 24, 64], [P_b])
            dbg_store("s5TM", TM_t[:, :, :, :], [128, 4, 64, 8], [TM_b])
            dbg_store("s5BB", BB_t[:, :, :, :], [128, 2, 64, 16], [BB_b])
            dbg_store("s5CC", CC_t[:, :, :, :], [128, 2, 64, 16], [CC_b], BF16)
            dbg_store("s5CS", CS_t[:, :, :, :], [128, 2, 64, 64], [CS_b], BF16)
            ckpt("s5prep")
            kb.release(m_p)

            WA_t, WA_b = kb.alloc("WA", [128, 5, 32, 64], BF16)
            WB_t, WB_b = kb.alloc("WB", [128, 9, 32, 64], BF16)
            memset("pool", WA_t[:, :, :, :], 0.0, [WA_b])
            memset("pool", WB_t[:, :, :, :], 0.0, [WB_b])
            m_e = kb.mark()
            mt_t, mt_b = kb.alloc("mtf", [128, 2, 3, 128], F32, nsub=2)
            mtb_t, mtb_b = kb.alloc("mtb", [128, 2, 2, 128], BF16, nsub=2)
            mx_t, mx_b = kb.alloc("mx", [128, 2, 2, 128], BF16, nsub=2)
            dm_t, dm_b = kb.alloc("dm", [128, 2, 2, 512], F32, nsub=2)

            def build_MT(gd, i2, which):
                ta = TM_t[:, 2 * which, gd, :, None].broadcast_to([128, 8, 16])
                tb_ = TM_t[:, 2 * which + 1, gd, :, None].broadcast_to([128, 8, 16])
                bre = BB_t[:, 0, gd, None, :].broadcast_to([128, 8, 16])
                bim = BB_t[:, 1, gd, None, :].broadcast_to([128, 8, 16])
                v = lambda k_: mt_t[:, i2, k_, :].rearrange("p (t c) -> p t c", c=16)
                tt("dve", v(0), ta, bre, ALU.mult, [TM_b, BB_b], [mt_b[i2]])
                tt("dve", v(1), tb_, bim, ALU.mult, [TM_b, BB_b], [mt_b[i2]])
                tt("dve", mtb_t[:, i2, which, :].rearrange("p (t c) -> p t c", c=16), v(0), v(1), ALU.add, [mt_b[i2]], [mtb_b[i2]])

            def e_stageA(it):
                g, dd = it // 2, it % 2
                gd = dd * 32 + g
                i2 = it % 2
                build_MT(gd, i2, 0)
                build_MT(gd, i2, 1)

            def e_stageA2(it):
                i2 = it % 2
                pv = psbf(4 + i2)
                tr(pv[:, 0:128], mtb_t[:, i2, 0, :], ident_t[:, :], [mtb_b[i2], ident_b], [PB[4 + i2]])
                tr(pv[:, 128:256], mtb_t[:, i2, 1, :], ident_t[:, :], [mtb_b[i2], ident_b], [PB[4 + i2]])
                cp("act", mx_t[:, i2, :, :], pv[:, 0:256].rearrange("p (w m) -> p w m", w=2), [PB[4 + i2]], [mx_b[i2]])

            def e_stageB(it):
                g, dd = it // 2, it % 2
                gd = dd * 32 + g
                i2 = it % 2
                cosv, sinv = CS_t[:, 0, gd, :], CS_t[:, 1, gd, :]
                if dd == 0:
                    pieces = [(512, 32, 0), (0, 256, 32)]
                else:
                    pieces = [(32, 512, 0), (0, 32, 0)]
                for pi2, (uc0, n_, pc0) in enumerate(pieces):
                    if dd == 0:
                        pe1, pe2 = 6, 7
                    else:
                        pe1, pe2 = (0, 1) if pi2 == 0 else (2, 3)
                    mm(PS[pe1][:, pc0:pc0 + n_], mx_t[:, i2, 0, :], U_t[:, g, uc0:uc0 + n_], True, True, [mx_b[i2], U_b], [PB[pe1]])
                    mm(PS[pe2][:, pc0:pc0 + n_], mx_t[:, i2, 1, :], U_t[:, g, uc0:uc0 + n_], True, True, [mx_b[i2], U_b], [PB[pe2]])
                    e1 = PS[pe1][:, pc0:pc0 + n_]
                    e2 = PS[pe2][:, pc0:pc0 + n_]
                    di = dd
                    d1 = dm_t[:, di, 0, 0:n_]
                    d2 = dm_t[:, di, 1, 0:n_]
                    if dd == 0 and pi2 == 0:
                        c_ap, s_ap = cosv[:, 32:64], sinv[:, 32:64]
                        o_ap = WA_t[:, 0, g, 32:64]
                        v3 = lambda a: a
                    elif dd == 0:
                        c_ap = CS_t[:, 0, gd, None, :].broadcast_to([128, 4, 64])
                        s_ap = CS_t[:, 1, gd, None, :].broadcast_to([128, 4, 64])
                        o_ap = WA_t[:, 1:5, g, :]
                        v3 = lambda a: a.rearrange("p (s j) -> p s j", j=64)
                    elif pi2 == 0:
                        c_ap = CS_t[:, 0, gd, None, ::-1].broadcast_to([128, 8, 64])
                        s_ap = CS_t[:, 1, gd, None, ::-1].broadcast_to([128, 8, 64])
                        o_ap = WB_t[:, 7::-1, g, ::-1]
                        v3 = lambda a: a.rearrange("p (s j) -> p s j", j=64)
                    else:
                        c_ap, s_ap = cosv[:, 31::-1], sinv[:, 31::-1]
                        o_ap = WB_t[:, 8, g, 31::-1]
                        v3 = lambda a: a
                    wb_ = WA_b if dd == 0 else WB_b
                    tt("dve", v3(d1), v3(e1), c_ap, ALU.mult, [PB[pe1], CS_b], [dm_b[di]])
                    tt("dve", v3(d2), v3(e2), s_ap, ALU.mult, [PB[pe2], CS_b], [dm_b[di]])
                    tt("pool", o_ap, v3(d1), v3(d2), ALU.add, [dm_b[di]], [wb_])

            e_stageA(0)
            e_stageA2(0)
            for it in range(64):
                if it + 1 < 64:
                    e_stageA(it + 1)
                e_stageB(it)
                if it + 1 < 64:
                    e_stageA2(it + 1)
            kb.release(m_e)
            dbg_store("s5Wpre", WB_t[:, :, 0:2, :], [128, 9, 2, 64], [WB_b], BF16)
            ckpt("s5E")

            m_s = kb.mark()
            RHO_t, RHO_b = kb.alloc("RHO", [128, 64, 64], F32)
            cp("dve", RHO_t[:, :, :], P_t[:, RHO1, :, None].broadcast_to([128, 64, 64]), [P_b], [RHO_b])
            memset("pool", RHO_t[:, :, 0:1], 0.0, [RHO_b])
            fx_t, fx_b = kb.alloc("fxt", [128, 2, 32], F32)
            for dd, (W_t, W_b, nseg) in enumerate(((WA_t, WA_b, 5), (WB_t, WB_b, 9))):
                for sg in range(nseg):
                    if sg > 0:
                        wend = W_t[:, sg - 1, :, 63]
                        mm(PS[6][:, 0:32], PSW_t[:, :], wend, True, True, [PSW_b, W_b], [PB[6]])
                        tt("dve", fx_t[:, 0, :], wend, FX_t[:, 0, dd * 32:(dd + 1) * 32], ALU.mult, [W_b, FX_b], [fx_b])
                        tt("dve", fx_t[:, 1, :], PS[6][:, 0:32], FX_t[:, 1, dd * 32:(dd + 1) * 32], ALU.mult, [PB[6], FX_b], [fx_b])
                        tt("dve", fx_t[:, 0, :], fx_t[:, 0, :], fx_t[:, 1, :], ALU.add, [fx_b], [fx_b])
                        tt("dve", W_t[:, sg, :, 0], W_t[:, sg, :, 0], fx_t[:, 0, :], ALU.add, [W_b, fx_b], [W_b])
                    wv = W_t[:, sg, :, :].rearrange("p g j -> p (g j)")
                    rv = RHO_t[:, dd * 32:(dd + 1) * 32, :].rearrange("p g j -> p (g j)")
                    kb.op("dve", (lambda wv_, rv_: (lambda e: e.tensor_tensor_scan(wv_, rv_, wv_, 0.0, ALU.mult, ALU.add)))(wv, rv),
                          [W_b, RHO_b], [W_b])
            kb.release(m_s)
            dbg_store("s5W", WB_t[:, :, 0:2, :], [128, 9, 2, 64], [WB_b], BF16)
            dbg_store("s5WA", WA_t[:, :, 0:2, :], [128, 5, 2, 64], [WA_b], BF16)
            ckpt("s5scan")

            m_y = kb.mark()
            TC_t, TC_b = kb.alloc("TC", [128, 3, 64, 16], F32)
            m_tc = kb.mark()
            tmi_t, tmi_b = kb.alloc("tmi2", [128, 32, 16], I32)
            tmf_t, tmf_b = kb.alloc("tmf2", [128, 3, 32, 16], F32)
            exi_t, exi_b = kb.alloc("exi2", [128, 32, 16], I32)
            exf_t, exf_b = kb.alloc("exf2", [128, 32, 16], F32)
            mg_t, mg_b = kb.alloc("mag2", [128, 32, 16], F32)
            an_t, an_b = kb.alloc("angx2", [128, 32, 16], F32)
            tmH = (tmf_t[:, 0], tmf_t[:, 1], tmf_t[:, 2], tmi_t[:, :, :], tmf_b)
            for dd in range(2):
                gs = slice(dd * 32, (dd + 1) * 32)
                if dd == 0:
                    iota(exi_t[:, :, :], [[0, 32], [1, 16]], -7, 0, [exi_b])
                else:
                    iota(exi_t[:, :, 0:8], [[0, 32], [-1, 8]], 0, 0, [exi_b])
                    iota(exi_t[:, :, 8:16], [[0, 32], [-1, 8]], 8, 0, [exi_b])
                cp("dve", exf_t[:, :, :], exi_t[:, :, :], [exi_b], [exf_b])
                tt("dve", mg_t[:, :, :], exf_t[:, :, :], P_t[:, LR, gs, None].broadcast_to([128, 32, 16]), ALU.mult, [exf_b, P_b], [mg_b])
                act(mg_t[:, :, :], mg_t[:, :, :], AF.Exp, [mg_b], [mg_b])
                tt("dve", an_t[:, :, :], exf_t[:, :, :], P_t[:, TH, gs, None].broadcast_to([128, 32, 16]), ALU.mult, [exf_b, P_b], [an_b])
                for i_, phi_ in enumerate((4, 5, 7)):
                    sin_turns(TC_t[:, i_, gs], an_t[:, :, :], 1.0, PH_t[:, phi_:phi_ + 1], [an_b, PH_b], [TC_b], tmH)
                    tt("dve", TC_t[:, i_, gs], TC_t[:, i_, gs], mg_t[:, :, :], ALU.mult, [TC_b, mg_b], [TC_b])
            dbg_store("s5TC", TC_t[:, :, :, :], [128, 3, 64, 16], [TC_b])
            kb.release(m_tc)
            Y_t, Y_b = kb.alloc("Ysb", [128, 32, 257], BF16, top=True)
            cm_t, cm_b = kb.alloc("cmf", [128, 2, 128], F32)
            cpw_t, cpw_b = kb.alloc("cpw", [128, 2, 6, 128], BF16, nsub=2)
            tp_t, tp_b = kb.alloc("toep", [128, 2, 128], BF16, nsub=2)
            tf_t, tf_b = kb.alloc("toepf", [128, 2, 128], F32)
            rm_t, rm_b = kb.alloc("rm", [128, 2, 4, 257], BF16, nsub=2)
            mt1_t, mt1_b = kb.alloc("mt1k", [128, 2, 2, 128], BF16, nsub=2)
            mtf2_t, mtf2_b = kb.alloc("mtf2", [128, 2, 128], F32)

            def build_C(gd, dst_ap, ta_i, k0, i2):
                ta = TC_t[:, ta_i, gd, k0:k0 + 8, None].broadcast_to([128, 8, 16])
                tb_ = TC_t[:, ta_i + 1, gd, k0:k0 + 8, None].broadcast_to([128, 8, 16])
                cre = CC_t[:, 0, gd, None, :].broadcast_to([128, 8, 16])
                cim = CC_t[:, 1, gd, None, :].broadcast_to([128, 8, 16])
                v = lambda k_: cm_t[:, k_, :].rearrange("p (t c) -> p t c", c=16)
                tt("dve", v(0), ta, cre, ALU.mult, [TC_b, CC_b], [cm_b])
                tt("dve", v(1), tb_, cim, ALU.mult, [TC_b, CC_b], [cm_b])
                tt("pool", dst_ap.rearrange("p (t c) -> p t c", c=16), v(0), v(1), ALU.add, [cm_b], [cpw_b[i2]])

            def y_stageA(g):
                i2 = g % 2
                for dd in range(2):
                    gd = dd * 32 + g
                    build_C(gd, cpw_t[:, i2, 3 * dd + 0, :], 0, 8, i2)
                    build_C(gd, cpw_t[:, i2, 3 * dd + 1, :], 1, 8, i2)
                    build_C(gd, cpw_t[:, i2, 3 * dd + 2, :], 0, 0, i2)
                    ta = TM_t[:, 0, gd, :, None].broadcast_to([128, 8, 16])
                    tb_ = TM_t[:, 1, gd, :, None].broadcast_to([128, 8, 16])
                    bre = BB_t[:, 0, gd, None, :].broadcast_to([128, 8, 16])
                    bim = BB_t[:, 1, gd, None, :].broadcast_to([128, 8, 16])
                    v = lambda k_: mtf2_t[:, k_, :].rearrange("p (t c) -> p t c", c=16)
                    tt("dve", v(0), ta, bre, ALU.mult, [TM_b, BB_b], [mtf2_b])
                    tt("dve", v(1), tb_, bim, ALU.mult, [TM_b, BB_b], [mtf2_b])
                    tt("pool", mt1_t[:, i2, dd, :].rearrange("p (t c) -> p t c", c=16), v(0), v(1), ALU.add, [mtf2_b], [mt1_b[i2]])

            def y_stageA2(g):
                i2 = g % 2
                for dd in range(2):
                    mm(PS[4 + dd][:, 0:128], mt1_t[:, i2, dd, :], cpw_t[:, i2, 3 * dd + 2, :], True, True, [mt1_b[i2], cpw_b[i2]], [PB[4 + dd]])
                tt("dve", tf_t[:, 0, :], PS[4][:, 0:128], MSK_t[:, 0, :], ALU.mult, [PB[4], MSK_b], [tf_b])
                tt("dve", tf_t[:, 1, :], PS[5][:, 0:128], MSK_t[:, 1, :], ALU.mult, [PB[5], MSK_b], [tf_b])
                tt("pool", tf_t[:, 0, :], tf_t[:, 0, :], tf_t[:, 1, :], ALU.add, [tf_b], [tf_b])
                stt(tp_t[:, i2, :], IDF_t[:, :], DC_t[:, g:g + 1], tf_t[:, 0, :], ALU.mult, ALU.add, [IDF_b, DC_b, tf_b], [tp_b[i2]])

            def y_stageB(g):
                i2 = g % 2
                R_ = [WA_b, WB_b, CS_b]
                for ci_ in range(2):
                    ca, cb = CS_t[:, ci_, g, :], CS_t[:, ci_, 32 + g, :]
                    eng = "dve" if ci_ == 0 else "pool"
                    tt(eng, rm_t[:, i2, ci_, 0:1], WA_t[:, 0, g, 63:64], ca[:, 63:64], ALU.mult, R_, [rm_b[i2]])
                    tt(eng, rm_t[:, i2, ci_, 1:257].rearrange("p (s j) -> p s j", j=64), WA_t[:, 1:5, g, :],
                       CS_t[:, ci_, g, None, :].broadcast_to([128, 4, 64]), ALU.mult, R_, [rm_b[i2]])
                    tt(eng, rm_t[:, i2, 2 + ci_, 0:31], WB_t[:, 8, g, 30::-1], cb[:, 30::-1], ALU.mult, R_, [rm_b[i2]])
                    tt(eng, rm_t[:, i2, 2 + ci_, 31:223].rearrange("p (s j) -> p s j", j=64), WB_t[:, 7:4:-1, g, ::-1],
                       CS_t[:, ci_, 32 + g, None, ::-1].broadcast_to([128, 3, 64]), ALU.mult, R_, [rm_b[i2]])
                    tt(eng, rm_t[:, i2, 2 + ci_, 223:257], WB_t[:, 4, g, 63:29:-1], cb[:, 63:29:-1], ALU.mult, R_, [rm_b[i2]])
                pb = 6 + i2
                mm(PS[pb][:, 0:257], tp_t[:, i2, :], U_t[:, g, 0:257], True, False, [tp_b[i2], U_b], [PB[pb]])
                mm(PS[pb][:, 0:257], cpw_t[:, i2, 0, :], rm_t[:, i2, 0, :], False, False, [cpw_b[i2], rm_b[i2]], [PB[pb]])
                mm(PS[pb][:, 0:257], cpw_t[:, i2, 1, :], rm_t[:, i2, 1, :], False, False, [cpw_b[i2], rm_b[i2]], [PB[pb]])
                mm(PS[pb][:, 0:257], cpw_t[:, i2, 3, :], rm_t[:, i2, 2, :], False, False, [cpw_b[i2], rm_b[i2]], [PB[pb]])
                mm(PS[pb][:, 0:257], cpw_t[:, i2, 4, :], rm_t[:, i2, 3, :], False, True, [cpw_b[i2], rm_b[i2]], [PB[pb]])
                cp("act", Y_t[:, g, :], PS[pb][:, 0:257], [PB[pb]], [Y_b])

            y_stageA(0)
            y_stageA2(0)
            for g in range(32):
                if g + 1 < 32:
                    y_stageA(g + 1)
                y_stageB(g)
                if g + 1 < 32:
                    y_stageA2(g + 1)
            dbg_store("s5Y", Y_t[:, 0:2, :], [128, 2, 257], [Y_b], BF16)
            ckpt("s5Y")

            kb.release(mR0)
            hp_t, hp_b = kb.alloc("hpre", [128, 4, 2056], BF16)
            gw_t, gw_b = kb.alloc("gluw", [128, 4, 512], BF16)
            wload(gw_t[:, :, :], I["glu_w"].rearrange("(kt p) n -> p kt n", p=128), 4, 512, gw_b)
            gx_t, gx_b = kb.alloc("gx", [128, 2, 4, 260], F32, nsub=2)
            gi = 0
            for ct in range(4):
                for t_ in range(8):
                    i2 = gi % 2
                    gi += 1
                    pb = 4 + i2
                    for gl in range(8):
                        mm(PS[pb][:, 0:257], RS_t[:, t_, 112 - 16 * gl:240 - 16 * gl], Y_t[:, ct * 8 + gl, :], gl == 0, gl == 7,
                           [RS_b, Y_b], [PB[pb]])
                    x_ = gx_t[:, i2, 0, 0:257]
                    cp("act", x_, PS[pb][:, 0:257], [PB[pb]], [gx_b[i2]])
                    tt("pool", gx_t[:, i2, 1, 0:257], x_, x_, ALU.mult, [gx_b[i2]], [gx_b[i2]])
                    ts("dve", gx_t[:, i2, 1, 0:257], gx_t[:, i2, 1, 0:257], 0.044715, 1.0, ALU.mult, ALU.add, [gx_b[i2]], [gx_b[i2]])
                    tt("pool", gx_t[:, i2, 1, 0:257], gx_t[:, i2, 1, 0:257], x_, ALU.mult, [gx_b[i2]], [gx_b[i2]])
                    act(gx_t[:, i2, 2, 0:257], gx_t[:, i2, 1, 0:257], AF.Sigmoid, [gx_b[i2]], [gx_b[i2]], scale=2.0 * math.sqrt(2.0 / math.pi))
                    tt("dve", hp_t[:, ct, t_:2056:8], x_, gx_t[:, i2, 2, 0:257], ALU.mult, [gx_b[i2]], [hp_b])
            hsT_t, hsT_b = kb.alloc("hsTs", [128, 4, NQ + 7], BF16)
            sgl_t, sgl_b = kb.alloc("sgl", [128, 2, 512], F32, nsub=2)
            gi = 0
            for (q0, nq) in ((0, 512), (512, 512), (1024, 512), (1536, 512), (2048, 8)):
                for c2 in range(4):
                    i2 = gi % 2
                    gi += 1
                    pb = 6 + i2
                    for ct in range(4):
                        mm(PS[pb][:, 0:nq], gw_t[:, ct, c2 * 128:(c2 + 1) * 128], hp_t[:, ct, q0:q0 + nq], ct == 0, ct == 3, [gw_b, hp_b], [PB[pb]])
                    act(sgl_t[:, i2, 0:nq], PS[pb][:, 0:nq], AF.Sigmoid, [PB[pb], V1_b], [sgl_b[i2]], bias=V1_t[:, GLUB + c2:GLUB + c2 + 1])
                    tt("dve", hsT_t[:, c2, q0:q0 + nq], hp_t[:, c2, q0:q0 + nq], sgl_t[:, i2, 0:nq], ALU.mult, [hp_b, sgl_b[i2]], [hsT_b])
            dbg_store("s5hs", hsT_t[:, :, 0:NQ], [128, 4, NQ], [hsT_b], BF16)
            kb.dma("sp", hsT_spill.rearrange("p (k n) -> p k n", k=4), hsT_t[:, :, :], R=[hsT_b], W=[hsT_spill_b])
            ckpt("s5end")
            kb.release(mR0)
            kb.release_top(SB_LIMIT)
            alloc_oT()
            alloc_hT()
            for kt in range(8):
                kb.dma("sp", hT_tiles["o"][:, kt, :], hT_spill[:, kt * NALL:kt * NALL + HSPLIT], R=[hT_spill_b], W=hT_b[0:17])
                kb.dma("sp", hT_tiles["r"][:, kt, :], hT_spill[:, kt * NALL + HSPLIT:(kt + 1) * NALL], R=[hT_spill_b], W=hT_b[17:34])

        if S5_ON:
            s5_stage()
        oT_t, oT_b = oT_box["t"], oT_box["b"]


        mA = kb.mark()
        lv_t, lv_b = kb.alloc("lv", [128, 4, 64], F32)
        kb.dma("sp", lv_t[:, :, :], I["lamv"].rearrange("a b -> (a b)").partition_broadcast(128).rearrange("p (a b) -> p a b", a=4), W=[lv_b])
        sm_t, sm_b = kb.alloc("sm", [128, 16], F32)
        tt("dve", lv_t[:, 0, :], lv_t[:, 0, :], lv_t[:, 1, :], ALU.mult, [lv_b], [lv_b])
        tt("dve", lv_t[:, 2, :], lv_t[:, 2, :], lv_t[:, 3, :], ALU.mult, [lv_b], [lv_b])
        kb.op("dve", lambda e: e.reduce_sum(sm_t[:, 0:1], lv_t[:, 0, :], AX.X), [lv_b], [sm_b])
        kb.op("dve", lambda e: e.reduce_sum(sm_t[:, 1:2], lv_t[:, 2, :], AX.X), [lv_b], [sm_b])
        act(sm_t[:, 2:4], sm_t[:, 0:2], AF.Exp, [sm_b], [sm_b])
        tt("dve", sm_t[:, 4:5], sm_t[:, 3:4], sm_t[:, 2:3], ALU.subtract, [sm_b], [sm_b])
        ts("dve", sm_t[:, 5:6], sm_t[:, 4:5], -0.2, None, ALU.add, None, [sm_b], [sm_b])
        NEGLAM = sm_t[:, 5:6]
        qk_t, qk_b = kb.alloc("qkw", [128, 2, 64], F32)
        kb.dma("sp", qk_t[:, 0, :], I["q_norm_w"].partition_broadcast(128), W=[qk_b])
        kb.dma("sp", qk_t[:, 1, :], I["k_norm_w"].partition_broadcast(128), W=[qk_b])
        kb.op("dve", lambda e: e.reduce_max(sm_t[:, 6:7], qk_t[:, 0, :], AX.X, apply_absolute_value=True), [qk_b], [sm_b])
        kb.op("dve", lambda e: e.reduce_max(sm_t[:, 7:8], qk_t[:, 1, :], AX.X, apply_absolute_value=True), [qk_b], [sm_b])
        tt("dve", sm_t[:, 8:9], sm_t[:, 6:7], sm_t[:, 7:8], ALU.mult, [sm_b], [sm_b])
        ts("dve", sm_t[:, 9:10], sm_t[:, 8:9], -8.0, None, ALU.mult, None, [sm_b], [sm_b])
        NBIAS = sm_t[:, 9:10]
        memset("pool", sm_t[:, 10:11], -0.5, [sm_b])
        MHALF = sm_t[:, 10:11]
        wcol_t, wcol_b = kb.alloc("wcol", [128, 2], F32)
        for half in range(2):
            kb.dma("sp", wcol_t[half * 64:(half + 1) * 64, 0:1], I["q_norm_w"].rearrange("(a b) -> a b", b=1), W=[wcol_b])
            kb.dma("sp", wcol_t[half * 64:(half + 1) * 64, 1:2], I["k_norm_w"].rearrange("(a b) -> a b", b=1), W=[wcol_b])
        sw_t, sw_b = kb.alloc("swbc", [128, 128], F32)
        kb.dma("sp", sw_t[:, :], I["subln_w"].partition_broadcast(128), W=[sw_b])
        ts("dve", sw_t[:, :], sw_t[:, :], 0.8, None, ALU.mult, None, [sw_b], [sw_b])
        bones_t, bones_b = kb.alloc("bones", [128, 128], BF16)
        memset("pool", bones_t[:, :], 0.0, [bones_b])
        memset("pool", bones_t[0:64, 0:64], 1.0, [bones_b])
        memset("pool", bones_t[64:128, 64:128], 1.0, [bones_b])
        prot_t, prot_b = kb.alloc("prot", [128, 128], BF16)
        memset("pool", prot_t[:, :], 0.0, [prot_b])
        prot_v = prot_t[:, :].rearrange("p (b h i) -> p b h i", b=4, h=2)
        asel(prot_v[:, :, 0, :], prot_v[:, :, 0, :], [[-32, 4], [-1, 16]], ALU.not_equal, -1.0, -16, 1, [prot_b], [prot_b])
        asel(prot_v[:, :, 1, :], prot_v[:, :, 1, :], [[-32, 4], [-1, 16]], ALU.not_equal, 1.0, 0, 1, [prot_b], [prot_b])

        def sin_turns(out_ap, x_ap, mul, add, n, R, W, tmps):
            (t_ap, f_ap, g_ap, k_ap, tb) = tmps
            ts("dve", t_ap, x_ap, mul, add, ALU.mult, ALU.add, R, [tb])
            cp("dve", k_ap, t_ap, [tb], [tb])
            cp("dve", f_ap, k_ap, [tb], [tb])
            tt("dve", f_ap, t_ap, f_ap, ALU.subtract, [tb], [tb])
            ts("dve", g_ap, f_ap, 0.5, None, ALU.is_gt, None, [tb], [tb])
            tt("dve", f_ap, f_ap, g_ap, ALU.subtract, [tb], [tb])
            ts("dve", g_ap, f_ap, -0.5, None, ALU.is_lt, None, [tb], [tb])
            tt("dve", f_ap, f_ap, g_ap, ALU.add, [tb], [tb])
            act(out_ap, f_ap, AF.Sin, [tb], W, scale=6.2831)

        cos_t, cos_b = kb.alloc("ropecos", [128, NLAT], BF16)
        sin_t, sin_b = kb.alloc("ropesin", [128, NLAT], BF16)
        mR = kb.mark()
        pi_t, pi_b = kb.alloc("posinfo", [128, 2], F32)
        kb.dma("sp", pi_t[:, :], I["posinfo"].partition_broadcast(128), W=[pi_b])
        pidx_t, pidx_b = kb.alloc("pidx", [128, 4], I32)
        pf_t, pf_b = kb.alloc("pf", [128, 12], F32)
        iota(pidx_t[:, 0:1], [[0, 1]], 0, 1, [pidx_b])
        cp("dve", pf_t[:, 5:6], pidx_t[:, 0:1], [pidx_b], [pf_b])
        ts("dve", pf_t[:, 6:7], pf_t[:, 5:6], 64.0, None, ALU.is_ge, None, [pf_b], [pf_b])
        stt(pf_t[:, 7:8], pf_t[:, 6:7], -64.0, pf_t[:, 5:6], ALU.mult, ALU.add, [pf_b], [pf_b])
        ts("dve", pf_t[:, 1:2], pf_t[:, 7:8], 32.0, None, ALU.is_ge, None, [pf_b], [pf_b])
        stt(pf_t[:, 8:9], pf_t[:, 1:2], -32.0, pf_t[:, 7:8], ALU.mult, ALU.add, [pf_b], [pf_b])
        ts("dve", pf_t[:, 9:10], pf_t[:, 8:9], 16.0, None, ALU.is_ge, None, [pf_b], [pf_b])
        stt(pf_t[:, 0:1], pf_t[:, 9:10], -16.0, pf_t[:, 8:9], ALU.mult, ALU.add, [pf_b], [pf_b])
        act(pf_t[:, 2:3], pf_t[:, 0:1], AF.Exp, [pf_b], [pf_b], scale=-math.log(10000.0) / 16.0)
        tt("dve", pf_t[:, 4:5], pf_t[:, 2:3], pf_t[:, 1:2], ALU.mult, [pf_b], [pf_b])
        tt("dve", pf_t[:, 3:4], pf_t[:, 2:3], pf_t[:, 4:5], ALU.subtract, [pf_b], [pf_b])
        ts("dve", pf_t[:, 3:5], pf_t[:, 3:5], 1.0 / TWO_PI, None, ALU.mult, None, [pf_b], [pf_b])
        RC = 1024
        ri_t, ri_b = kb.alloc("ri", [128, RC], I32)
        rf_t, rf_b = kb.alloc("rf", [128, RC], F32)
        cf_t, cf_b = kb.alloc("cf", [128, RC], F32)
        ang_t, ang_b = kb.alloc("ang", [128, RC], F32)
        tA_t, tA_b = kb.alloc("tA", [128, RC], F32)
        tB_t, _ = kb.alloc("tB", [128, RC], F32)
        tC_t, _ = kb.alloc("tC", [128, RC], F32)
        for rc in range(NLAT // RC):
            iota(ri_t[:, :].rearrange("p (a b) -> p a b", a=RC // 64), [[1, RC // 64], [0, 64]], rc * (RC // 64), 0, [ri_b])
            cp("dve", rf_t[:, :], ri_t[:, :], [ri_b], [rf_b])
            iota(ri_t[:, :].rearrange("p (a b) -> p a b", a=RC // 64), [[0, RC // 64], [1, 64]], 0, 0, [ri_b])
            cp("dve", cf_t[:, :], ri_t[:, :], [ri_b], [cf_b])
            ts("dve", rf_t[:, :], rf_t[:, :], pi_t[:, 0:1], pi_t[:, 1:2], ALU.mult, ALU.add, [rf_b, pi_b], [rf_b])
            ts("dve", cf_t[:, :], cf_t[:, :], pi_t[:, 0:1], pi_t[:, 1:2], ALU.mult, ALU.add, [cf_b, pi_b], [cf_b])
            ts("dve", ang_t[:, :], rf_t[:, :], pf_t[:, 3:4], None, ALU.mult, None, [rf_b, pf_b], [ang_b])
            stt(ang_t[:, :], cf_t[:, :], pf_t[:, 4:5], ang_t[:, :], ALU.mult, ALU.add, [cf_b, pf_b, ang_b], [ang_b])
            tmps = (tA_t[:, :], tB_t[:, :], tC_t[:, :], ri_t[:, :], tA_b)
            sin_turns(sin_t[:, rc * RC:(rc + 1) * RC], ang_t[:, :], 1.0, 0.0, RC, [ang_b], [sin_b, ri_b], tmps)
            sin_turns(cos_t[:, rc * RC:(rc + 1) * RC], ang_t[:, :], 1.0, 0.25, RC, [ang_b], [cos_b, ri_b], tmps)
        kb.release(mR)
        dbg_store("rope", cos_t[:, 0:256], [128, 256], [cos_b], BF16)
        dbg_store("ropes", sin_t[:, 0:256], [128, 256], [sin_b], BF16)
        ckpt("rope")

        wqkv_t, wqkv_b = kb.alloc("wqkv", [128, 2, 8, 384], BF16, nsub=2)
        KT_t, KT_b = kb.alloc("KT", [128, NALL], BF16)
        QT_t, QT_b = kb.alloc("QT", [128, NQ + 7], BF16)
        V_t, V_b = kb.alloc("Vaug", [128, 34, 129], BF16)
        memset("pool", V_t[:, :, :], 1.0, [V_b])
        sq_t, sq_b = kb.alloc("sq", [128, 2, 512], BF16, nsub=2)
        kw_t, kw_b = kb.alloc("kw", [128, 2, 512], BF16, nsub=2)
        u_t, u_b = kb.alloc("uu", [128, 2, 512], F32, nsub=2)
        t1_t, t1_b = kb.alloc("t1", [128, 2, 512], F32, nsub=2)
        t2_t, t2_b = kb.alloc("t2", [128, 2, 512], F32, nsub=2)
        PT_t, PT_b = kb.alloc("PT", [128, 4, 512], BF16, nsub=4)
        o_t, o_b = kb.alloc("oacc", [128, 2, 128], F32, nsub=2)
        on_t, on_b = kb.alloc("onrm", [128, 2, 128], BF16, nsub=2)
        fs_t, fs_b = kb.alloc("fstat", [128, 2, 8], F32, nsub=2)
        ojunk_t, ojunk_b = kb.alloc("ojunk", [128, 128], F32)
        blk_i = [0]

        def load_head_w(hh):
            bi = hh % 2
            for part in range(3):
                c0 = part * 1024 + hh * 128
                wload(wqkv_t[:, bi, :, part * 128:(part + 1) * 128], w_in_v[:, :, c0:c0 + 128], 8, 128, wqkv_b[bi])

        BSETS = ((5, 6, 7), (0, 1, 2))

        def qk_P1(hh, blk, spec):
            (dst_t, dst_b, dcol, scol, n, wsel, rope, ropecol) = spec
            bi = hh % 2
            pa = BSETS[blk % 2][0]
            hb = hT_bufs(scol, n)
            for kt in range(8):
                mm(PS[pa][:, 0:n], wqkv_t[:, bi, kt, wsel * 128:(wsel + 1) * 128], hT(kt, scol, n),
                   kt == 0, kt == 7, [wqkv_b[bi]] + hb, [PB[pa]])

        def qk_P2(hh, blk, spec):
            (dst_t, dst_b, dcol, scol, n, wsel, rope, ropecol) = spec
            pa, pb_, pc = BSETS[blk % 2]
            i2 = blk % 2
            act(sq_t[:, i2, 0:n], PS[pa][:, 0:n], AF.Square, [PB[pa]], [sq_b[i2]])
            act(kw_t[:, i2, 0:n], PS[pa][:, 0:n], AF.Copy, [PB[pa], wcol_b], [kw_b[i2]], scale=wcol_t[:, wsel:wsel + 1])
            mm(PS[pb_][:, 0:n], bones_t[:, :], sq_t[:, i2, 0:n], True, True, [bones_b, sq_b[i2]], [PB[pb_]])
            if rope:
                mm(PS[pc][:, 0:n], prot_t[:, :], kw_t[:, i2, 0:n], True, True, [prot_b, kw_b[i2]], [PB[pc]])
            act(u_t[:, i2, 0:n], PS[pb_][:, 0:n], AF.Ln, [PB[pb_], EPS_b], [u_b[i2]], bias=EPS_t[:, 0:1], scale=1.0 / 64.0)
            act(u_t[:, i2, 0:n], u_t[:, i2, 0:n], AF.Exp, [u_b[i2]], [u_b[i2]], scale=-0.5)
            if rope:
                tt("pool", t1_t[:, i2, 0:n], kw_t[:, i2, 0:n], cos_t[:, ropecol:ropecol + n], ALU.mult, [kw_b[i2], cos_b], [t1_b[i2]])
                tt("dve", t2_t[:, i2, 0:n], PS[pc][:, 0:n], sin_t[:, ropecol:ropecol + n], ALU.mult, [PB[pc], sin_b], [t2_b[i2]])
                tt("dve", t2_t[:, i2, 0:n], t2_t[:, i2, 0:n], t1_t[:, i2, 0:n], ALU.add, [t1_b[i2], t2_b[i2]], [t2_b[i2]])
                tt("dve", dst_t[:, dcol:dcol + n], t2_t[:, i2, 0:n], u_t[:, i2, 0:n], ALU.mult, [t2_b[i2], u_b[i2]], [dst_b])
            else:
                tt("dve", dst_t[:, dcol:dcol + n], kw_t[:, i2, 0:n], u_t[:, i2, 0:n], ALU.mult, [kw_b[i2], u_b[i2]], [dst_b])

        def qk_all(hh):
            specs = []
            for (c0_, n_) in ((0, 512), (512, 512), (1024, 512), (1536, 512), (2048, 128), (2176, 512), (2688, 512), (3200, 512), (3712, 384)):
                specs.append((KT_t, KT_b, c0_, c0_, n_, 1, True, c0_))
            specs.append((KT_t, KT_b, 4096, 4096, 256, 1, False, 0))
            for tb in range(4):
                specs.append((QT_t, QT_b, tb * 512, tb * 512, 512, 0, True, tb * 512))
            specs.append((QT_t, QT_b, 2048, 2048, 1, 0, True, 2048))
            qk_P1(hh, 0, specs[0])
            for i_, sp in enumerate(specs):
                if i_ + 1 < len(specs):
                    qk_P1(hh, i_ + 1, specs[i_ + 1])
                qk_P2(hh, i_, sp)
                if i_ < 9:
                    v_group(hh, i_)

        def v_group(hh, g4):
            bi = hh % 2
            nt = 4 if g4 < 8 else 2
            pb = 3 + (g4 % 2)
            for j in range(nt):
                kti = g4 * 4 + j
                for kt in range(8):
                    mm(PS[pb][:, j * 128:(j + 1) * 128], hT(kt, kti * 128, 128),
                       wqkv_t[:, bi, kt, 256:384], kt == 0, kt == 7, [wqkv_b[bi], hT_b[kti]], [PB[pb]])
            eng = "act" if g4 % 2 == 0 else "dve"
            cp(eng, V_t[:, g4 * 4:g4 * 4 + nt, 0:128], PS[pb][:, 0:nt * 128].rearrange("p (j e) -> p j e", e=128),
               [PB[pb]], [V_b])

        def acc_ap(m, j, rows):
            if j < 3:
                return PS[2 + m][0:rows, j * 129:(j + 1) * 129], PB[2 + m]
            return PS[4][0:rows, m * 129:(m + 1) * 129], PB[4]

        pt_i = [0]
        fin_i = [0]

        accS_t, accS_b = kb.alloc("accS", [128, 2, 8, 129], F32, nsub=2)

        def attn_head(hh):
            qblocks = [(0, 512), (512, 512), (1024, 512), (1536, 512), (2048, 1)]
            pending = []
            for qbi, (q0, nq) in enumerate(qblocks):
                nj = (nq + 127) // 128
                SB = (0, 1, 5, 6)

                def st_mm(kk, m, q0=q0, nq=nq):
                    sbk = SB[(kk % 2) * 2 + m]
                    mm(PS[sbk][:, 0:nq], KT_t[m * 64:(m + 1) * 64, kk * 128:(kk + 1) * 128],
                       QT_t[m * 64:(m + 1) * 64, q0:q0 + nq], True, True, [KT_b, QT_b], [PB[sbk]])

                st_mm(0, 0)
                st_mm(0, 1)
                for kk in range(34):
                    if kk + 1 < 34:
                        st_mm(kk + 1, 0)
                        st_mm(kk + 1, 1)
                    pis = []
                    for m in range(2):
                        sbk = SB[(kk % 2) * 2 + m]
                        pi_ = pt_i[0] % 4
                        pt_i[0] += 1
                        pis.append(pi_)
                        act(PT_t[:, pi_, 0:nq], PS[sbk][:, 0:nq], AF.Exp, [PB[sbk], sm_b], [PT_b[pi_]], bias=NBIAS, scale=0.125)
                    for m in range(2):
                        pi_ = pis[m]
                        for j in range(nj):
                            rows = min(128, nq - j * 128)
                            ap, pbuf = acc_ap(m, j, rows)
                            mm(ap, PT_t[:, pi_, j * 128:j * 128 + rows], V_t[:, kk, :], kk == 0, kk == 33,
                               [PT_b[pi_], V_b], [pbuf])
                    while pending and pending[0][0] <= kk:
                        pending.pop(0)[1]()
                ai = qbi % 2
                r0 = min(128, nq)
                nj3 = min(nj, 3)
                for m in range(2):
                    cp("dve", accS_t[0:r0, ai, m * 4:m * 4 + nj3, :], PS[2 + m][0:r0, 0:nj3 * 129].rearrange("p (j e) -> p j e", e=129),
                       [PB[2 + m]], [accS_b[ai]])
                if nj == 4:
                    cp("dve", accS_t[:, ai, 3:8:4, :], PS[4][:, 0:258].rearrange("p (m e) -> p m e", e=129), [PB[4]], [accS_b[ai]])

                def fin_dve(j, fi, q0=q0, nq=nq, ai=ai):
                    rows = min(128, nq - j * 128)
                    a0 = accS_t[0:rows, ai, j, :]
                    a1 = accS_t[0:rows, ai, 4 + j, :]
                    fs = fs_t[0:rows, fi, :]
                    fb = fs_b[fi]
                    ab = accS_b[ai]
                    recip(fs[:, 0:1], a0[:, 128:129], [ab], [fb])
                    recip(fs[:, 1:2], a1[:, 128:129], [ab], [fb])
                    tt("dve", fs[:, 2:3], fs[:, 1:2], NEGLAM[0:rows, :], ALU.mult, [fb, sm_b], [fb])
                    ts("dve", o_t[0:rows, fi, :], a0[:, 0:128], fs[:, 0:1], None, ALU.mult, None, [ab, fb], [o_b[fi]])
                    stt(o_t[0:rows, fi, :], a1[:, 0:128], fs[:, 2:3], o_t[0:rows, fi, :], ALU.mult, ALU.add, [ab, fb, o_b[fi]], [o_b[fi]])
                    stt(ojunk_t[0:rows, :], o_t[0:rows, fi, :], 1.0, o_t[0:rows, fi, :], ALU.mult, ALU.mult,
                        [o_b[fi]], [ojunk_b, fb], accum=fs[:, 3:4])
                    ts("dve", fs[:, 4:5], fs[:, 3:4], 1.0 / 128.0, EPS, ALU.mult, ALU.add, [fb], [fb])
                    tt("pool", fs[:, 5:6], fs[:, 4:5], MHALF[0:rows, :], ALU.pow, [fb, sm_b], [fb])
                    stt(on_t[0:rows, fi, :], o_t[0:rows, fi, :], fs[:, 5:6], sw_t[0:rows, :], ALU.mult, ALU.mult,
                        [o_b[fi], fb, sw_b], [on_b[fi]])

                def fin_pe(j, fi, q0=q0, nq=nq, hh=hh):
                    rows = min(128, nq - j * 128)
                    pv = psbf(7)
                    tr(pv[:, 0:rows], on_t[0:rows, fi, :], ident_t[0:rows, 0:rows], [on_b[fi], ident_b], [PB[7]])
                    cp("dve", oT_t[:, hh, q0 + j * 128:q0 + j * 128 + rows], pv[:, 0:rows], [PB[7]], [oT_b])

                while pending:
                    pending.pop(0)[1]()
                for j in range(nj):
                    fi = fin_i[0] % 2
                    fin_i[0] += 1
                    pending.append((2 + 7 * j, (lambda j=j, fi=fi, f=fin_dve: f(j, fi))))
                    pending.append((7 + 7 * j, (lambda j=j, fi=fi, f=fin_pe: f(j, fi))))
                pending.sort(key=lambda t: t[0])
            while pending:
                pending.pop(0)[1]()

        ckpt("b_alloc")
        load_head_w(0)
        ckpt("b_ld")
        for hh in range(N_HEADS_RUN):
            if hh + 1 < 8:
                load_head_w(hh + 1)
            ckpt("b_ld2")
            qk_all(hh)
            if hh == 0:
                dbg_store("KT", KT_t[:, 0:256], [128, 256], [KT_b], BF16)
                dbg_store("KTc", KT_t[:, 4096:4352], [128, 256], [KT_b], BF16)
                dbg_store("QT", QT_t[:, 0:256], [128, 256], [QT_b], BF16)
                dbg_store("V", V_t[:, 0:2, :], [128, 2, 129], [V_b], BF16)
                ckpt("proj0")
            attn_head(hh)
            if hh == 0:
                ckpt("attn0")
        kb.release(hT_tiles["m"])
        dbg_store("oT", oT_t[:, :, :], [128, 8, NQ], [oT_b], BF16)
        ckpt("attn")

        qblocks = [(0, 512), (512, 512), (1024, 512), (1536, 512), (2048, 1)]
        h2T_t, h2T_b = kb.alloc("h2T", [128, 8, NQ + 7], BF16, top=True)
        mT_t, mT_b = kb.alloc("mT", [128, 8, NQ], BF16, top=True)
        mM = kb.mark()
        hsT_t, hsT_b = kb.alloc("hsT", [128, 4, NQ + 7], BF16)
        if not S5_ON:
            memset("pool", hsT_t[:, :, :], 0.0, [hsT_b])
        else:
            kb.dma("sp", hsT_t[:, :, :], hsT_spill.rearrange("p (k n) -> p k n", k=4), R=[hsT_spill_b], W=[hsT_b])
        wm_t, wm_b = kb.alloc("wm", [128, 2, 28, 128], BF16, nsub=2)
        sg_t, sg_b_ = kb.alloc("sg", [128, 1, 2, 512], F32)
        sg_b = [sg_b_, sg_b_]
        w_bs_v = I["w_bs"].rearrange("(kt p) n -> p kt n", p=128)
        w_ba_v = I["w_ba"].rearrange("(kt p) n -> p kt n", p=128)

        def load_merge_w(ft):
            bi = ft % 2
            wload(wm_t[:, bi, 0:4, :], w_bs_v[:, :, ft * 128:(ft + 1) * 128], 4, 128, wm_b[bi])
            wload(wm_t[:, bi, 4:12, :], w_ba_v[:, :, ft * 128:(ft + 1) * 128], 8, 128, wm_b[bi])
            wload(wm_t[:, bi, 12:20, :], w_in_v[:, :, 3584 + ft * 128:3584 + (ft + 1) * 128], 8, 128, wm_b[bi])
            wload(wm_t[:, bi, 20:28, :], w_in_v[:, :, 4608 + ft * 128:4608 + (ft + 1) * 128], 8, 128, wm_b[bi])

        load_merge_w(0)
        mi = 0
        for ft in range(8):
            if ft + 1 < 8:
                load_merge_w(ft + 1)
            bi = ft % 2
            for (q0, nq) in qblocks:
                i2 = mi % 2
                mi += 1
                hb = hT_bufs(q0, nq)
                for kt in range(4):
                    mm(PS[0][:, 0:nq], wm_t[:, bi, kt, :], hsT_t[:, kt, q0:q0 + nq], kt == 0, kt == 3, [wm_b[bi], hsT_b], [PB[0]])
                for kt in range(8):
                    mm(PS[1][:, 0:nq], wm_t[:, bi, 4 + kt, :], oT_t[:, kt, q0:q0 + nq], kt == 0, kt == 7, [wm_b[bi], oT_b], [PB[1]])
                for kt in range(8):
                    mm(PS[2][:, 0:nq], wm_t[:, bi, 12 + kt, :], hT(kt, q0, nq), kt == 0, kt == 7, [wm_b[bi]] + hb, [PB[2]])
                for kt in range(8):
                    mm(PS[3][:, 0:nq], wm_t[:, bi, 20 + kt, :], hT(kt, q0, nq), kt == 0, kt == 7, [wm_b[bi]] + hb, [PB[3]])
                act(sg_t[:, 0, 0, 0:nq], PS[2][:, 0:nq], AF.Sigmoid, [PB[2], V1_b], [sg_b[i2]], bias=V1_t[:, BG + ft:BG + ft + 1])
                act(sg_t[:, 0, 1, 0:nq], PS[3][:, 0:nq], AF.Sigmoid, [PB[3], V1_b], [sg_b[i2]], bias=V1_t[:, BG + 8 + ft:BG + 9 + ft])
                tt("dve", sg_t[:, 0, 0, 0:nq], PS[0][:, 0:nq], sg_t[:, 0, 0, 0:nq], ALU.mult, [PB[0], sg_b[i2]], [sg_b[i2]])
                tt("dve", sg_t[:, 0, 1, 0:nq], PS[1][:, 0:nq], sg_t[:, 0, 1, 0:nq], ALU.mult, [PB[1], sg_b[i2]], [sg_b[i2]])
                tt("pool", mT_t[:, ft, q0:q0 + nq], sg_t[:, 0, 0, 0:nq], sg_t[:, 0, 1, 0:nq], ALU.add, [sg_b[i2]], [mT_b])
        kb.release(mR0)
        dbg_store("mT", mT_t[:, :, :], [128, 8, NQ], [mT_b], BF16)
        ckpt("merge")

        mX = kb.mark()
        wo_t, wo_b = kb.alloc("wo", [128, 8, D], BF16)
        wload(wo_t[:, :, :], I["w_out"].rearrange("(kt p) n -> p kt n", p=128), 8, D, wo_b)
        xr_t, xr_b = kb.alloc("xr", [128, 2, D], F32, nsub=2)
        xm_t, xm_b = kb.alloc("xm", [128, 2, D], F32, nsub=2)
        xn2_t, xn2_b = kb.alloc("xn2", [128, 2, D], BF16, nsub=2)
        st2_t, st2_b = kb.alloc("st2", [128, 2, 4], F32, nsub=2)
        xjunk_t, xjunk_b = kb.alloc("xjunk", [128, D], F32)
        xm_out = [Buf(f"xmout{i}") for i in range(16)]
        for ti in range(17):
            rows = 128 if ti < 16 else 1
            c0 = ti * 128
            i2 = ti % 2
            src = I["xo"][c0:c0 + rows, :] if ti < 16 else I["xt"][0:1, :]
            kb.dma("sp", xr_t[0:rows, i2, :], src, W=[xr_b[i2]])
            for hf in range(2):
                pb = 4 + hf
                for kt in range(8):
                    mm(PS[pb][0:rows, :], mT_t[:, kt, c0:c0 + rows], wo_t[:, kt, hf * 512:(hf + 1) * 512], kt == 0, kt == 7,
                       [mT_b, wo_b], [PB[pb]])
                tt("dve", xm_t[0:rows, i2, hf * 512:(hf + 1) * 512], PS[pb][0:rows, :], ga_t[0:rows, 0, hf * 512:(hf + 1) * 512],
                   ALU.mult, [PB[pb], ga_b], [xm_b[i2]])
            tt("pool", xm_t[0:rows, i2, :], xm_t[0:rows, i2, :], xr_t[0:rows, i2, :], ALU.add, [xm_b[i2], xr_b[i2]], [xm_b[i2]])
            if ti < 16:
                kb.dma("pool", out_d[c0:c0 + 128, :], xm_t[:, i2, :], R=[xm_b[i2]], W=[xm_out[ti]])
            sv = st2_t[0:rows, i2, :]
            stt(xjunk_t[0:rows, :], xm_t[0:rows, i2, :], 1.0, xm_t[0:rows, i2, :], ALU.mult, ALU.mult, [xm_b[i2]], [xjunk_b, st2_b[i2]],
                accum=sv[:, 0:1])
            ts("dve", sv[:, 1:2], sv[:, 0:1], 1.0 / D, EPS, ALU.mult, ALU.add, [st2_b[i2]], [st2_b[i2]])
            act(sv[:, 3:4], sv[:, 1:2], AF.Ln, [st2_b[i2]], [st2_b[i2]])
            act(sv[:, 2:3], sv[:, 3:4], AF.Exp, [st2_b[i2]], [st2_b[i2]], scale=-0.5)
            ts("dve", xn2_t[0:rows, i2, :], xm_t[0:rows, i2, :], sv[:, 2:3], None, ALU.mult, None, [xm_b[i2], st2_b[i2]], [xn2_b[i2]])
            pb = 6 + (ti % 2)
            pv = psbf(pb)
            for kt in range(8):
                tr(pv[:, kt * 128:kt * 128 + rows], xn2_t[0:rows, i2, kt * 128:(kt + 1) * 128], ident_t[0:rows, 0:rows],
                   [xn2_b[i2], ident_b], [PB[pb]])
            for kt in range(8):
                ts("dve", h2T_t[:, kt, c0:c0 + rows], pv[:, kt * 128:kt * 128 + rows], MOD_t[:, 4, kt:kt + 1], MOD_t[:, 5, kt:kt + 1],
                   ALU.mult, ALU.add, [PB[pb], MOD_b], [h2T_b])
        kb.release(mX)
        dbg_store("h2T", h2T_t[:, :, 0:NQ], [128, 8, NQ], [h2T_b], BF16)
        ckpt("xmid")

        actT_t, actT_b = kb.alloc("actT", [128, 22, NOWN], BF16)
        mF = kb.mark()
        wup_t, wup_b = kb.alloc("wup", [128, 2, 8, 256], BF16, nsub=2)
        ya_t, ya_b = kb.alloc("ya", [128, 2, 512], F32, nsub=2)
        yg_t, yg_b = kb.alloc("yg", [128, 2, 512], F32, nsub=2)
        sl_t, sl_b = kb.alloc("sl", [128, 2, 512], F32, nsub=2)
        w_up_v = I["w_up"].rearrange("(kt p) n -> p kt n", p=128)

        def load_up_w(fc):
            bi = fc % 2
            wload(wup_t[:, bi, :, 0:128], w_up_v[:, :, fc * 128:(fc + 1) * 128], 8, 128, wup_b[bi])
            wload(wup_t[:, bi, :, 128:256], w_up_v[:, :, DFF + fc * 128:DFF + (fc + 1) * 128], 8, 128, wup_b[bi])

        load_up_w(0)
        wd_v = I["w_down"].rearrange("(kt p) n -> p kt n", p=128)
        fblocks = [(0, 500), (500, 1000), (1000, 1500), (1500, 2000), (2000, 2048)]
        fi_ = 0
        for fc in range(22):
            if fc + 1 < 22:
                load_up_w(fc + 1)
            bi = fc % 2
            for (s0, e0) in fblocks:
                i2 = fi_ % 2
                fi_ += 1
                cin = max(s0 - 1, 0)
                nin = e0 + 1 - cin
                L = e0 - s0
                off = s0 - cin
                pa, pg = (0, 1) if i2 == 0 else (2, 3)
                for kt in range(8):
                    mm(PS[pa][:, 0:nin], wup_t[:, bi, kt, 0:128], h2T_t[:, kt, cin:cin + nin], kt == 0, kt == 7, [wup_b[bi], h2T_b], [PB[pa]])
                for kt in range(8):
                    mm(PS[pg][:, 0:nin], wup_t[:, bi, kt, 128:256], h2T_t[:, kt, cin:cin + nin], kt == 0, kt == 7, [wup_b[bi], h2T_b], [PB[pg]])
                for (pp, y_t, y_b, fcol) in ((pa, ya_t, ya_b, fc), (pg, yg_t, yg_b, 22 + fc)):
                    yv = y_t[:, i2, 0:L]
                    act(yv, PS[pp][:, off:off + L], AF.Identity, [PB[pp], V2_b], [y_b[i2]],
                        bias=V2_t[:, 132 + fcol:133 + fcol], scale=V2_t[:, 44 + fcol:45 + fcol])
                    stt(yv, PS[pp][:, off + 1:off + 1 + L], V2_t[:, 88 + fcol:89 + fcol], yv, ALU.mult, ALU.add, [PB[pp], V2_b, y_b[i2]], [y_b[i2]])
                    lo = 1 if s0 == 0 else 0
                    stt(y_t[:, i2, lo:L], PS[pp][:, off - 1 + lo:off - 1 + L], V2_t[:, fcol:fcol + 1], y_t[:, i2, lo:L], ALU.mult, ALU.add,
                        [PB[pp], V2_b, y_b[i2]], [y_b[i2]])
                act(sl_t[:, i2, 0:L], yg_t[:, i2, 0:L], AF.Silu, [yg_b[i2]], [sl_b[i2]])
                tt("pool", actT_t[:, fc, s0:e0], sl_t[:, i2, 0:L], ya_t[:, i2, 0:L], ALU.mult, [sl_b[i2], ya_b[i2]], [actT_b])
        dbg_store("actT", actT_t[:, :, 0:256], [128, 22, 256], [actT_b], BF16)
        ckpt("ffn_up")
        kb.release(mF)
        kb.release_top(SB_LIMIT)
        wd_t, wd_b = kb.alloc("wd", [128, 22, D], BF16)
        wload(wd_t[:, :, :], wd_v[:, :, :], 22, D, wd_b, cast_engs=("pool", "dve", "act"))
        xo2_t, xo2_b = kb.alloc("xo2", [128, 2, D], F32, nsub=2)
        fo_t, fo_b = kb.alloc("fo", [128, 2, D], F32, nsub=2)
        for ti in range(16):
            c0 = ti * 128
            i2 = ti % 2
            kb.dma("sp", xo2_t[:, i2, :], out_d[c0:c0 + 128, :], R=[xm_out[ti]], W=[xo2_b[i2]])
            for hf in range(2):
                pb = 4 + hf + 2 * (ti % 2)
                for fc in range(22):
                    mm(PS[pb][:, :], actT_t[:, fc, c0:c0 + 128], wd_t[:, fc, hf * 512:(hf + 1) * 512], fc == 0, fc == 21,
                       [actT_b, wd_b], [PB[pb]])
                tt("dve", fo_t[:, i2, hf * 512:(hf + 1) * 512], PS[pb][:, :], ga_t[:, 1, hf * 512:(hf + 1) * 512], ALU.mult,
                   [PB[pb], ga_b], [fo_b[i2]])
            tt("pool", fo_t[:, i2, :], fo_t[:, i2, :], xo2_t[:, i2, :], ALU.add, [fo_b[i2], xo2_b[i2]], [fo_b[i2]])
            ob = Buf(f"outf{ti}")
            kb.dma("pool", out_d[c0:c0 + 128, :], fo_t[:, i2, :], R=[fo_b[i2], xo2_b[i2]], W=[ob, xm_out[ti]])
            obufs.append(ob)

    try:
        body()
    except _Stop:
        pass

    kb.wait_all("sp", obufs)
    kb.emit()
    print("SBUF peak bytes/partition:", kb.sb_peak - SB_BASE, " ops:", {e: len(kb.q[e]) for e in ENGS})
    return nc, dbg_out


def make_in_maps(inputs):
    x = np.asarray(inputs["x"], np.float32)
    ctx = np.asarray(inputs["ctx"], np.float32)
    c = np.asarray(inputs["c"], np.float32)
    c_ctx = np.asarray(inputs["c_ctx"], np.float32)
    g = lambda k: np.ascontiguousarray(np.asarray(inputs[k], np.float32)[0])
    lamv = np.stack([g("lam_q1"), g("lam_k1"), g("lam_q2"), g("lam_k2")], 0)
    common = {
        "ada_w": g("ada_w"), "ada_b": g("ada_b"), "norm1_w": g("norm1_w"), "w_in": g("w_in"),
        "b_gate": g("b_gate"), "q_norm_w": g("q_norm_w"), "k_norm_w": g("k_norm_w"), "lamv": lamv,
        "subln_w": g("subln_w"), "s5_d": g("s5_d"), "glu_w": g("glu_w"), "glu_b": g("glu_b"),
        "w_bs": g("w_branch_s5"), "w_ba": g("w_branch_attn"), "w_out": g("w_out"), "norm2_w": g("norm2_w"),
        "w_up": g("w_up"), "conv_b": g("conv_b"), "w_down": g("w_down"),
    }
    maps = []
    for cid in range(8):
        b, h = cid // 2, cid % 2
        m = dict(common)
        if h == 0:
            m["xo"] = np.ascontiguousarray(x[b, 0:NOWN])
            m["xt"] = np.ascontiguousarray(x[b, NOWN:])
            m["cx"] = np.ascontiguousarray(ctx[b])
            do = [0, 1]
            m["conv_w"] = g("conv_w")
            m["posinfo"] = np.array([1.0, 0.0], np.float32)
        else:
            m["xo"] = np.ascontiguousarray(x[b, :NOWN - 1:-1])
            m["xt"] = np.ascontiguousarray(x[b, NOWN - 1::-1])
            m["cx"] = np.ascontiguousarray(ctx[b, ::-1])
            do = [1, 0]
            m["conv_w"] = np.ascontiguousarray(g("conv_w")[::-1])
            m["posinfo"] = np.array([-1.0, 63.0], np.float32)
        m["cvec"] = np.ascontiguousarray(np.stack([c[b], c_ctx], 0))
        for k in ("s5_a_re", "s5_a_im", "s5_b_re", "s5_b_im", "s5_c_re", "s5_c_im"):
            m[k] = np.ascontiguousarray(g(k)[do])
        m["s5_log_dt"] = np.ascontiguousarray(g("s5_log_dt")[do].reshape(64))
        maps.append(m)
    return maps


def assemble(results):
    out = np.zeros((4, NLAT, D), np.float32)
    for cid in range(8):
        b, h = cid // 2, cid % 2
        r = results[cid]["out"]
        if h == 0:
            out[b, 0:NOWN] = r
        else:
            out[b, NOWN:] = r[::-1]
    return out


_NC_CACHE = {}


def kernel(**inputs):
    if "nc" not in _NC_CACHE:
        _NC_CACHE["nc"] = build()[0]
    nc = _NC_CACHE["nc"]
    res = run_bass_kernel_spmd(nc, make_in_maps(inputs), core_ids=list(range(8)))
    return assemble(res.results)
```
